# Optimizing a Trainium2 kernel written in Bass

```python
import math
import jax
import jax.numpy as jnp
from jax import lax
import numpy as np

D_MODEL = 1024
BATCH = 16
SEQ = 2048
DEPTH = 2

CTX_LEN = 256
GRID_W = 64
MIX_WIDTH = D_MODEL
GROUP_W = MIX_WIDTH // 4
HEAD = 64
N_RWKV_HEADS = GROUP_W // HEAD
DECAY_LORA = 64
AAA_LORA = 64
GATE_LORA = 128
LORA_COLS = 2 * DECAY_LORA + 2 * AAA_LORA + GATE_LORA
LORA_SPLITS = (DECAY_LORA, 2 * DECAY_LORA, 2 * DECAY_LORA + AAA_LORA, 2 * DECAY_LORA + 2 * AAA_LORA)
POOL_WINDOWS = (2, 4, 8, 16)
POOL_GROUPS = 4
POOL_CH = GROUP_W // POOL_GROUPS
CHUNK = 128
GMLP_GROUPS = 4
GMLP_CH = GROUP_W // GMLP_GROUPS
FNET_GROUPS = 4
FNET_CH = GROUP_W // FNET_GROUPS
RWKV_COLS = 3 * GROUP_W + LORA_COLS
IN_COLS = RWKV_COLS + GROUP_W + 2 * GROUP_W + GROUP_W
IN_SPLITS = (3 * GROUP_W, RWKV_COLS, RWKV_COLS + GROUP_W, RWKV_COLS + 3 * GROUP_W)
D_FF = -(-8 * D_MODEL // (3 * 256)) * 256
ALPHA = (2 * DEPTH) ** 0.25
BETA = (8 * DEPTH) ** -0.25
LN_EPS = 1e-5
GN_EPS = 64e-5

kernel_name = 'hybrid_rwkv_pool_gmlp_fnet_dit'


def layer_norm(x, g, b, eps=LN_EPS):
    xf = x.astype(jnp.float32)
    mu = jnp.mean(xf, -1, keepdims=True)
    var = jnp.mean(jnp.square(xf - mu), -1, keepdims=True)
    return ((xf - mu) * lax.rsqrt(var + eps) * g.astype(jnp.float32) + b.astype(jnp.float32)).astype(x.dtype)


def pos_embed_2d(n_tok, dim, dtype):
    rows = n_tok // GRID_W
    row, col = jnp.meshgrid(jnp.arange(rows, dtype=jnp.float32), jnp.arange(GRID_W, dtype=jnp.float32), indexing='ij')
    quarter = dim // 4
    freqs = jnp.exp(-math.log(10000.0) * jnp.arange(quarter, dtype=jnp.float32) / quarter)
    def enc(p):
        ang = p.reshape(-1, 1) * freqs[None, :]
        return jnp.concatenate([jnp.sin(ang), jnp.cos(ang)], -1)
    return jnp.concatenate([enc(row), enc(col)], -1).astype(dtype)


def modulation(cond, w, b):
    m = (jax.nn.silu(cond) @ w + b)[..., None, :]
    return jnp.split(m, 6, axis=-1)


def short_conv(z, w):
    zp = jnp.pad(z, ((0, 0), (1, 1), (0, 0)))
    return zp[:, :-2] * w[0] + zp[:, 1:-1] * w[1] + zp[:, 2:] * w[2]


def rwkv_prep(z_rkv, z_lora, conv, w0, w2, a0, a2, k_k, k_a):
    B, L, _ = z_rkv.shape
    def heads(t):
        return t.reshape(B, L, N_RWKV_HEADS, HEAD).astype(jnp.float32)
    r, k, v = jnp.split(short_conv(z_rkv, conv), 3, axis=-1)
    dw_f, dw_b, da_f, da_b, dg = jnp.split(z_lora, LORA_SPLITS, axis=-1)
    kk = heads(k * k_k)
    kk = kk / jnp.maximum(jnp.sqrt(jnp.sum(kk * kk, -1, keepdims=True)), 1e-12)
    kf = k.astype(jnp.float32)
    dirs = []
    for d, (dw, da) in enumerate(((dw_f, da_f), (dw_b, da_b))):
        logw = -jax.nn.softplus(-(w0[d] + jnp.tanh(dw) @ w2[d]).astype(jnp.float32)) - 0.5
        decay = jnp.exp(-jnp.exp(logw))
        a = jax.nn.sigmoid((a0[d] + da @ a2[d]).astype(jnp.float32))
        k_d = kf * (1.0 + (a - 1.0) * k_a.astype(jnp.float32))
        dirs.append((heads(k_d), heads(decay), heads(a)))
    return heads(r), heads(v), kk, dg, dirs


def rwkv_scan(prep, direction, s0):
    r, v, kk, _, dirs = prep
    k, decay, a = dirs[direction]
    def step(S, inp):
        r_t, k_t, v_t, w_t, kk_t, a_t = inp
        s_kk = jnp.einsum('bhvk,bhk->bhv', S, -kk_t)
        S = (S * w_t[:, :, None, :]
             + s_kk[..., None] * (kk_t * a_t)[:, :, None, :]
             + v_t[..., None] * k_t[:, :, None, :])
        return S, jnp.einsum('bhvk,bhk->bhv', S, r_t)
    xs = tuple(jnp.moveaxis(t, 1, 0) for t in (r, k, v, decay, kk, a))
    s_final, ys = lax.scan(step, s0, xs, reverse=(direction == 1))
    return jnp.moveaxis(ys, 0, 1), s_final


def rwkv_readout(prep, y_f, y_b, r_k, gn_g, gn_b, gate_g2):
    r, v, kk, dg, dirs = prep
    B, L, H, N = r.shape
    y = y_f + y_b
    mu = jnp.mean(y, -1, keepdims=True)
    var = jnp.mean(jnp.square(y - mu), -1, keepdims=True)
    y = (y - mu) * lax.rsqrt(var + GN_EPS)
    bonus = jnp.sum(r * (dirs[0][0] + dirs[1][0]) * r_k.astype(jnp.float32), -1, keepdims=True) * v
    y = y.reshape(B, L, H * N) * gn_g.astype(jnp.float32) + gn_b.astype(jnp.float32) + bonus.reshape(B, L, H * N)
    gate = jax.nn.sigmoid(dg) @ gate_g2
    return (y * gate.astype(jnp.float32)).astype(dg.dtype)


def pool_mixer(z, pool_w, pool_scale):
    B, L, _ = z.shape
    zg = z.reshape(B, L, POOL_GROUPS, POOL_CH).astype(jnp.float32)
    cs = jnp.pad(jnp.cumsum(zg, axis=1), ((0, 0), (1, 0), (0, 0), (0, 0)))
    pos = jnp.arange(L)
    pooled = []
    for gi, w in enumerate(POOL_WINDOWS):
        lo = jnp.clip(pos - w // 2, 0, L)
        hi = jnp.clip(pos + w - w // 2, 0, L)
        cnt = (hi - lo).astype(jnp.float32)[None, :, None]
        pooled.append((cs[:, hi, gi] - cs[:, lo, gi]) / cnt)
    d = (jnp.stack(pooled, axis=2) - zg).astype(z.dtype)
    y = jnp.einsum('blgc,gcd->blgd', d, pool_w)
    return y.reshape(B, L, GROUP_W) * pool_scale


def gmlp_mixer(z, ln_g, ln_b, ws, bs):
    B, L, _ = z.shape
    u, v = jnp.split(jax.nn.gelu(z), 2, axis=-1)
    v = v.reshape(B, L // CHUNK, CHUNK, GMLP_GROUPS, GMLP_CH)
    v = layer_norm(v, ln_g.reshape(GMLP_GROUPS, GMLP_CH), ln_b.reshape(GMLP_GROUPS, GMLP_CH))
    sv = jnp.einsum('gpq,bnqgc->bnpgc', ws, v) + bs.T[None, None, :, :, None]
    return u * sv.reshape(B, L, GROUP_W)


def fourier_mixer(z, fnet_w, fnet_b):
    B, L, _ = z.shape
    zg = jnp.transpose(z.reshape(B, L, FNET_GROUPS, FNET_CH).astype(jnp.float32), (0, 2, 1, 3))
    f = jnp.transpose(jnp.fft.fft2(zg, norm='ortho').real, (0, 2, 1, 3)).astype(z.dtype)
    return jnp.einsum('blgc,gcd->blgd', f, fnet_w).reshape(B, L, GROUP_W) + fnet_b


def mix_out(parts, prep, y_f, y_b, r_k, gn_g, gn_b, gate_g2, pool_w, pool_scale,
            gmlp_ln_g, gmlp_ln_b, gmlp_ws, gmlp_bs, fnet_w, fnet_b, w_out):
    a = rwkv_readout(prep, y_f, y_b, r_k, gn_g, gn_b, gate_g2)
    b = pool_mixer(parts[2], pool_w, pool_scale)
    g = gmlp_mixer(parts[3], gmlp_ln_g, gmlp_ln_b, gmlp_ws, gmlp_bs)
    f = fourier_mixer(parts[4], fnet_w, fnet_b)
    return jnp.concatenate([a, b, g, f], axis=-1) @ w_out


def swiglu(h, w1, w2):
    gate, up = jnp.split(h @ w1, 2, axis=-1)
    return (jax.nn.silu(gate) * up) @ w2


def setup_inputs(seed: int = 0) -> dict:
    key = jax.random.key(seed)
    ks = iter(jax.random.split(key, 40))
    def nrm(shape, s):
        return s * jax.random.normal(next(ks), shape, jnp.float32)
    D = D_MODEL
    centre = jnp.asarray([0.0, 1.0, 0.0], jnp.float32)[None, :, None]
    return {
        'x': nrm((BATCH, SEQ, D), 1.0),
        'c': nrm((BATCH, D), 1.0),
        'ctx': nrm((BATCH, CTX_LEN, D), 1.0),
        'c_ctx': nrm((D,), 1.0),
        'w_mod': nrm((DEPTH, D, 6 * D), 0.5 * D ** -0.5),
        'b_mod': nrm((DEPTH, 6 * D), 0.01),
        'w_in': nrm((DEPTH, D, IN_COLS), D ** -0.5),
        'rkv_conv': centre + nrm((DEPTH, 3, 3 * GROUP_W), 0.3),
        'decay_w0': nrm((DEPTH, 2, GROUP_W), 1.0),
        'decay_w2': nrm((DEPTH, 2, DECAY_LORA, GROUP_W), 0.5 * DECAY_LORA ** -0.5),
        'iclr_a0': nrm((DEPTH, 2, GROUP_W), 0.5),
        'iclr_a2': nrm((DEPTH, 2, AAA_LORA, GROUP_W), 0.5 * AAA_LORA ** -0.5),
        'gate_g2': nrm((DEPTH, GATE_LORA, GROUP_W), GATE_LORA ** -0.5),
        'k_k': 0.85 + nrm((DEPTH, GROUP_W), 0.05),
        'k_a': 1.0 + nrm((DEPTH, GROUP_W), 0.05),
        'r_k': nrm((DEPTH, N_RWKV_HEADS, HEAD), 0.1),
        'gn_g': 1.0 + nrm((DEPTH, GROUP_W), 0.05),
        'gn_b': nrm((DEPTH, GROUP_W), 0.01),
        'pool_w': nrm((DEPTH, POOL_GROUPS, POOL_CH, POOL_CH), POOL_CH ** -0.5),
        'pool_scale': 1.0 + nrm((DEPTH, GROUP_W), 0.1),
        'gmlp_ln_g': 1.0 + nrm((DEPTH, GROUP_W), 0.05),
        'gmlp_ln_b': nrm((DEPTH, GROUP_W), 0.01),
        'gmlp_ws': nrm((DEPTH, GMLP_GROUPS, CHUNK, CHUNK), CHUNK ** -0.5),
        'gmlp_bs': 1.0 + nrm((DEPTH, GMLP_GROUPS, CHUNK), 0.1),
        'fnet_w': nrm((DEPTH, FNET_GROUPS, FNET_CH, FNET_CH), FNET_CH ** -0.5),
        'fnet_b': nrm((DEPTH, GROUP_W), 0.01),
        'w_out': nrm((DEPTH, MIX_WIDTH, D), BETA * MIX_WIDTH ** -0.5),
        'ln1_g': 1.0 + nrm((DEPTH, D), 0.1),
        'ln1_b': nrm((DEPTH, D), 0.01),
        'ln2_g': 1.0 + nrm((DEPTH, D), 0.1),
        'ln2_b': nrm((DEPTH, D), 0.01),
        'ffn_w1': nrm((DEPTH, D, 2 * D_FF), D ** -0.5),
        'ffn_w2': nrm((DEPTH, D_FF, D), BETA * D_FF ** -0.5),
    }


def reference(x, c, ctx, c_ctx, w_mod, b_mod, w_in, rkv_conv, decay_w0, decay_w2, iclr_a0, iclr_a2,
              gate_g2, k_k, k_a, r_k, gn_g, gn_b, pool_w, pool_scale, gmlp_ln_g, gmlp_ln_b, gmlp_ws,
              gmlp_bs, fnet_w, fnet_b, w_out, ln1_g, ln1_b, ln2_g, ln2_b, ffn_w1, ffn_w2):
    B, L, D = x.shape
    xs = x + pos_embed_2d(L, D, x.dtype)[None]
    cs = ctx
    s0 = jnp.zeros((B, N_RWKV_HEADS, HEAD, HEAD), jnp.float32)
    for l in range(DEPTH):
        last = l == DEPTH - 1
        sh1, sc1, g1, sh2, sc2, g2 = modulation(c, w_mod[l], b_mod[l])
        csh1, csc1, cg1, csh2, csc2, cg2 = modulation(c_ctx[None], w_mod[l], b_mod[l])
        hx = xs * (1.0 + sc1) + sh1
        hc = cs * (1.0 + csc1) + csh1
        px = jnp.split(hx @ w_in[l], IN_SPLITS, axis=-1)
        if last:
            pc = jnp.split(hc @ w_in[l][:, :RWKV_COLS], (3 * GROUP_W,), axis=-1)
        else:
            pc = jnp.split(hc @ w_in[l], IN_SPLITS, axis=-1)
        rw = (rkv_conv[l], decay_w0[l], decay_w2[l], iclr_a0[l], iclr_a2[l], k_k[l], k_a[l])
        prep_c = rwkv_prep(pc[0], pc[1], *rw)
        prep_x = rwkv_prep(px[0], px[1], *rw)
        yc_f, sc_f = rwkv_scan(prep_c, 0, s0)
        yc_b, sc_b = rwkv_scan(prep_c, 1, s0)
        yx_f, _ = rwkv_scan(prep_x, 0, sc_f)
        yx_b, _ = rwkv_scan(prep_x, 1, sc_b)
        op = (r_k[l], gn_g[l], gn_b[l], gate_g2[l], pool_w[l], pool_scale[l], gmlp_ln_g[l], gmlp_ln_b[l],
              gmlp_ws[l], gmlp_bs[l], fnet_w[l], fnet_b[l], w_out[l])
        mix_x = mix_out(px, prep_x, yx_f, yx_b, *op)
        xs = layer_norm(ALPHA * xs + g1 * mix_x, ln1_g[l], ln1_b[l])
        xs = layer_norm(ALPHA * xs + g2 * swiglu(xs * (1.0 + sc2) + sh2, ffn_w1[l], ffn_w2[l]), ln2_g[l], ln2_b[l])
        if not last:
            mix_c = mix_out(pc, prep_c, yc_f, yc_b, *op)
            cs = layer_norm(ALPHA * cs + cg1 * mix_c, ln1_g[l], ln1_b[l])
            cs = layer_norm(ALPHA * cs + cg2 * swiglu(cs * (1.0 + csc2) + csh2, ffn_w1[l], ffn_w2[l]), ln2_g[l], ln2_b[l])
    return xs
```

```python
import math
from contextlib import ExitStack
import numpy as np
import ml_dtypes
import concourse.bass as bass
import concourse.mybir as mybir
from concourse.bass_utils import run_bass_kernel_spmd

F32 = mybir.dt.float32
BF16 = mybir.dt.bfloat16
AF = mybir.ActivationFunctionType
ALU = mybir.AluOpType
AX = mybir.AxisListType

D = 1024
NBATCH = 16
SEQ = 2048
CTX = 256
DEPTH = 2
NCORES = 8
NB = NBATCH // NCORES
T = CTX + SEQ
GRID_W = 64
GW = 256
HEAD = 64
INC = 2176
RWC = 1152
DFF = 2816
ALPHA = (2 * DEPTH) ** 0.25
LN_EPS = 1e-5
GN_EPS = 64e-5
POOLW = (2, 4, 8, 16)
CH = 64
SCAN_DT = F32
USE_F32R = True
SES = True

PV = {}
_c = 0
for _n, _k in (("conv", 18), ("w0", 4), ("a0", 4), ("kk", 2), ("ka", 2), ("rk", 2), ("gng", 2), ("gnb", 2),
               ("psc", 2), ("fnb", 2), ("l1g", 8), ("l1b", 8), ("l2g", 8), ("l2b", 8)):
    PV[_n] = _c
    _c += _k
NPV = _c


class Buf:
    __slots__ = ("name", "lw", "rd", "r32")

    def __init__(self, name=""):
        self.name = name
        self.lw = None
        self.rd = {}
        self.r32 = False


class TB:
    __slots__ = ("t", "b")

    def __init__(self, t, name=""):
        self.t = t
        self.b = Buf(name)


class Trk:
    NDMA = 16

    def __init__(self, nc, ses=True):
        self.nc = nc
        self.ses = ses
        self.eng = {"pe": nc.tensor, "act": nc.scalar, "dve": nc.vector, "pool": nc.gpsimd, "sp": nc.sync}
        self.sem = {k: nc.alloc_semaphore("s_" + k) for k in self.eng}
        self.cnt = {k: 0 for k in self.eng}
        self.dq = {}
        for q in ("sp", "pool", "act"):
            self.dq[q] = dict(sems=[nc.alloc_semaphore("d_%s%d" % (q, i)) for i in range(self.NDMA)], k=0)
        self.waited = {k: {} for k in self.eng}

    def _wait(self, e, key, semh, val):
        if self.waited[e].get(key, 0) >= val:
            return
        self.eng[e].wait_ge(semh, val)
        self.waited[e][key] = val

    def _need(self, e, ev):
        key, val = ev
        if key[0] == "e":
            if key[1] == e and (e == "pe" or e == "sp" or not self.ses):
                return
            self._wait(e, key, self.sem[key[1]], val)
        else:
            self._wait(e, key, self.dq[key[1]]["sems"][key[2]], val)

    def deps(self, e, reads, writes):
        for b in reads:
            if b.lw is not None:
                self._need(e, b.lw)
        for b in writes:
            if b.lw is not None:
                self._need(e, b.lw)
            for k, v in b.rd.items():
                self._need(e, (k, v))

    def _mark(self, ev, reads, writes):
        for b in reads:
            b.rd[ev[0]] = ev[1]
        for b in writes:
            b.lw = ev
            b.rd = {}

    def op(self, e, fn, reads=(), writes=()):
        self.deps(e, reads, writes)
        inst = fn(self.eng[e])
        self.cnt[e] += 1
        inst.then_inc(self.sem[e], 1)
        self._mark((("e", e), self.cnt[e]), reads, writes)
        return inst

    def dma(self, q, out, in_, reads=(), writes=()):
        d = self.dq[q]
        k = d["k"]
        slot = k % self.NDMA
        rnd = k // self.NDMA
        key = ("d", q, slot)
        if rnd > 0:
            self._wait(q, key, d["sems"][slot], 16 * rnd)
        self.deps(q, reads, writes)
        inst = self.eng[q].dma_start(out=out, in_=in_)
        inst.then_inc(d["sems"][slot], 16)
        d["k"] = k + 1
        self._mark((key, 16 * (rnd + 1)), reads, writes)
        return inst

    def barrier(self):
        for e in self.eng:
            for o in self.eng:
                if o != e and self.cnt[o] > 0:
                    self._wait(e, ("e", o), self.sem[o], self.cnt[o])
            for q, d in self.dq.items():
                k = d["k"]
                for slot in range(self.NDMA):
                    n = (k - slot + self.NDMA - 1) // self.NDMA
                    if n > 0:
                        self._wait(e, ("d", q, slot), d["sems"][slot], 16 * n)


def build(nlayers=DEPTH, dbg=()):
    nc = bass.Bass("TRN2", target_bir_lowering=False)
    tk = Trk(nc, ses=SES)

    def din(name, shape, dt=F32):
        return nc.dram_tensor(name, list(shape), dt, kind="ExternalInput").ap()

    def dscr(name, shape, dt=F32):
        kind = "ExternalOutput" if name in dbg else "Internal"
        return nc.dram_tensor(name, list(shape), dt, kind=kind).ap()

    xT_d = din("xT", [NB, 128, 8, SEQ])
    ctxT_d = din("ctxT", [NB, 128, 8, CTX])
    posT_d = din("posT", [128, 8, SEQ])
    cT_d = din("cT", [128, 8, 4])
    wmod_d = din("w_mod", [DEPTH, 128, 8, 6 * D])
    bmod_d = din("b_mod", [DEPTH, 128, 48])
    win_d = din("w_in", [DEPTH, 128, 8, INC])
    wout_d = din("w_out", [DEPTH, 128, 8, D])
    w1_d = din("ffn_w1", [DEPTH, 128, 8, 2 * DFF])
    w2_d = din("ffn_w2", [DEPTH, 128, 22, D])
    pv_d = din("pv", [DEPTH, 128, NPV])
    w2s_d = din("w2s", [DEPTH, 128, 256])
    a2s_d = din("a2s", [DEPTH, 128, 256])
    g2_d = din("g2", [DEPTH, 128, 256])
    poolbd_d = din("poolbd", [DEPTH, 128, 2, 128])
    fnetbd_d = din("fnetbd", [DEPTH, 128, 2, 128])
    wsT_d = din("wsT", [DEPTH, 128, 4, 128])
    bsb_d = din("bsb", [DEPTH, 128, 2, 128])
    glg_d = din("glg", [DEPTH, 128, 256])
    glb_d = din("glb", [DEPTH, 128, 256])
    cl_x_d = din("cl_x", [128, 16, SEQ], BF16)
    sl_x_d = din("sl_x", [128, 16, SEQ], BF16)
    cl_c_d = din("cl_c", [128, 2, CTX], BF16)
    sl_c_d = din("sl_c", [128, 2, CTX], BF16)
    cs64_d = din("cs64", [128, 128], BF16)
    rc_x_d = din("rc_x", [128, 2, SEQ])
    rc_c_d = din("rc_c", [128, 2, CTX])
    cst_d = din("cst", [128, 1920])
    out_d = nc.dram_tensor("out", [NB, 128, 8, SEQ], F32, kind="ExternalOutput").ap()

    xs_d = dscr("xs_s", [NB, 128, 8, T])
    px_d = dscr("px_s", [NB, 17, 128, T])
    yf_d = dscr("yf_s", [NB, 2, 128, T])
    mix_d = dscr("mix_s", [NB, 128, 8, T], BF16)
    winb_d = woutb_d = w1b_d = w2b_d = None

    es_glob = ExitStack()
    uid = [0]

    class Scope:
        def __init__(self):
            self.es = ExitStack()

        def __enter__(self):
            self.es.__enter__()
            return self

        def __exit__(self, *a):
            return self.es.__exit__(*a)

        def sb(self, name, shape, dt=F32):
            uid[0] += 1
            t = self.es.enter_context(nc.sbuf_tensor("%s_%d" % (name, uid[0]), list(shape), dt))
            return TB(t, name)

    def gsb(name, shape, dt=F32):
        t = es_glob.enter_context(nc.sbuf_tensor("g_" + name, list(shape), dt))
        return TB(t, name)

    banks = [TB(es_glob.enter_context(nc.psum_tensor("bank%d" % i, [128, 512], F32)), "bank%d" % i) for i in range(8)]
    bank_i = [0]

    def nb_():
        b = banks[bank_i[0] % 8]
        bank_i[0] += 1
        return b

    rr = [0]

    def ee():
        rr[0] += 1
        return "dve" if rr[0] % 2 else "act"

    F32R = mybir.dt.float32r

    def PE(out, lhsT, rhs, start, stop, reads, writes):
        if USE_F32R and lhsT.dtype == F32 and all(b.r32 for b in reads):
            lhsT = lhsT.bitcast(F32R)
            rhs = rhs.bitcast(F32R)
        tk.op("pe", lambda e: e.matmul(out, lhsT=lhsT, rhs=rhs, start=start, stop=stop), reads, writes)

    def PET(out, in_, reads, writes):
        kp = in_.shape[0]
        if in_.dtype == BF16:
            tk.op("pe", lambda e: e.transpose(out, in_, ident_bf.t[0:kp, 0:kp]), list(reads) + [ident_bf.b], writes)
        else:
            tk.op("pe", lambda e: e.transpose(out, in_, ident.t[0:kp, 0:kp]), list(reads) + [cstb], writes)

    def RO(out, writes):
        if USE_F32R and out.dtype == F32 and any(b.r32 for b in writes):
            return out.bitcast(F32R)
        return out

    def CP(out, in_, reads, writes, eng=None):
        out = RO(out, writes)
        eng = eng or ee()
        if eng == "act":
            tk.op("act", lambda e: e.copy(out=out, in_=in_), reads, writes)
        else:
            tk.op(eng, lambda e: e.tensor_copy(out=out, in_=in_), reads, writes)

    def TT(out, in0, in1, op, reads, writes, eng="dve"):
        out = RO(out, writes)
        tk.op(eng, lambda e: e.tensor_tensor(out=out, in0=in0, in1=in1, op=op), reads, writes)

    def TS(out, in0, s1, s2, op0, op1, reads, writes, eng="dve"):
        out = RO(out, writes)
        if op1 is None:
            tk.op(eng, lambda e: e.tensor_scalar(out=out, in0=in0, scalar1=s1, scalar2=None, op0=op0), reads, writes)
        else:
            tk.op(eng, lambda e: e.tensor_scalar(out=out, in0=in0, scalar1=s1, scalar2=s2, op0=op0, op1=op1), reads, writes)

    def STT(out, in0, scalar, in1, op0, op1, reads, writes):
        out = RO(out, writes)
        tk.op("dve", lambda e: e.scalar_tensor_tensor(out=out, in0=in0, scalar=scalar, in1=in1, op0=op0, op1=op1), reads, writes)

    def ACT(out, in_, func, reads, writes, bias=0.0, scale=1.0):
        out = RO(out, writes)
        tk.op("act", lambda e: e.activation(out=out, in_=in_, func=func, bias=bias, scale=scale), reads, writes)

    def MOD(out, in_, s1, s2, reads, writes, eng=None):
        eng = eng or ee()
        if eng == "act":
            ACT(out, in_, AF.Identity, reads, writes, bias=s2, scale=s1)
        else:
            TS(out, in_, s1, s2, ALU.mult, ALU.add, reads, writes, eng=eng)

    def BIGDMA(dst_tb, src_ap, nrow, step, q="sp"):
        rowB = [None] * nrow
        for r in range(0, nrow, step):
            r2 = min(nrow, r + step)
            pb_ = Buf()
            for rr_ in range(r, r2):
                rowB[rr_] = pb_
            tk.dma(q, dst_tb.t[:, r:r2, :], src_ap[:, r:r2, :], writes=[pb_])
        return rowB

    def BIGDMAC(dst_tb, src_ap, nrow, step):
        rowB = [None] * nrow
        for r in range(0, nrow, step):
            r2 = min(nrow, r + step)
            pb_ = Buf()
            for rr_ in range(r, r2):
                rowB[rr_] = pb_
            tk.dma("pool", dst_tb.t[:, r:r2, :], src_ap[:, r:r2, :], writes=[pb_])
        return rowB

    def MEMSET(ap, val, writes, eng="pool"):
        ap = RO(ap, writes)
        tk.op(eng, lambda e: e.memset(ap, val), (), writes)

    cst = gsb("cst", [128, 1920])
    cstb = cst.b
    tk.dma("sp", cst.t[:], cst_d[:, :], writes=[cstb])
    ident = TB(cst.t[:, 0:128])
    ident.b = cstb
    headones = cst.t[:, 128:256]
    SU4 = cst.t[:, 256:768].rearrange("p (a b) -> p a b", a=4)
    SL4 = cst.t[:, 768:1280].rearrange("p (a b) -> p a b", a=4)
    UI4 = cst.t[:, 1280:1536].rearrange("p (a b) -> p a b", a=4)
    LI4 = cst.t[:, 1536:1792].rearrange("p (a b) -> p a b", a=4)
    ident4 = None
    ident_bf = gsb("ident_bf", [128, 128], BF16)
    CP(ident_bf.t[:], cst.t[:, 0:128], [cstb], [ident_bf.b], eng="dve")
    ones_bf = gsb("ones_bf", [128, 128], BF16)
    MEMSET(ones_bf.t[:], 1.0, [ones_bf.b])
    id4 = gsb("id4", [128, 4, 128])
    for a in range(4):
        CP(id4.t[:, a, :], cst.t[:, 0:128], [cstb], [id4.b], eng="dve")
    scanmask = gsb("scanmask", [128, 256])
    MEMSET(scanmask.t[:], 1.0, [scanmask.b])
    for c in range(4):
        MEMSET(scanmask.t[:, c * 64:c * 64 + 1], 0.0, [scanmask.b])
    eps_kk = gsb("eps_kk", [128, 1])
    MEMSET(eps_kk.t[:], 1e-20, [eps_kk.b])
    eps_gn = gsb("eps_gn", [128, 1])
    MEMSET(eps_gn.t[:], GN_EPS, [eps_gn.b])
    eps_ln = gsb("eps_ln", [128, 1])
    MEMSET(eps_ln.t[:], LN_EPS, [eps_ln.b])
    eps_lna = gsb("eps_lna", [128, 1])
    MEMSET(eps_lna.t[:], LN_EPS / (ALPHA * ALPHA), [eps_lna.b])
    modL = [gsb("mod%d" % i, [128, 48, 3]) for i in range(DEPTH)]
    pvL = [gsb("pv%d" % i, [128, NPV]) for i in range(DEPTH)]
    omkaL = [gsb("omka%d" % i, [128, 2]) for i in range(DEPTH)]
    mod = TB(modL[0].t)
    pv = TB(pvL[0].t)
    omka = TB(omkaL[0].t)

    def set_layer(l):
        mod.t, mod.b = modL[l].t, modL[l].b
        pv.t, pv.b = pvL[l].t, pvL[l].b
        omka.t, omka.b = omkaL[l].t, omkaL[l].b
    set_layer(0)
    sil = gsb("sil", [128, 8, 4])
    cTt = gsb("cTt", [128, 8, 4])
    tk.dma("sp", cTt.t[:], cT_d[:, :, :], writes=[cTt.b])
    ACT(sil.t[:], cTt.t[:], AF.Silu, [cTt.b], [sil.b])

    def pvc(name, j=0):
        c = PV[name] + j
        return pv.t[:, c:c + 1]

    blocks = [(0, CTX, 'c')] + [(CTX + i * 512, 512, 'x') for i in range(SEQ // 512)]

    def run_window(gens, width):
        gens = list(gens)
        active = []
        while gens or active:
            while gens and len(active) < width:
                active.append(gens.pop(0))
            for t in list(active):
                try:
                    next(t)
                except StopIteration:
                    active.remove(t)

    def run_tasks(tasks):
        tasks = [t for t in tasks if t is not None]
        while tasks:
            for t in list(tasks):
                try:
                    next(t)
                except StopIteration:
                    tasks.remove(t)

    def cast_gen(l, s, chunk, nbuf, engs, pref, ql="sp", qs="pool"):
        fb = [s.sb("%sf%d" % (pref, i), [128, chunk]) for i in range(nbuf)]
        bb = [s.sb("%sb%d" % (pref, i), [128, chunk], BF16) for i in range(nbuf)]

        def gen():
            allw = ((win_d, winb_d, 8, INC), (wout_d, woutb_d, 8, D), (w1_d, w1b_d, 8, 2 * DFF), (w2_d, w2b_d, 22, D))
            items = []
            for src, dst, rows, cols in allw:
                sv = src[l].rearrange("p a b -> p (a b)")
                dv = dst[l].rearrange("p a b -> p (a b)")
                tot = rows * cols
                for o in range(0, tot, chunk):
                    items.append((sv, dv, o, min(chunk, tot - o)))

            def load(i):
                if i < len(items):
                    sv, dv, o, n = items[i]
                    tk.dma(ql, fb[i % nbuf].t[:, :n], sv[:, o:o + n], writes=[fb[i % nbuf].b])
            for i in range(nbuf - 1):
                load(i)
            for i, (sv, dv, o, n) in enumerate(items):
                f = fb[i % nbuf]
                b_ = bb[i % nbuf]
                e = engs[i % len(engs)]
                CP(b_.t[:, :n], f.t[:, :n], [f.b], [b_.b], eng=e)
                tk.dma(qs, dv[:, o:o + n], b_.t[:, :n], reads=[b_.b])
                load(i + nbuf - 1)
                yield
        return gen()

    def stage_cast_init():
        with Scope() as s:
            xb = [s.sb("ixb%d" % i, [128, 8, 512]) for i in range(2)]
            pb = [s.sb("ipb%d" % i, [128, 8, 512]) for i in range(2)]

            def init_gen():
                it = 0
                for b in range(NB):
                    tk.dma("pool", xs_d[b][:, :, 0:CTX], ctxT_d[b])
                    for i in range(SEQ // 512):
                        X = xb[it % 2]
                        P = pb[it % 2]
                        it += 1
                        tk.dma("sp", X.t[:], xT_d[b][:, :, i * 512:(i + 1) * 512], writes=[X.b])
                        tk.dma("sp", P.t[:], posT_d[:, :, i * 512:(i + 1) * 512], writes=[P.b])
                        yield
                        TT(X.t[:], X.t[:], P.t[:], ALU.add, [X.b, P.b], [X.b], eng="pool")
                        tk.dma("pool", xs_d[b][:, :, CTX + i * 512:CTX + (i + 1) * 512], X.t[:], reads=[X.b])
                        yield
            shared = ([s.sb("wm%d" % i, [128, 8, 512]) for i in range(2)], s.sb("mrow", [4, 6 * D]))

            def mods():
                for l in range(1):
                    for _ in mod_gen(l, s, shared):
                        yield
            run_tasks([init_gen(), mods()])
        tk.barrier()

    def mod_gen(l, s, shared):
        mod_, pv_l, omka_ = modL[l], pvL[l], omkaL[l]
        wm, mrow = shared
        bm = s.sb("bm%d" % l, [128, 48])

        def gen():
            tk.dma("sp", bm.t[:], bmod_d[l], writes=[bm.b])
            tk.dma("sp", pv_l.t[:], pv_d[l], writes=[pv_l.b])
            for g in range(12):
                W = wm[g % 2]
                tk.dma("sp", W.t[:], wmod_d[l][:, :, g * 512:(g + 1) * 512], writes=[W.b])
                P = nb_()
                for kc in range(8):
                    PE(P.t[0:4, :], sil.t[:, kc, :], W.t[:, kc, :], kc == 0, kc == 7, [W.b, sil.b], [P.b])
                CP(mrow.t[:, g * 512:(g + 1) * 512], P.t[0:4, :], [P.b], [mrow.b], eng="dve")
                yield
            PT = nb_()
            ptv = PT.t[:, 0:192].rearrange("p (a b) -> p a b", b=4)
            for j in range(48):
                PET(ptv[:, j, :], mrow.t[:, j * 128:(j + 1) * 128], [mrow.b], [PT.b])
            TT(mod_.t[:, :, :], ptv[:, :, 0:3], bm.t[:, :].unsqueeze(2).to_broadcast([128, 48, 3]), ALU.add, [PT.b, bm.b], [mod_.b])
            for w in (1, 4):
                TS(mod_.t[:, w * 8:(w + 1) * 8, :], mod_.t[:, w * 8:(w + 1) * 8, :], 1.0, None, ALU.add, None, [mod_.b], [mod_.b])
            TS(omka_.t[:], pv_l.t[:, PV["ka"]:PV["ka"] + 2], -1.0, 1.0, ALU.mult, ALU.add, [pv_l.b], [omka_.b])
            for w in (2, 5):
                TS(mod_.t[:, w * 8:(w + 1) * 8, :], mod_.t[:, w * 8:(w + 1) * 8, :], 1.0 / ALPHA, None, ALU.mult, None, [mod_.b], [mod_.b])
            yield
        return gen()

    def mcol(which, kc, ni):
        return mod.t[:, which * 8 + kc, ni:ni + 1]

    def stage_A(l, last, win, winB):
        with Scope() as s:
            xsb = [s.sb("axs%d" % i, [128, 8, 512]) for i in range(2)]
            hx = [s.sb("ahx%d" % i, [128, 8, 512], BF16) for i in range(2)]
            hxB = [[Buf() for _ in range(8)] for i in range(2)]
            stg = [s.sb("astg%d" % i, [128, 512]) for i in range(4)]
            work = [(b, t0, n, kind) for b in range(NB) for (t0, n, kind) in blocks]
            ie = 0

            def load(k):
                b, t0, n, kind = work[k]
                X = xsb[k % 2]
                tk.dma("sp", X.t[:, :, :n], xs_d[b][:, :, t0:t0 + n], writes=[X.b])

            def domod(k):
                b, t0, n, kind = work[k]
                ni = 2 if kind == 'c' else b
                X, H = xsb[k % 2], hx[k % 2]
                for kc in range(8):
                    MOD(H.t[:, kc, :n], X.t[:, kc, :n], mcol(1, kc, ni), mcol(0, kc, ni), [X.b, mod.b], [hxB[k % 2][kc]])

            load(0)
            domod(0)
            for k in range(len(work)):
                b, t0, n, kind = work[k]
                H = hx[k % 2]
                nt = 9 if (last and kind == 'c') else 17
                for ct in range(nt):
                    if ct == 1 and k + 1 < len(work):
                        load(k + 1)
                    if ct == 7 and k + 1 < len(work):
                        domod(k + 1)
                    P = nb_()
                    for kc in range(8):
                        PE(P.t[:, :n], win.t[:, kc, ct * 128:(ct + 1) * 128], H.t[:, kc, :n], kc == 0, kc == 7,
                           [winB[kc], hxB[k % 2][kc]], [P.b])
                    S = stg[ie % 4]
                    ie += 1
                    CP(S.t[:, :n], P.t[:, :n], [P.b], [S.b])
                    tk.dma("pool", px_d[b, ct][:, t0:t0 + n], S.t[:, :n], reads=[S.b])
        tk.barrier()

    def stage_R(l, last):
        rwkv_pass(l, last)
        tk.barrier()

    def rwkv_pass(l, last):
        BK = 256
        NQ = BK // CH
        xdt = BF16
        base = [(0, CTX, 'c')] + [(CTX + i * BK, BK, 'x') for i in range(SEQ // BK)]
        blks = []
        for d_ in range(2):
            for b_ in range(NB):
                bl = base if d_ == 0 else [base[0]] + base[:0:-1]
                for i_, (t0_, n_, k_) in enumerate(bl):
                    blks.append((t0_, n_, k_, b_, d_, i_ == 0))
        yfd_b = [Buf("yfd%d" % i) for i in range(NB)]
        nblk = len(blks)
        with Scope() as s:
            n = BK
            w2s = s.sb("w2s", [128, 256])
            a2s = s.sb("a2s", [128, 256])
            g2 = s.sb("g2", [128, 256])
            tk.dma("sp", w2s.t[:], w2s_d[l], writes=[w2s.b])
            tk.dma("sp", a2s.t[:], a2s_d[l], writes=[a2s.b])
            tk.dma("sp", g2.t[:], g2_d[l], writes=[g2.b])
            zp = [[s.sb("zp%d_%d" % (p, i), [128, BK + 2]) for i in range(6)] for p in range(2)]
            lo6 = [s.sb("lo6_%d" % p, [128, BK]) for p in range(2)]
            lo7 = [s.sb("lo7_%d" % p, [128, BK]) for p in range(3)]
            lo8 = [s.sb("lo8_%d" % p, [128, BK]) for p in range(3)]
            cv = [[s.sb("cv%d_%d" % (p, i), [128, BK]) for i in range(6)] for p in range(3)]
            aaT = [[s.sb("aa%d_%d" % (p, m), [128, BK]) for m in range(2)] for p in range(3)]
            names = ["tdw", "lw", "Pp", "Qq", "Gx", "Gi", "eG", "eGn", "eGx", "eH", "kraw", "sq",
                     "rs", "kkn", "tmp", "kd", "bq"]
            tbM = [{nm: s.sb("r%d_%s" % (mm, nm), [128, BK]) for nm in names} for mm in range(2)]
            tbM[1]["tdw"] = tbM[0]["tdw"]
            tb = tbM[0]
            RpT = [[s.sb("Rp%d_%d" % (p, m), [128, BK]) for m in range(2)] for p in range(2)]
            RpbT = [[s.sb("Rpb%d_%d" % (p, m), [128, BK], BF16) for m in range(2)] for p in range(2)]
            etotT = [[s.sb("etot%d_%d" % (p, m), [128, NQ]) for m in range(2)] for p in range(2)]
            DgT = [[s.sb("Dg%d_%d" % (p, m), [128, NQ, 128]) for m in range(2)] for p in range(2)]
            exn = ["Ab", "Bb", "Kb", "Bh", "Kh", "Vb"]
            exT = [[{nm: s.sb("x%d%d_%s" % (p, m, nm), [128, NQ, 2, 64], xdt) for nm in exn} for m in range(2)] for p in range(2)]
            for p in range(2):
                for m in range(2):
                    for nm in exn:
                        MEMSET(exT[p][m][nm].t[:], 0.0, [exT[p][m][nm].b], eng="pool" if (m + p) % 2 else "dve")
            qn = ["X0", "X1", "XT0", "XT1", "T0", "T1", "W1", "W2", "BhT", "KhT", "P1T", "P2T", ]
            qT = [{nm: s.sb("q%d_%s" % (m, nm), [128, NQ, 128], xdt) for nm in qn} for m in range(2)]
            MbrT = [s.sb("Mbr%d" % m, [128, NQ, 64], xdt) for m in range(2)]
            MkrT = [s.sb("Mkr%d" % m, [128, NQ, 64], xdt) for m in range(2)]
            QG = [[dict(Q1=s.sb("Q1_%d%d" % (p, m), [128, NQ, 64], xdt), Q2=s.sb("Q2_%d%d" % (p, m), [128, NQ, 64], xdt),
                        G1=s.sb("G1_%d%d" % (p, m), [128, NQ, 128], xdt), G2=s.sb("G2_%d%d" % (p, m), [128, NQ, 128], xdt),
                        Vt=s.sb("Vt_%d%d" % (p, m), [128, NQ, 128], xdt)) for m in range(2)] for p in range(2)]
            St = [[s.sb("St%d%d" % (m, i), [128, 128], xdt) for i in range(2)] for m in range(2)]
            sti = [0, 0]
            for m in range(2):
                MEMSET(St[m][0].t[:], 0.0, [St[m][0].b])
            ybT = [[s.sb("yb%d_%d" % (p, m), [128, BK]) for m in range(2)] for p in range(2)]
            if True:
                yfT = [[s.sb("yf%d_%d" % (p, m), [128, BK]) for m in range(2)] for p in range(3)]
                ro = {nm: s.sb("ro_" + nm, [128, BK]) for nm in ["ys", "sq", "mu", "var", "rstd", "bon", "sg", "af", "t2"]}
                ob = [s.sb("ob%d" % m, [128, BK], BF16) for m in range(2)]
            PYb = [banks[6], banks[7]]

            def nbk():
                b_ = banks[bank_i[0] % 6]
                bank_i[0] += 1
                return b_

            def is_readout(k):
                return (blks[k][4] == 1) and not (last and blks[k][2] == 'c')

            def prep(k):
                t0, _, kind, b, d, _f = blks[k]
                p2, p3 = k % 2, k % 3
                s0, L = (0, CTX) if kind == 'c' else (CTX, SEQ)
                first = (t0 == s0)
                lastb = (t0 + n == s0 + L)
                for ct in range(6):
                    Z = zp[p2][ct]
                    lo_ = t0 - (0 if first else 1)
                    hi_ = t0 + n + (0 if lastb else 1)
                    o0 = 1 - (t0 - lo_)
                    tk.dma("sp", Z.t[:, o0:o0 + (hi_ - lo_)], px_d[b, ct][:, lo_:hi_], writes=[Z.b])
                    if first:
                        MEMSET(Z.t[:, 0:1], 0.0, [Z.b])
                    if lastb:
                        MEMSET(Z.t[:, n + 1:n + 2], 0.0, [Z.b])
                L6, L7, L8 = lo6[p2], lo7[p3], lo8[p3]
                tk.dma("sp", L6.t[:, :], px_d[b, 6][:, t0:t0 + n], writes=[L6.b])
                tk.dma("sp", L7.t[:, :], px_d[b, 7][:, t0:t0 + n], writes=[L7.b])
                if is_readout(k):
                    tk.dma("sp", L8.t[:, :], px_d[b, 8][:, t0:t0 + n], writes=[L8.b])
                    for m in range(2):
                        tk.dma("sp", yfT[p3][m].t[:, :], yf_d[b, m][:, t0:t0 + n], reads=[yfd_b[b]], writes=[yfT[p3][m].b])
                yield
                for ct in range(6):
                    Z = zp[p2][ct]
                    C = cv[p3][ct]
                    ACT(C.t[:, :], Z.t[:, 1:n + 1], AF.Identity, [Z.b, pv.b], [C.b], scale=pvc("conv", 6 + ct))
                    STT(C.t[:, :], Z.t[:, 0:n], pvc("conv", ct), C.t[:, :], ALU.mult, ALU.add, [Z.b, pv.b, C.b], [C.b])
                    STT(C.t[:, :], Z.t[:, 2:n + 2], pvc("conv", 12 + ct), C.t[:, :], ALU.mult, ALU.add, [Z.b, pv.b, C.b], [C.b])
                    yield
                hs = slice(d * 64, (d + 1) * 64)
                ACT(tb["tdw"].t[hs, :], L6.t[hs, :], AF.Tanh, [L6.b], [tb["tdw"].b])

                def pm(m, tb):
                    rr_, kk_, vv_ = cv[p3][m], cv[p3][2 + m], cv[p3][4 + m]
                    ms = slice(m * 128, (m + 1) * 128)
                    aa = aaT[p3][m]
                    Rp, Rpb, etot, ex = RpT[p2][m], RpbT[p2][m], etotT[p2][m], exT[p2][m]
                    P = nbk()
                    PE(P.t[:, :n], w2s.t[hs, ms], tb["tdw"].t[hs, :], True, True, [w2s.b, tb["tdw"].b], [P.b])
                    ACT(tb["lw"].t[:, :], P.t[:, :n], AF.Sigmoid, [P.b, pv.b], [tb["lw"].b], bias=pvc("w0", d * 2 + m))
                    ACT(tb["lw"].t[:, :], tb["lw"].t[:, :], AF.Identity, [tb["lw"].b], [tb["lw"].b], scale=-math.exp(-0.5))
                    P = nbk()
                    PE(P.t[:, :n], a2s.t[hs, ms], L7.t[hs, :], True, True, [a2s.b, L7.b], [P.b])
                    ACT(aa.t[:, :], P.t[:, :n], AF.Sigmoid, [P.b, pv.b], [aa.b], bias=pvc("a0", d * 2 + m))
                    yield
                    lw = tb["lw"]
                    Pp, Qq, Gx, Gi = tb["Pp"], tb["Qq"], tb["Gx"], tb["Gi"]
                    tk.op("dve", lambda e: e.tensor_tensor_scan(out=Pp.t[:, :], data0=scanmask.t[:, :n], data1=lw.t[:, :],
                                                                 initial=0.0, op0=ALU.mult, op1=ALU.add),
                          [scanmask.b, lw.b], [Pp.b])
                    P3 = Pp.t[:, :].rearrange("p (c j) -> p c j", j=CH)
                    tot_b = P3[:, :, CH - 1:CH].to_broadcast([128, NQ, CH])
                    TT(Qq.t[:, :].rearrange("p (c j) -> p c j", j=CH), tot_b, P3, ALU.subtract, [Pp.b], [Qq.b])
                    TT(Gx.t[:, :], Pp.t[:, :], lw.t[:, :], ALU.subtract, [Pp.b, lw.b], [Gx.b], eng="pool")
                    if d == 0:
                        Gin, Gex, Hh = Pp, Gx, Qq
                    else:
                        TT(Gi.t[:, :], Qq.t[:, :], lw.t[:, :], ALU.add, [Qq.b, lw.b], [Gi.b], eng="pool")
                        Gin, Gex, Hh = Gi, Qq, Gx
                    yield
                    ACT(tb["eG"].t[:, :], Gin.t[:, :], AF.Exp, [Gin.b], [tb["eG"].b])
                    ACT(tb["eGn"].t[:, :], Gin.t[:, :], AF.Exp, [Gin.b], [tb["eGn"].b], scale=-1.0)
                    ACT(tb["eGx"].t[:, :], Gex.t[:, :], AF.Exp, [Gex.b], [tb["eGx"].b])
                    ACT(tb["eH"].t[:, :], Hh.t[:, :], AF.Exp, [Hh.b], [tb["eH"].b])
                    ACT(etot.t[:, :], P3[:, :, CH - 1], AF.Exp, [Pp.b], [etot.b])
                    TT(DgT[p2][m].t[:], id4.t[:], etot.t[:, :].unsqueeze(2).to_broadcast([128, NQ, 128]), ALU.mult,
                       [id4.b, etot.b], [DgT[p2][m].b], eng="pool")
                    yield
                    kraw, sq, rs, kkn = tb["kraw"], tb["sq"], tb["rs"], tb["kkn"]
                    ACT(kraw.t[:, :], kk_.t[:, :], AF.Identity, [kk_.b, pv.b], [kraw.b], scale=pvc("kk", m))
                    ACT(sq.t[:, :], kk_.t[:, :], AF.Square, [kk_.b, pv.b], [sq.b], scale=pvc("kk", m))
                    P = nbk()
                    PE(P.t[:, :n], headones, sq.t[:, :], True, True, [cstb, sq.b], [P.b])
                    ACT(rs.t[:, :], P.t[:, :n], AF.Ln, [P.b], [rs.b], bias=eps_kk.t[:, 0:1])
                    ACT(rs.t[:, :], rs.t[:, :], AF.Exp, [rs.b], [rs.b], scale=-0.5)
                    TT(kkn.t[:, :], kraw.t[:, :], rs.t[:, :], ALU.mult, [kraw.b, rs.b], [kkn.b])
                    yield
                    tmp, kd, bq = tb["tmp"], tb["kd"], tb["bq"]
                    TS(tmp.t[:, :], aa.t[:, :], pvc("ka", m), omka.t[:, m:m + 1], ALU.mult, ALU.add, [aa.b, pv.b, omka.b], [tmp.b], eng="pool")
                    TT(kd.t[:, :], kk_.t[:, :], tmp.t[:, :], ALU.mult, [kk_.b, tmp.b], [kd.b], eng="pool")
                    TT(bq.t[:, :], kkn.t[:, :], aa.t[:, :], ALU.mult, [kkn.b, aa.b], [bq.b])
                    TT(Rp.t[:, :], rr_.t[:, :], tb["eG"].t[:, :], ALU.mult, [rr_.b, tb["eG"].b], [Rp.b])
                    CP(Rpb.t[:, :], Rp.t[:, :], [Rp.b], [Rpb.b], eng="pool")
                    yield

                    def v3(tbuf, hh):
                        return tbuf.t[hh * 64:(hh + 1) * 64, :].rearrange("p (c j) -> p c j", j=CH)

                    for hh in range(2):
                        def xo(nm):
                            return ex[nm].t[hh * 64:(hh + 1) * 64, :, hh, :]
                        STT(xo("Ab"), v3(kkn, hh), -1.0, v3(tb["eGx"], hh), ALU.mult, ALU.mult, [kkn.b, tb["eGx"].b], [ex["Ab"].b])
                        TT(xo("Bb"), v3(bq, hh), v3(tb["eGn"], hh), ALU.mult, [bq.b, tb["eGn"].b], [ex["Bb"].b])
                        TT(xo("Kb"), v3(kd, hh), v3(tb["eGn"], hh), ALU.mult, [kd.b, tb["eGn"].b], [ex["Kb"].b], eng="pool")
                        TT(xo("Bh"), v3(bq, hh), v3(tb["eH"], hh), ALU.mult, [bq.b, tb["eH"].b], [ex["Bh"].b])
                        TT(xo("Kh"), v3(kd, hh), v3(tb["eH"], hh), ALU.mult, [kd.b, tb["eH"].b], [ex["Kh"].b], eng="pool")
                        CP(xo("Vb"), v3(vv_, hh), [vv_.b], [ex["Vb"].b], eng="pool")
                        yield

                alive = [pm(0, tbM[0]), pm(1, tbM[1])]
                while alive:
                    for g in list(alive):
                        try:
                            next(g)
                        except StopIteration:
                            alive.remove(g)
                    yield

            def chain(k, m):
                p2 = k % 2
                d = blks[k][4]
                maskN, maskNT, maskM = (SU4, SL4, UI4) if d == 0 else (SL4, SU4, LI4)
                Rp, Rpb, etot, ex = RpT[p2][m], RpbT[p2][m], etotT[p2][m], exT[p2][m]
                Dg = DgT[p2][m]
                q = qT[m]
                Mbr, Mkr = MbrT[m], MkrT[m]
                o = QG[p2][m]
                Q1, Q2, G1, G2, Vt = o["Q1"], o["Q2"], o["G1"], o["G2"], o["Vt"]

                def xc(nm, c):
                    return ex[nm].t[:, c, :, :].rearrange("p a b -> p (a b)")

                def quad_mm(lhs_fn, rhs_fn, ncol, reads):
                    P = nbk()
                    pv_ = P.t[:, 0:4 * ncol].rearrange("p (a b) -> p a b", a=4)
                    for c in range(NQ):
                        PE(pv_[:, c, :], lhs_fn(c), rhs_fn(c), True, True, reads, [P.b])
                    return P, pv_

                X0, X1, XT0, XT1, T0, T1 = q["X0"], q["X1"], q["XT0"], q["XT1"], q["T0"], q["T1"]
                W1, W2, BhT, KhT, P1T, P2T = q["W1"], q["W2"], q["BhT"], q["KhT"], q["P1T"], q["P2T"]
                P, pv_ = quad_mm(lambda c: xc("Bb", c), lambda c: xc("Ab", c), 128, [ex["Bb"].b, ex["Ab"].b])
                TT(X0.t[:], pv_, maskN, ALU.mult, [P.b, cstb], [X0.b])
                TT(T0.t[:], X0.t[:], id4.t[:], ALU.add, [X0.b, id4.b], [T0.b], eng="pool")
                yield
                P, pv_ = quad_mm(lambda c: xc("Ab", c), lambda c: xc("Bb", c), 128, [ex["Bb"].b, ex["Ab"].b])
                TT(XT0.t[:], pv_, maskNT, ALU.mult, [P.b, cstb], [XT0.b])
                yield
                Xc, XTc, Tc = X0, XT0, T0
                Xn, XTn, Tn = X1, XT1, T1
                side = [
                    ("mm", lambda c: xc("Ab", c), lambda c: xc("Kb", c), 128, [ex["Kb"].b, ex["Ab"].b], W2, maskNT),
                    ("mm", lambda c: xc("Bb", c), lambda c: Rpb.t[:, c * 64:(c + 1) * 64], 64, [ex["Bb"].b, Rpb.b], Mbr, maskM),
                    ("mm", lambda c: xc("Kb", c), lambda c: Rpb.t[:, c * 64:(c + 1) * 64], 64, [ex["Kb"].b, Rpb.b], Mkr, maskM),
                    ("tr", "Ab", W1), ("tr", "Bh", BhT), ("tr", "Kh", KhT), ("tr", "Vb", Vt)]

                def do_side():
                    if not side:
                        return
                    it = side.pop(0)
                    if it[0] == "mm":
                        _, lf, rf, nco, rd, dst, msk = it
                        P, pv_ = quad_mm(lf, rf, nco, rd)
                        TT(dst.t[:], pv_, msk, ALU.mult, [P.b, cstb], [dst.b])
                    else:
                        _, nm, dst = it
                        P = nbk()
                        pv_ = P.t[:, 0:256].bitcast(BF16).rearrange("p (a b) -> p a b", a=4)
                        for c in range(NQ):
                            PET(pv_[:, c, :], xc(nm, c), [ex[nm].b], [P.b])
                        CP(dst.t[:], pv_, [P.b], [dst.b], eng="act")

                for lev in range(1, 6):
                    if lev < 5:
                        P, pv_ = quad_mm(lambda c: XTc.t[:, c, :], lambda c: Xc.t[:, c, :], 128, [XTc.b, Xc.b])
                        CP(Xn.t[:], pv_, [P.b], [Xn.b], eng="act")
                    P, pv_ = quad_mm(lambda c: Xc.t[:, c, :], lambda c: XTc.t[:, c, :], 128, [XTc.b, Xc.b])
                    CP(XTn.t[:], pv_, [P.b], [XTn.b], eng="act")
                    do_side()
                    yield
                    P, pv_ = quad_mm(lambda c: XTn.t[:, c, :], lambda c: Tc.t[:, c, :], 128, [XTn.b, Tc.b])
                    TT(Tn.t[:], pv_, Tc.t[:], ALU.add, [P.b, Tc.b], [Tn.b])
                    do_side()
                    yield
                    Xc, Xn = Xn, Xc
                    XTc, XTn = XTn, XTc
                    Tc, Tn = Tn, Tc
                while side:
                    do_side()
                    yield
                P, pv_ = quad_mm(lambda c: Tc.t[:, c, :], lambda c: W1.t[:, c, :], 128, [Tc.b, W1.b])
                CP(P1T.t[:], pv_, [P.b], [P1T.b], eng="act")
                P, pv_ = quad_mm(lambda c: Tc.t[:, c, :], lambda c: W2.t[:, c, :], 128, [Tc.b, W2.b])
                CP(P2T.t[:], pv_, [P.b], [P2T.b], eng="act")
                yield
                P, pv_ = quad_mm(lambda c: P1T.t[:, c, :], lambda c: Mbr.t[:, c, :], 64, [P1T.b, Mbr.b])
                TT(Q1.t[:], pv_, Rp.t[:, :].rearrange("p (c j) -> p c j", j=CH), ALU.add, [P.b, Rp.b], [Q1.b])
                P, pv_ = quad_mm(lambda c: P2T.t[:, c, :], lambda c: Mbr.t[:, c, :], 64, [P2T.b, Mbr.b])
                TT(Q2.t[:], pv_, Mkr.t[:], ALU.add, [P.b, Mkr.b], [Q2.b])
                yield
                P, pv_ = quad_mm(lambda c: P1T.t[:, c, :], lambda c: BhT.t[:, c, :], 128, [P1T.b, BhT.b])
                TT(G1.t[:], pv_, Dg.t[:], ALU.add, [P.b, Dg.b], [G1.b])
                P, pv_ = quad_mm(lambda c: P2T.t[:, c, :], lambda c: BhT.t[:, c, :], 128, [P2T.b, BhT.b])
                TT(G2.t[:], pv_, KhT.t[:], ALU.add, [P.b, KhT.b], [G2.b])
                yield

            def seq(k):
                t0, _, kind, b, d, fpass = blks[k]
                p2, p3 = k % 2, k % 3
                if fpass:
                    for m in range(2):
                        MEMSET(St[m][sti[m] % 2].t[:], 0.0, [St[m][sti[m] % 2].b], eng="dve")
                corder = list(range(NQ)) if d == 0 else list(range(NQ - 1, -1, -1))
                for ci, c in enumerate(corder):
                    for m in range(2):
                        o = QG[p2][m]
                        Sc = St[m][sti[m] % 2]
                        Sn = St[m][(sti[m] + 1) % 2]
                        sti[m] += 1
                        PY = PYb[m]
                        yv = PY.t[:, 0:256].rearrange("p (a b) -> p a b", a=4)[:, c, :]
                        PE(yv, Sc.t[:], o["Q1"].t[:, c, :], True, False, [Sc.b, o["Q1"].b], [PY.b])
                        PE(yv, o["Vt"].t[:, c, :], o["Q2"].t[:, c, :], False, True, [o["Vt"].b, o["Q2"].b], [PY.b])
                        PS = nbk()
                        PE(PS.t[:, 0:128], o["G1"].t[:, c, :], Sc.t[:], True, False, [o["G1"].b, Sc.b], [PS.b])
                        PE(PS.t[:, 0:128], o["G2"].t[:, c, :], o["Vt"].t[:, c, :], False, True, [o["G2"].b, o["Vt"].b], [PS.b])
                        CP(Sn.t[:], PS.t[:, 0:128], [PS.b], [Sn.b], eng="act")
                    yield
                for m in range(2):
                    CP(ybT[p2][m].t[:, :], PYb[m].t[:, 0:256], [PYb[m].b], [ybT[p2][m].b], eng="act")
                yield
                if d == 0:
                    for m in range(2):
                        tk.dma("pool", yf_d[b, m][:, t0:t0 + n], ybT[p2][m].t[:, :], reads=[ybT[p2][m].b], writes=[yfd_b[b]])
                    return
                if not is_readout(k):
                    return
                L7, L8 = lo7[p3], lo8[p3]
                ACT(ro["sg"].t[:, :], L8.t[:, :], AF.Sigmoid, [L8.b], [ro["sg"].b])
                for m in range(2):
                    rr_, kk_, vv_ = cv[p3][m], cv[p3][2 + m], cv[p3][4 + m]
                    ms = slice(m * 128, (m + 1) * 128)
                    aa = aaT[p3][m]
                    yb, yf = ybT[p2][m], yfT[p3][m]
                    P = nbk()
                    PE(P.t[:, :n], a2s.t[0:64, ms], L7.t[0:64, :], True, True, [a2s.b, L7.b], [P.b])
                    af = ro["af"]
                    ACT(af.t[:, :], P.t[:, :n], AF.Sigmoid, [P.b, pv.b], [af.b], bias=pvc("a0", m))
                    t2 = ro["t2"]
                    TT(t2.t[:, :], af.t[:, :], aa.t[:, :], ALU.add, [af.b, aa.b], [t2.b], eng="pool")
                    TS(t2.t[:, :], t2.t[:, :], pvc("ka", m), None, ALU.mult, None, [t2.b, pv.b], [t2.b])
                    STT(t2.t[:, :], omka.t[:, m:m + 1].to_broadcast([128, n]), 2.0, t2.t[:, :], ALU.mult, ALU.add,
                        [omka.b, t2.b], [t2.b])
                    TT(t2.t[:, :], t2.t[:, :], kk_.t[:, :], ALU.mult, [t2.b, kk_.b], [t2.b])
                    TT(t2.t[:, :], t2.t[:, :], rr_.t[:, :], ALU.mult, [t2.b, rr_.b], [t2.b])
                    TS(t2.t[:, :], t2.t[:, :], pvc("rk", m), None, ALU.mult, None, [t2.b, pv.b], [t2.b])
                    yield
                    PB = nbk()
                    PE(PB.t[:, :n], headones, t2.t[:, :], True, True, [cstb, t2.b], [PB.b])
                    bon = ro["bon"]
                    TT(bon.t[:, :], PB.t[:, :n], vv_.t[:, :], ALU.mult, [PB.b, vv_.b], [bon.b])
                    ys, sq2, mu, var, rstd = ro["ys"], ro["sq"], ro["mu"], ro["var"], ro["rstd"]
                    TT(ys.t[:, :], yf.t[:, :], yb.t[:, :], ALU.add, [yf.b, yb.b], [ys.b], eng="pool")
                    TT(sq2.t[:, :], ys.t[:, :], ys.t[:, :], ALU.mult, [ys.b], [sq2.b], eng="pool")
                    yield
                    P1 = nbk()
                    PE(P1.t[:, :n], headones, ys.t[:, :], True, True, [cstb, ys.b], [P1.b])
                    P2 = nbk()
                    PE(P2.t[:, :n], headones, sq2.t[:, :], True, True, [cstb, sq2.b], [P2.b])
                    TS(mu.t[:, :], P1.t[:, :n], 1.0 / 64, None, ALU.mult, None, [P1.b], [mu.b])
                    TT(var.t[:, :], mu.t[:, :], mu.t[:, :], ALU.mult, [mu.b], [var.b])
                    STT(var.t[:, :], P2.t[:, :n], 1.0 / 64, var.t[:, :], ALU.mult, ALU.subtract, [P2.b, var.b], [var.b])
                    ACT(rstd.t[:, :], var.t[:, :], AF.Ln, [var.b], [rstd.b], bias=eps_gn.t[:, 0:1])
                    ACT(rstd.t[:, :], rstd.t[:, :], AF.Exp, [rstd.b], [rstd.b], scale=-0.5)
                    yield
                    TT(ys.t[:, :], ys.t[:, :], mu.t[:, :], ALU.subtract, [ys.b, mu.b], [ys.b], eng="pool")
                    TT(ys.t[:, :], ys.t[:, :], rstd.t[:, :], ALU.mult, [ys.b, rstd.b], [ys.b])
                    TS(ys.t[:, :], ys.t[:, :], pvc("gng", m), pvc("gnb", m), ALU.mult, ALU.add, [ys.b, pv.b], [ys.b])
                    TT(ys.t[:, :], ys.t[:, :], bon.t[:, :], ALU.add, [ys.b, bon.b], [ys.b])
                    PG = nbk()
                    PE(PG.t[:, :n], g2.t[:, ms], ro["sg"].t[:, :], True, True, [g2.b, ro["sg"].b], [PG.b])
                    TT(ob[m].t[:, :], ys.t[:, :], PG.t[:, :n], ALU.mult, [ys.b, PG.b], [ob[m].b])
                    tk.dma("pool", mix_d[b][:, m, t0:t0 + n], ob[m].t[:, :], reads=[ob[m].b])
                    yield

            run_tasks([prep(0)])
            for r in range(nblk + 1):
                tasks = []
                if r < nblk:
                    tasks += [chain(r, 0), chain(r, 1)]
                if r + 1 < nblk:
                    tasks.append(prep(r + 1))
                if r >= 1:
                    tasks.append(seq(r - 1))
                run_tasks(tasks)

    def seqs(last):
        return [(CTX, SEQ, 'x')] if last else [(0, CTX, 'c'), (CTX, SEQ, 'x')]

    def stage_P(l, last):
        with Scope() as s:
            pw = s.sb("pw", [128, 2, 128])
            pwb = s.sb("pwb", [128, 2, 128], BF16)
            tk.dma("sp", pw.t[:], poolbd_d[l], writes=[pw.b])
            CP(pwb.t[:], pw.t[:], [pw.b], [pwb.b])
            rcx = s.sb("rcx", [128, 2, SEQ])
            rcc = s.sb("rcc", [128, 2, CTX])
            tk.dma("sp", rcx.t[:], rc_x_d[:, :, :], writes=[rcx.b])
            tk.dma("sp", rcc.t[:], rc_c_d[:, :, :], writes=[rcc.b])
            PADW = SEQ + 16
            zz = [s.sb("pz%d" % i, [128, PADW]) for i in range(2)]
            sA = [s.sb("ps%d" % i, [128, PADW]) for i in range(4)]
            acc = s.sb("pacc", [128, SEQ])
            dd = [s.sb("pdd%d" % i, [128, SEQ], BF16) for i in range(2)]
            ob = [s.sb("pob%d" % i, [128, 512], BF16) for i in range(2)]
            io = 0
            units = [(b, s0, L, kind, m) for b in range(NB) for (s0, L, kind) in seqs(last) for m in range(2)]

            def load(u):
                b, s0, L, kind, m = units[u]
                z = zz[u % 2]
                MEMSET(z.t[:, 0:8], 0.0, [z.b])
                MEMSET(z.t[:, 8 + L:16 + L], 0.0, [z.b])
                tk.dma("sp", z.t[:, 8:8 + L], px_d[b, 9 + m][:, s0:s0 + L], writes=[z.b])

            load(0)

            def pgen():
              io = 0
              for u, (b, s0, L, kind, m) in enumerate(units):
                  rc = rcc if kind == 'c' else rcx
                  z = zz[u % 2]
                  D_ = dd[u % 2]
                  if u + 1 < len(units):
                      load(u + 1)
                  prev = z
                  step = 1
                  wl = L + 16
                  nlev = 2 if m == 0 else 4
                  for lev in range(nlev):
                      cur = sA[lev]
                      wl2 = wl - step
                      TT(cur.t[:, 0:wl2], prev.t[:, 0:wl2], prev.t[:, step:step + wl2], ALU.add, [prev.b], [cur.b])
                      prev = cur
                      step *= 2
                      wl = wl2
                  for hh in range(2):
                      w = POOLW[2 * m + hh]
                      lev = int(math.log2(w)) - 1
                      hs = slice(hh * 64, (hh + 1) * 64)
                      o = 8 - w // 2
                      TT(acc.t[hs, :L], sA[lev].t[hs, o:o + L], rc.t[hs, m, :L], ALU.mult, [sA[lev].b, rc.b], [acc.b])
                  TT(D_.t[:, :L], acc.t[:, :L], z.t[:, 8:8 + L], ALU.subtract, [acc.b, z.b], [D_.b])
                  yield
                  for o in range(0, L, 512):
                      n = min(512, L - o)
                      P = nb_()
                      PE(P.t[:, :n], pwb.t[:, m, :], D_.t[:, o:o + n], True, True, [pwb.b, D_.b], [P.b])
                      O = ob[io % 2]
                      io += 1
                      ACT(O.t[:, :n], P.t[:, :n], AF.Identity, [P.b, pv.b], [O.b], scale=pvc("psc", m))
                      tk.dma("pool", mix_d[b][:, 2 + m, s0 + o:s0 + o + n], O.t[:, :n], reads=[O.b])
                      yield

            tasks = [pgen()]
            if l + 1 < nlayers:
                shared = ([s.sb("pwm%d" % i, [128, 8, 512]) for i in range(2)], s.sb("pmrow", [4, 6 * D]))
                tasks.append(mod_gen(l + 1, s, shared))
            run_tasks(tasks)
        tk.barrier()

    def stage_G(l, last):
        with Scope() as s:
            wsf = s.sb("wsf", [128, 4, 128])
            wsb = s.sb("wsb", [128, 4, 128], BF16)
            tk.dma("sp", wsf.t[:], wsT_d[l], writes=[wsf.b])
            CP(wsb.t[:], wsf.t[:], [wsf.b], [wsb.b])
            bsb = s.sb("bsb", [128, 2, 128])
            tk.dma("sp", bsb.t[:], bsb_d[l], writes=[bsb.b])
            glg = s.sb("glg", [128, 256])
            glb = s.sb("glb", [128, 256])
            tk.dma("sp", glg.t[:], glg_d[l], writes=[glg.b])
            tk.dma("sp", glb.t[:], glb_d[l], writes=[glb.b])
            zin = [s.sb("gz%d" % i, [128, 512]) for i in range(4)]
            NS = 2
            geS = [[s.sb("gg%d_%d" % (u, i), [128, 512]) for i in range(4)] for u in range(NS)]
            vtS = [s.sb("gvt%d" % u, [128, 4, 256]) for u in range(NS)]
            vsqS = [s.sb("gvsq%d" % u, [128, 4, 256]) for u in range(NS)]
            vnS = [s.sb("gvn%d" % u, [128, 4, 256], BF16) for u in range(NS)]
            stS = [[s.sb("gs%d_%d" % (u, i), [128, 16]) for i in range(4)] for u in range(NS)]
            svS = [s.sb("gsv%d" % u, [128, 4, 128]) for u in range(NS)]
            ob = [s.sb("gob%d" % i, [128, 512], BF16) for i in range(2)]
            cnt = [0]
            work = [(b, t0, n) for b in range(NB) for (t0, n, kind) in blocks if not (last and kind == 'c')]

            def blk(k):
                b, t0, n = work[k]
                u = k % NS
                ge, vt, vsq, vn, sv = geS[u], vtS[u], vsqS[u], vnS[u], svS[u]
                st1, st2, mu, rstd = stS[u]
                nck = n // 128
                for i in range(4):
                    tk.dma("sp", zin[i].t[:, :n], px_d[b, 11 + i][:, t0:t0 + n], writes=[zin[i].b])
                    ACT(ge[i].t[:, :n], zin[i].t[:, :n], AF.Gelu_apprx_tanh, [zin[i].b], [ge[i].b])
                yield
                for mm in range(2):
                    P = nb_()
                    pv_ = P.t[:, :].rearrange("p (a b) -> p a b", a=4)
                    for ck in range(nck):
                        PET(pv_[:, ck, :], ge[2 + mm].t[:, ck * 128:(ck + 1) * 128], [ge[2 + mm].b], [P.b])
                    CP(vt.t[:, :nck, mm * 128:(mm + 1) * 128], pv_[:, :nck, :], [P.b], [vt.b], eng="act")
                yield
                v4 = vt.t[:, :nck, :].rearrange("p a (g c) -> p (a g) c", c=64)
                ng = nck * 4
                tk.op("dve", lambda e: e.tensor_reduce(out=st1.t[:, :ng], in_=v4, axis=AX.X, op=ALU.add), [vt.b], [st1.b])
                ACT(vsq.t[:, :nck, :], vt.t[:, :nck, :], AF.Square, [vt.b], [vsq.b])
                q4 = vsq.t[:, :nck, :].rearrange("p a (g c) -> p (a g) c", c=64)
                tk.op("dve", lambda e: e.tensor_reduce(out=st2.t[:, :ng], in_=q4, axis=AX.X, op=ALU.add), [vsq.b], [st2.b])
                yield
                TS(mu.t[:, :ng], st1.t[:, :ng], 1.0 / 64, None, ALU.mult, None, [st1.b], [mu.b])
                TT(st1.t[:, :ng], mu.t[:, :ng], mu.t[:, :ng], ALU.mult, [mu.b], [st1.b])
                STT(st2.t[:, :ng], st2.t[:, :ng], 1.0 / 64, st1.t[:, :ng], ALU.mult, ALU.subtract, [st2.b, st1.b], [st2.b])
                ACT(rstd.t[:, :ng], st2.t[:, :ng], AF.Ln, [st2.b], [rstd.b], bias=eps_ln.t[:, 0:1])
                ACT(rstd.t[:, :ng], rstd.t[:, :ng], AF.Exp, [rstd.b], [rstd.b], scale=-0.5)
                yield
                TT(v4, v4, mu.t[:, :ng].unsqueeze(2).to_broadcast([128, ng, 64]), ALU.subtract, [vt.b, mu.b], [vt.b], eng="pool")
                yield
                TT(v4, v4, rstd.t[:, :ng].unsqueeze(2).to_broadcast([128, ng, 64]), ALU.mult, [vt.b, rstd.b], [vt.b])
                yield
                gb = glg.t[:, :].unsqueeze(1).to_broadcast([128, nck, 256])
                bb_ = glb.t[:, :].unsqueeze(1).to_broadcast([128, nck, 256])
                TT(vt.t[:, :nck, :], vt.t[:, :nck, :], gb, ALU.mult, [vt.b, glg.b], [vt.b])
                yield
                TT(vn.t[:, :nck, :], vt.t[:, :nck, :], bb_, ALU.add, [vt.b, glb.b], [vn.b], eng="pool")
                yield
                for mm in range(2):
                    PA = nb_()
                    PBk = nb_()
                    pa = PA.t[:, :].rearrange("p (a b) -> p a b", a=4)
                    pb_ = PBk.t[:, :].rearrange("p (a b) -> p a b", a=4)
                    for ck in range(nck):
                        PE(pa[:, ck, :], vn.t[:, ck, mm * 128:(mm + 1) * 128], wsb.t[:, 2 * mm, :], True, True, [vn.b, wsb.b], [PA.b])
                        PE(pb_[:, ck, :], vn.t[:, ck, mm * 128:(mm + 1) * 128], wsb.t[:, 2 * mm + 1, :], True, True, [vn.b, wsb.b], [PBk.b])
                    bs0 = bsb.t[0:64, mm, :].unsqueeze(1).to_broadcast([64, nck, 128])
                    bs1 = bsb.t[64:128, mm, :].unsqueeze(1).to_broadcast([64, nck, 128])
                    TT(sv.t[0:64, :nck, :], pa[0:64, :nck, :], bs0, ALU.add, [PA.b, bsb.b], [sv.b])
                    TT(sv.t[64:128, :nck, :], pb_[64:128, :nck, :], bs1, ALU.add, [PBk.b, bsb.b], [sv.b])
                    O = ob[cnt[0] % 2]
                    cnt[0] += 1
                    TT(O.t[:, :n], ge[mm].t[:, :n], sv.t[:, :nck, :].rearrange("p a b -> p (a b)"), ALU.mult,
                       [ge[mm].b, sv.b], [O.b])
                    tk.dma("pool", mix_d[b][:, 4 + mm, t0:t0 + n], O.t[:, :n], reads=[O.b])
                    yield

            run_window([blk(k) for k in range(len(work))], NS)
        tk.barrier()

    def stage_F(l, last, clx, slx, clxB, slxB):
        with Scope() as s:
            if not last:
                clc = s.sb("clc", [128, 2, CTX], BF16)
                slc = s.sb("slc", [128, 2, CTX], BF16)
                tk.dma("sp", clc.t[:], cl_c_d[:, :, :], writes=[clc.b])
                tk.dma("sp", slc.t[:], sl_c_d[:, :, :], writes=[slc.b])
            cs64 = s.sb("cs64", [128, 128], BF16)
            tk.dma("sp", cs64.t[:], cs64_d[:, :], writes=[cs64.b])
            fw = s.sb("fw", [128, 2, 128])
            fwb = s.sb("fwb", [128, 2, 128], BF16)
            tk.dma("sp", fw.t[:], fnetbd_d[l], writes=[fw.b])
            CP(fwb.t[:], fw.t[:], [fw.b], [fwb.b])
            zf = [s.sb("fz%d" % i, [128, 512]) for i in range(2)]
            units = [(b, s0, L, kind) for b in range(NB) for (s0, L, kind) in seqs(last)]
            zbU = [[s.sb("fzb%d_%d" % (u, i), [128, SEQ], BF16) for i in range(2)] for u in range(2)]
            zcsU = [s.sb("zcs%d" % u, [128, 16, 2, 256], BF16) for u in range(2)]
            fb = [s.sb("ffb%d" % i, [128, 512], BF16) for i in range(2)]
            ob = [s.sb("fob%d" % i, [128, 512], BF16) for i in range(2)]
            cnt = [0, 0, 0]

            def nbf():
                b_ = banks[bank_i[0] % 6]
                bank_i[0] += 1
                return b_

            def front(ui):
                b, s0, L, kind = units[ui]
                zb, zcs = zbU[ui % 2], zcsU[ui % 2]
                ntc = L // 128
                for m in range(2):
                    for o in range(0, L, 512):
                        n = min(512, L - o)
                        Z = zf[cnt[1] % 2]
                        cnt[1] += 1
                        tk.dma("sp", Z.t[:, :n], px_d[b, 15 + m][:, s0 + o:s0 + o + n], writes=[Z.b])
                        CP(zb[m].t[:, o:o + n], Z.t[:, :n], [Z.b], [zb[m].b])
                        yield
                for tc in range(ntc):
                    zv = zcs.t[:, tc, :, :].rearrange("p x (m h c) -> p h m x c", m=2, h=2)
                    for hh in range(2):
                        P = nbf()
                        hs = slice(hh * 64, (hh + 1) * 64)
                        for m in range(2):
                            PE(P.t[:, m * 128:(m + 1) * 128], zb[m].t[hs, tc * 128:(tc + 1) * 128], cs64.t[hs, :], True, True,
                               [zb[m].b, cs64.b], [P.b])
                        CP(zv[:, hh], P.t[:, 0:256].rearrange("p (m x c) -> p m x c", m=2, x=2), [P.b], [zcs.b])
                    yield

            def back(ui):
                b, s0, L, kind = units[ui]
                zcs = zcsU[ui % 2]
                cl, sl = (clc, slc) if kind == 'c' else (clx, slx)
                clB = [clc.b] * 2 if kind == 'c' else clxB
                slB = [slc.b] * 2 if kind == 'c' else slxB
                ntc = L // 128
                for m in range(2):
                    for o in range(0, L, 512):
                        n = min(512, L - o)
                        P = banks[6 + cnt[2] % 2]
                        cnt[2] += 1
                        for tc in range(ntc):
                            PE(P.t[:, :n], zcs.t[:, tc, 0, m * 128:(m + 1) * 128], cl.t[:, tc, o:o + n], tc == 0, False,
                               [zcs.b, clB[tc]], [P.b])
                            PE(P.t[:, :n], zcs.t[:, tc, 1, m * 128:(m + 1) * 128], sl.t[:, tc, o:o + n], False, tc == ntc - 1,
                               [zcs.b, slB[tc]], [P.b])
                            if tc % 4 == 3:
                                yield
                        Fb = fb[cnt[0] % 2]
                        O = ob[cnt[0] % 2]
                        cnt[0] += 1
                        CP(Fb.t[:, :n], P.t[:, :n], [P.b], [Fb.b])
                        P2 = nbf()
                        PE(P2.t[:, :n], fwb.t[:, m, :], Fb.t[:, :n], True, True, [fwb.b, Fb.b], [P2.b])
                        TS(O.t[:, :n], P2.t[:, :n], pvc("fnb", m), None, ALU.add, None, [P2.b, pv.b], [O.b])
                        tk.dma("pool", mix_d[b][:, 6 + m, s0 + o:s0 + o + n], O.t[:, :n], reads=[O.b])
                        yield

            run_tasks([front(0)])
            for ui in range(len(units)):
                run_tasks([back(ui), front(ui + 1) if ui + 1 < len(units) else None])
        tk.barrier()

    def ln_gen(y, n, gname, bname, sc, eps_t=None):
        eps_t = eps_t or eps_ln
        ybf, ysq, mu, rstd, var = sc
        CP(ybf.t[:, :, :n], y.t[:, :, :n], [y.b], [ybf.b], eng="dve")
        ACT(ysq.t[:, :, :n], y.t[:, :, :n], AF.Square, [y.b], [ysq.b])
        yield
        P1 = nb_()
        P2 = nb_()
        for ot in range(8):
            PE(P1.t[:, :n], ones_bf.t[:], ybf.t[:, ot, :n], ot == 0, ot == 7, [ones_bf.b, ybf.b], [P1.b])
        for ot in range(8):
            PE(P2.t[:, :n], ones_bf.t[:], ysq.t[:, ot, :n], ot == 0, ot == 7, [ones_bf.b, ysq.b], [P2.b])
        TS(mu.t[:, :n], P1.t[:, :n], 1.0 / D, None, ALU.mult, None, [P1.b], [mu.b])
        TT(var.t[:, :n], mu.t[:, :n], mu.t[:, :n], ALU.mult, [mu.b], [var.b])
        STT(var.t[:, :n], P2.t[:, :n], 1.0 / D, var.t[:, :n], ALU.mult, ALU.subtract, [P2.b, var.b], [var.b])
        yield
        ACT(rstd.t[:, :n], var.t[:, :n], AF.Ln, [var.b], [rstd.b], bias=eps_t.t[:, 0:1])
        ACT(rstd.t[:, :n], rstd.t[:, :n], AF.Exp, [rstd.b], [rstd.b], scale=-0.5)
        yield
        mub = mu.t[:, :n].unsqueeze(1).to_broadcast([128, 8, n])
        rsb = rstd.t[:, :n].unsqueeze(1).to_broadcast([128, 8, n])
        TT(y.t[:, :, :n], y.t[:, :, :n], mub, ALU.subtract, [y.b, mu.b], [y.b])
        yield
        TT(y.t[:, :, :n], y.t[:, :, :n], rsb, ALU.mult, [y.b, rstd.b], [y.b])
        yield
        for ot in range(8):
            MOD(y.t[:, ot, :n], y.t[:, ot, :n], pvc(gname, ot), pvc(bname, ot), [y.b, pv.b], [y.b], eng="act")
        yield

    def ln_gen2(y, yb, n, gname, bname, sc, eps_t, sub_eng="pool"):
        ybf, ysq, mu, rstd, var = sc
        for ot in range(8):
            CP(ybf.t[:, ot, :n], y.t[:, ot, :n], [yb[ot]], [ybf.b], eng="act")
            ACT(ysq.t[:, ot, :n], y.t[:, ot, :n], AF.Square, [yb[ot]], [ysq.b])
            if ot % 2:
                yield
        P1 = nb_()
        P2 = nb_()
        for ot in range(8):
            PE(P1.t[:, :n], ones_bf.t[:], ybf.t[:, ot, :n], ot == 0, ot == 7, [ones_bf.b, ybf.b], [P1.b])
        for ot in range(8):
            PE(P2.t[:, :n], ones_bf.t[:], ysq.t[:, ot, :n], ot == 0, ot == 7, [ones_bf.b, ysq.b], [P2.b])
        TS(mu.t[:, :n], P1.t[:, :n], 1.0 / D, None, ALU.mult, None, [P1.b], [mu.b])
        TT(var.t[:, :n], mu.t[:, :n], mu.t[:, :n], ALU.mult, [mu.b], [var.b])
        STT(var.t[:, :n], P2.t[:, :n], 1.0 / D, var.t[:, :n], ALU.mult, ALU.subtract, [P2.b, var.b], [var.b])
        yield
        ACT(rstd.t[:, :n], var.t[:, :n], AF.Ln, [var.b], [rstd.b], bias=eps_t.t[:, 0:1])
        ACT(rstd.t[:, :n], rstd.t[:, :n], AF.Exp, [rstd.b], [rstd.b], scale=-0.5)
        yield
        for ot in range(8):
            TT(y.t[:, ot, :n], y.t[:, ot, :n], mu.t[:, :n], ALU.subtract, [yb[ot], mu.b], [yb[ot]], eng=sub_eng)
            TT(y.t[:, ot, :n], y.t[:, ot, :n], rstd.t[:, :n], ALU.mult, [yb[ot], rstd.b], [yb[ot]])
            MOD(y.t[:, ot, :n], y.t[:, ot, :n], pvc(gname, ot), pvc(bname, ot), [yb[ot], pv.b], [yb[ot]], eng="act")
            if ot % 2:
                yield

    def ln_feat(y, n, gname, bname, sc, eps_t=None):
        for _ in ln_gen(y, n, gname, bname, sc, eps_t):
            pass

    def ln_scratch(s, nmax, pref):
        return (s.sb(pref + "ybf", [128, 8, nmax], BF16), s.sb(pref + "ysq", [128, 8, nmax], BF16),
                s.sb(pref + "mu", [128, nmax]), s.sb(pref + "rstd", [128, nmax]), s.sb(pref + "var", [128, nmax]))

    def stage_O(l, last, pre=None):
        OB = 256
        NW = 2
        with Scope() as s:
            wo = s.sb("wo", [128, 8, D], BF16)
            woB = BIGDMAC(wo, wout_d[l], 8, 2)
            NBUF = 2 * NW
            mx = [s.sb("omx%d" % i, [128, 8, OB], BF16) for i in range(NBUF)]
            xs = [s.sb("oxs%d" % i, [128, 8, OB]) for i in range(NBUF)]
            xsb = [[Buf() for _ in range(8)] for i in range(NBUF)]
            scs = [ln_scratch(s, OB, "o%d" % i) for i in range(NW)]
            work = []
            for b in range(NB):
                for (t0b, nb, kind) in blocks:
                    if last and kind == 'c':
                        continue
                    for t0 in range(t0b, t0b + nb, OB):
                        work.append((b, t0, OB, 2 if kind == 'c' else b))

            def load(k):
                if k >= len(work):
                    return
                b, t0, n, ni = work[k]
                M, X, XB = mx[k % NBUF], xs[k % NBUF], xsb[k % NBUF]
                tk.dma("sp", M.t[:, :, :n], mix_d[b][:, :, t0:t0 + n], writes=[M.b])
                tk.dma("sp", X.t[:, :, :n], xs_d[b][:, :, t0:t0 + n], writes=XB)

            def blk(k):
                b, t0, n, ni = work[k]
                M, X, XB, sc = mx[k % NBUF], xs[k % NBUF], xsb[k % NBUF], scs[k % NW]
                load(k + NW)
                if pre is not None:
                    for _ in range(2):
                        try:
                            next(pre)
                        except StopIteration:
                            break
                yield
                for ot in range(8):
                    P = nb_()
                    for kc in range(8):
                        PE(P.t[:, :n], wo.t[:, kc, ot * 128:(ot + 1) * 128], M.t[:, kc, :n], kc == 0, kc == 7, [woB[kc], M.b], [P.b])
                    STT(X.t[:, ot, :n], P.t[:, :n], mcol(2, ot, ni), X.t[:, ot, :n], ALU.mult, ALU.add, [P.b, mod.b, XB[ot]], [XB[ot]])
                    if ot % 2:
                        yield
                for _ in ln_gen2(X, XB, n, "l1g", "l1b", sc, eps_lna):
                    yield
                tk.dma("pool", xs_d[b][:, :, t0:t0 + n], X.t[:, :, :n], reads=XB)
                yield

            for k in range(NW):
                load(k)
            run_window([blk(k) for k in range(len(work))], NW)
            if pre is not None:
                for _ in pre:
                    pass
        tk.barrier()

    def stage_FF(l, last, w1a, w1b, w1B):
        NBK = 256
        with Scope() as s:
            w2 = s.sb("w2", [128, 22, D], BF16)
            w2B = BIGDMAC(w2, w2_d[l], 22, 2)

            def w1s(kc, c0, c1):
                t_ = w1a if kc < 4 else w1b
                return t_.t[:, kc % 4, c0:c1], w1B[kc][1 if c0 >= DFF else 0]
            side = None
            if l + 1 < nlayers:
                side = None
            xs = [s.sb("fxs%d" % i, [128, 8, NBK]) for i in range(2)]
            xsb = [[Buf() for _ in range(8)] for i in range(2)]
            h2 = [s.sb("fh2%d" % i, [128, 8, NBK], BF16) for i in range(2)]
            act = s.sb("fact", [128, 22, NBK], BF16)
            sg = [s.sb("fsg%d" % i, [128, NBK]) for i in range(2)]
            sc = ln_scratch(s, NBK, "f")
            work = []
            for b in range(NB):
                for (t0b, nb, kind) in blocks:
                    if last and kind == 'c':
                        continue
                    for t0 in range(t0b, t0b + nb, NBK):
                        work.append((b, t0, 2 if kind == 'c' else b))
            n = NBK
            nw = len(work)

            def load_mod(k):
                b, t0, ni = work[k]
                X = xs[k % 2]
                H = h2[k % 2]
                XB = xsb[k % 2]
                tk.dma("sp", X.t[:, :, :], xs_d[b][:, :, t0:t0 + n], writes=XB)
                for kc in range(8):
                    MOD(H.t[:, kc, :], X.t[:, kc, :], mcol(4, kc, ni), mcol(3, kc, ni), [XB[kc], mod.b], [H.b], eng="pool")

            def fin(k):
                b, t0, ni = work[k]
                X = xs[k % 2]
                XB = xsb[k % 2]
                for _ in ln_gen2(X, XB, n, "l2g", "l2b", sc, eps_lna, sub_eng="pool"):
                    yield
                if last:
                    tk.dma("act", out_d[b][:, :, t0 - CTX:t0 - CTX + n], X.t[:, :, :], reads=XB)
                else:
                    tk.dma("act", xs_d[b][:, :, t0:t0 + n], X.t[:, :, :], reads=XB)
                yield

            load_mod(0)
            for k in range(nw):
                b, t0, ni = work[k]
                X = xs[k % 2]
                H = h2[k % 2]
                fq = fin(k - 1) if k >= 1 else None
                loaded = False
                for ft in range(22):
                    if fq is not None:
                        try:
                            next(fq)
                        except StopIteration:
                            fq = None
                    elif not loaded and k + 1 < nw:
                        load_mod(k + 1)
                        loaded = True
                    Pg = nb_()
                    Pu = nb_()
                    for kc in range(8):
                        wa, wb_ = w1s(kc, ft * 128, (ft + 1) * 128)
                        PE(Pg.t[:, :n], wa, H.t[:, kc, :], kc == 0, kc == 7, [wb_, H.b], [Pg.b])
                    for kc in range(8):
                        wa, wb_ = w1s(kc, DFF + ft * 128, DFF + (ft + 1) * 128)
                        PE(Pu.t[:, :n], wa, H.t[:, kc, :], kc == 0, kc == 7, [wb_, H.b], [Pu.b])
                    S = sg[ft % 2]
                    ACT(S.t[:, :], Pg.t[:, :n], AF.Silu, [Pg.b], [S.b])
                    TT(act.t[:, ft, :], S.t[:, :], Pu.t[:, :n], ALU.mult, [S.b, Pu.b], [act.b])
                if fq is not None:
                    for _ in fq:
                        pass
                if not loaded and k + 1 < nw:
                    load_mod(k + 1)
                if side is not None:
                    nside = 12 if k < nw - 1 else 100000
                    for _ in range(nside):
                        try:
                            next(side)
                        except StopIteration:
                            side = None
                            break
                for ot in range(8):
                    P = nb_()
                    for ft in range(22):
                        PE(P.t[:, :n], w2.t[:, ft, ot * 128:(ot + 1) * 128], act.t[:, ft, :], ft == 0, ft == 21, [w2B[ft], act.b], [P.b])
                    STT(X.t[:, ot, :], P.t[:, :n], mcol(5, ot, ni), X.t[:, ot, :], ALU.mult, ALU.add, [P.b, mod.b, xsb[k % 2][ot]], [xsb[k % 2][ot]])
            for _ in fin(nw - 1):
                pass
        tk.barrier()

    def run(name, fn, *a):
        with nc.named_scope(name):
            fn(*a)

    tk.barrier()
    for l in range(nlayers):
        last = (l == DEPTH - 1)
        with Scope() as so:
            set_layer(l)
            win = so.sb("win", [128, 8, INC], BF16)
            winB = BIGDMAC(win, win_d[l], 8, 1)
            if l == 0:
                run("cast", stage_cast_init)
            run("A%d" % l, stage_A, l, last, win, winB)
        if "stopA" in dbg:
            break
        if "skipR" not in dbg:
            run("R%d" % l, stage_R, l, last)
        if "stopR" in dbg:
            break
        if "skipP" not in dbg:
            run("P%d" % l, stage_P, l, last)
        with Scope() as so:
            clx = so.sb("clx", [128, 16, SEQ], BF16)
            slx = so.sb("slx", [128, 16, SEQ], BF16)
            clxB = BIGDMA(clx, cl_x_d, 16, 2, q="pool")
            slxB = BIGDMA(slx, sl_x_d, 16, 2, q="pool")
            if "skipG" not in dbg:
                run("G%d" % l, stage_G, l, last)
            if "skipF" not in dbg:
                run("F%d" % l, stage_F, l, last, clx, slx, clxB, slxB)
        if "stopM" in dbg:
            break
        with Scope() as so:
            w1a = so.sb("w1a", [128, 4, 2 * DFF], BF16)
            w1b = so.sb("w1b", [128, 4, 2 * DFF], BF16)
            w1B = [[Buf() for _ in range(2)] for _ in range(8)]

            def w1pre(l=l, w1a=w1a, w1b=w1b, w1B=w1B):
                for r in range(8):
                    dst = w1a if r < 4 else w1b
                    for h in range(2):
                        tk.dma("pool", dst.t[:, r % 4, h * DFF:(h + 1) * DFF], w1_d[l][:, r, h * DFF:(h + 1) * DFF], writes=[w1B[r][h]])
                        yield
            run("O%d" % l, stage_O, l, last, w1pre())
            run("FF%d" % l, stage_FF, l, last, w1a, w1b, w1B)
    tk.barrier()
    es_glob.close()
    return nc


def _pos_embed():
    rows = SEQ // GRID_W
    row, col = np.meshgrid(np.arange(rows, dtype=np.float32), np.arange(GRID_W, dtype=np.float32), indexing='ij')
    quarter = D // 4
    freqs = np.exp(np.float32(-math.log(10000.0)) * np.arange(quarter, dtype=np.float32) / np.float32(quarter)).astype(np.float32)

    def enc(p):
        ang = p.reshape(-1, 1).astype(np.float32) * freqs[None, :]
        return np.concatenate([np.sin(ang), np.cos(ang)], -1)
    return np.concatenate([enc(row), enc(col)], -1).astype(np.float32)


def _fm(a):
    sh = a.shape
    kc = sh[-2] // 128
    a = a.reshape(sh[:-2] + (kc, 128, sh[-1]))
    return np.ascontiguousarray(np.swapaxes(a, -3, -2))


def _cols(v):
    return np.ascontiguousarray(v.reshape(-1, 128).T)


def _consts():
    bf = ml_dtypes.bfloat16
    c = {}
    c["posT"] = _fm(np.ascontiguousarray(_pos_embed().T))
    for L, nm in ((SEQ, "x"), (CTX, "c")):
        idx = np.arange(L, dtype=np.int64)
        ang = 2.0 * np.pi * ((idx[:, None] * idx[None, :]) % L).astype(np.float64) / L
        cl = np.cos(ang) / math.sqrt(L)
        sl = -np.sin(ang) / math.sqrt(L)
        c["cl_" + nm] = _fm(cl.astype(np.float32)).astype(bf)
        c["sl_" + nm] = _fm(sl.astype(np.float32)).astype(bf)
        rc = np.zeros((2, 128, L), np.float32)
        for m in range(2):
            for hh in range(2):
                w = POOLW[2 * m + hh]
                lo = np.clip(idx - w // 2, 0, L)
                hi = np.clip(idx + w - w // 2, 0, L)
                rc[m, hh * 64:(hh + 1) * 64, :] = (1.0 / (hi - lo).astype(np.float32))[None, :]
        c["rc_" + nm] = np.ascontiguousarray(rc.transpose(1, 0, 2))
    i64 = np.arange(64)
    a64 = 2.0 * np.pi * ((i64[:, None] * i64[None, :]) % 64) / 64.0
    cs = np.concatenate([np.cos(a64), np.sin(a64)], 1) / 8.0
    c["cs64"] = np.concatenate([cs, cs], 0).astype(np.float32).astype(bf)
    ident = np.eye(128, dtype=np.float32)
    ho = np.zeros((128, 128), np.float32)
    ho[:64, :64] = 1
    ho[64:, 64:] = 1
    su = np.triu(np.ones((128, 128), np.float32), 1)
    sl_ = np.tril(np.ones((128, 128), np.float32), -1)
    ui = np.triu(np.ones((64, 64), np.float32), 0)
    li = np.tril(np.ones((64, 64), np.float32), 0)
    ui_st = np.concatenate([ui, ui], 0)
    li_st = np.concatenate([li, li], 0)
    cst = np.concatenate([ident, ho, np.tile(su, (1, 4)), np.tile(sl_, (1, 4)), np.tile(ui_st, (1, 4)), np.tile(li_st, (1, 4)),
                          np.zeros((128, 128), np.float32)], 1)
    assert cst.shape == (128, 1920)
    c["cst"] = cst
    return c


_CONSTS = None


def _prep_shared(inp):
    global _CONSTS
    if _CONSTS is None:
        _CONSTS = _consts()
    f = np.float32
    sh = dict(_CONSTS)
    sh["w_mod"] = _fm(np.asarray(inp["w_mod"], f))
    sh["b_mod"] = np.stack([_cols(np.asarray(inp["b_mod"], f)[l]) for l in range(DEPTH)])
    sh["w_in"] = _fm(np.asarray(inp["w_in"], f))
    sh["w_out"] = _fm(np.asarray(inp["w_out"], f))
    sh["ffn_w1"] = _fm(np.asarray(inp["ffn_w1"], f))
    sh["ffn_w2"] = _fm(np.asarray(inp["ffn_w2"], f))
    pv = np.zeros((DEPTH, 128, NPV), f)
    for l in range(DEPTH):
        def put(name, arr):
            arr = np.asarray(arr, f)
            pv[l, :, PV[name]:PV[name] + arr.shape[1]] = arr
        conv = np.asarray(inp["rkv_conv"], f)[l]
        put("conv", np.concatenate([_cols(conv[j]) for j in range(3)], 1))
        put("w0", np.concatenate([_cols(np.asarray(inp["decay_w0"], f)[l, d]) for d in range(2)], 1))
        put("a0", np.concatenate([_cols(np.asarray(inp["iclr_a0"], f)[l, d]) for d in range(2)], 1))
        put("kk", _cols(np.asarray(inp["k_k"], f)[l]))
        put("ka", _cols(np.asarray(inp["k_a"], f)[l]))
        put("rk", _cols(np.asarray(inp["r_k"], f)[l].reshape(-1)))
        put("gng", _cols(np.asarray(inp["gn_g"], f)[l]))
        put("gnb", _cols(np.asarray(inp["gn_b"], f)[l]))
        put("psc", _cols(np.asarray(inp["pool_scale"], f)[l]))
        put("fnb", _cols(np.asarray(inp["fnet_b"], f)[l]))
        put("l1g", _cols(np.asarray(inp["ln1_g"], f)[l]))
        put("l1b", _cols(np.asarray(inp["ln1_b"], f)[l]))
        put("l2g", _cols(np.asarray(inp["ln2_g"], f)[l]))
        put("l2b", _cols(np.asarray(inp["ln2_b"], f)[l]))
    sh["pv"] = pv
    sh["w2s"] = np.ascontiguousarray(np.asarray(inp["decay_w2"], f).reshape(DEPTH, 128, 256))
    sh["a2s"] = np.ascontiguousarray(np.asarray(inp["iclr_a2"], f).reshape(DEPTH, 128, 256))
    sh["g2"] = np.ascontiguousarray(np.asarray(inp["gate_g2"], f))

    def bd(w):
        o = np.zeros((DEPTH, 128, 2, 128), f)
        for m in range(2):
            o[:, 0:64, m, 0:64] = w[:, 2 * m]
            o[:, 64:128, m, 64:128] = w[:, 2 * m + 1]
        return o
    sh["poolbd"] = bd(np.asarray(inp["pool_w"], f))
    sh["fnetbd"] = bd(np.asarray(inp["fnet_w"], f))
    ws = np.asarray(inp["gmlp_ws"], f)
    sh["wsT"] = np.ascontiguousarray(ws.transpose(0, 3, 1, 2))
    bs = np.asarray(inp["gmlp_bs"], f)
    bsb = np.zeros((DEPTH, 128, 2, 128), f)
    for m in range(2):
        bsb[:, 0:64, m, :] = bs[:, 2 * m][:, None, :]
        bsb[:, 64:128, m, :] = bs[:, 2 * m + 1][:, None, :]
    sh["bsb"] = bsb
    sh["glg"] = np.ascontiguousarray(np.broadcast_to(np.asarray(inp["gmlp_ln_g"], f)[:, None, :], (DEPTH, 128, 256)))
    sh["glb"] = np.ascontiguousarray(np.broadcast_to(np.asarray(inp["gmlp_ln_b"], f)[:, None, :], (DEPTH, 128, 256)))
    return sh


def _prep_core(inp, sh, i):
    f = np.float32
    m = dict(sh)
    xb = np.asarray(inp["x"], f)[i * NB:(i + 1) * NB]
    m["xT"] = _fm(np.ascontiguousarray(np.swapaxes(xb, 1, 2)))
    cb = np.asarray(inp["ctx"], f)[i * NB:(i + 1) * NB]
    m["ctxT"] = _fm(np.ascontiguousarray(np.swapaxes(cb, 1, 2)))
    cc = np.zeros((4, D), f)
    cc[0:NB] = np.asarray(inp["c"], f)[i * NB:(i + 1) * NB]
    cc[2] = np.asarray(inp["c_ctx"], f)
    m["cT"] = _fm(np.ascontiguousarray(cc.T))
    return m


_NC = None


def kernel(**inputs):
    global _NC
    if _NC is None:
        _NC = build()
    sh = _prep_shared(inputs)
    in_maps = [_prep_core(inputs, sh, i) for i in range(NCORES)]
    res = run_bass_kernel_spmd(_NC, in_maps, core_ids=list(range(NCORES)))
    outs = []
    for i in range(NCORES):
        o = np.asarray(res.results[i]["out"])
        o = o.transpose(0, 2, 1, 3).reshape(NB, D, SEQ)
        outs.append(np.swapaxes(o, 1, 2))
    return np.ascontiguousarray(np.concatenate(outs, 0)).astype(np.float32)
```

```python
import math
from contextlib import ExitStack
import numpy as np
import ml_dtypes
import concourse.bass as bass
import concourse.mybir as mybir
from concourse.bass_utils import run_bass_kernel_spmd

F32 = mybir.dt.float32
BF16 = mybir.dt.bfloat16
AF = mybir.ActivationFunctionType
ALU = mybir.AluOpType
AX = mybir.AxisListType

D = 1024
NBATCH = 16
SEQ = 2048
CTX = 256
DEPTH = 2
NCORES = 8
NB = NBATCH // NCORES
T = CTX + SEQ
GRID_W = 64
GW = 256
HEAD = 64
INC = 2176
RWC = 1152
DFF = 2816
ALPHA = (2 * DEPTH) ** 0.25
LN_EPS = 1e-5
GN_EPS = 64e-5
POOLW = (2, 4, 8, 16)
CH = 64
SCAN_DT = F32
USE_F32R = True
SES = True

PV = {}
_c = 0
for _n, _k in (("conv", 18), ("w0", 4), ("a0", 4), ("kk", 2), ("ka", 2), ("rk", 2), ("gng", 2), ("gnb", 2),
               ("psc", 2), ("fnb", 2), ("l1g", 8), ("l1b", 8), ("l2g", 8), ("l2b", 8)):
    PV[_n] = _c
    _c += _k
NPV = _c


class Buf:
    __slots__ = ("name", "lw", "rd", "r32")

    def __init__(self, name=""):
        self.name = name
        self.lw = None
        self.rd = {}
        self.r32 = False


class TB:
    __slots__ = ("t", "b")

    def __init__(self, t, name=""):
        self.t = t
        self.b = Buf(name)


class Trk:
    NDMA = 8

    def __init__(self, nc, ses=True):
        self.nc = nc
        self.ses = ses
        self.eng = {"pe": nc.tensor, "act": nc.scalar, "dve": nc.vector, "pool": nc.gpsimd, "sp": nc.sync}
        self.sem = {k: nc.alloc_semaphore("s_" + k) for k in self.eng}
        self.cnt = {k: 0 for k in self.eng}
        self.dq = {}
        for q in ("sp", "pool", "act"):
            self.dq[q] = dict(sems=[nc.alloc_semaphore("d_%s%d" % (q, i)) for i in range(self.NDMA)], k=0)
        self.waited = {k: {} for k in self.eng}

    def _wait(self, e, key, semh, val):
        if self.waited[e].get(key, 0) >= val:
            return
        self.eng[e].wait_ge(semh, val)
        self.waited[e][key] = val

    def _need(self, e, ev):
        key, val = ev
        if key[0] == "e":
            if key[1] == e and (e == "pe" or e == "sp" or not self.ses):
                return
            self._wait(e, key, self.sem[key[1]], val)
        else:
            self._wait(e, key, self.dq[key[1]]["sems"][key[2]], val)

    def deps(self, e, reads, writes):
        for b in reads:
            if b.lw is not None:
                self._need(e, b.lw)
        for b in writes:
            if b.lw is not None:
                self._need(e, b.lw)
            for k, v in b.rd.items():
                self._need(e, (k, v))

    def _mark(self, ev, reads, writes):
        for b in reads:
            b.rd[ev[0]] = ev[1]
        for b in writes:
            b.lw = ev
            b.rd = {}

    def op(self, e, fn, reads=(), writes=()):
        self.deps(e, reads, writes)
        inst = fn(self.eng[e])
        self.cnt[e] += 1
        inst.then_inc(self.sem[e], 1)
        self._mark((("e", e), self.cnt[e]), reads, writes)
        return inst

    def dma(self, q, out, in_, reads=(), writes=()):
        d = self.dq[q]
        k = d["k"]
        slot = k % self.NDMA
        rnd = k // self.NDMA
        key = ("d", q, slot)
        if rnd > 0:
            self._wait(q, key, d["sems"][slot], 16 * rnd)
        self.deps(q, reads, writes)
        inst = self.eng[q].dma_start(out=out, in_=in_)
        inst.then_inc(d["sems"][slot], 16)
        d["k"] = k + 1
        self._mark((key, 16 * (rnd + 1)), reads, writes)
        return inst

    def barrier(self):
        for e in self.eng:
            for o in self.eng:
                if o != e and self.cnt[o] > 0:
                    self._wait(e, ("e", o), self.sem[o], self.cnt[o])
            for q, d in self.dq.items():
                k = d["k"]
                for slot in range(self.NDMA):
                    n = (k - slot + self.NDMA - 1) // self.NDMA
                    if n > 0:
                        self._wait(e, ("d", q, slot), d["sems"][slot], 16 * n)


def build(nlayers=DEPTH, dbg=()):
    nc = bass.Bass("TRN2", target_bir_lowering=False)
    tk = Trk(nc, ses=SES)

    def din(name, shape, dt=F32):
        return nc.dram_tensor(name, list(shape), dt, kind="ExternalInput").ap()

    def dscr(name, shape, dt=F32):
        kind = "ExternalOutput" if name in dbg else "Internal"
        return nc.dram_tensor(name, list(shape), dt, kind=kind).ap()

    xT_d = din("xT", [NB, 128, 8, SEQ])
    ctxT_d = din("ctxT", [NB, 128, 8, CTX])
    posT_d = din("posT", [128, 8, SEQ])
    cT_d = din("cT", [128, 8, 4])
    wmod_d = din("w_mod", [DEPTH, 128, 8, 6 * D])
    bmod_d = din("b_mod", [DEPTH, 128, 48])
    win_d = din("w_in", [DEPTH, 128, 8, INC])
    wout_d = din("w_out", [DEPTH, 128, 8, D])
    w1_d = din("ffn_w1", [DEPTH, 128, 8, 2 * DFF])
    w2_d = din("ffn_w2", [DEPTH, 128, 22, D])
    pv_d = din("pv", [DEPTH, 128, NPV])
    w2s_d = din("w2s", [DEPTH, 128, 256])
    a2s_d = din("a2s", [DEPTH, 128, 256])
    g2_d = din("g2", [DEPTH, 128, 256])
    poolbd_d = din("poolbd", [DEPTH, 128, 2, 128])
    fnetbd_d = din("fnetbd", [DEPTH, 128, 2, 128])
    wsT_d = din("wsT", [DEPTH, 128, 4, 128])
    bsb_d = din("bsb", [DEPTH, 128, 2, 128])
    glg_d = din("glg", [DEPTH, 128, 256])
    glb_d = din("glb", [DEPTH, 128, 256])
    cl_x_d = din("cl_x", [128, 16, SEQ], BF16)
    sl_x_d = din("sl_x", [128, 16, SEQ], BF16)
    cl_c_d = din("cl_c", [128, 2, CTX], BF16)
    sl_c_d = din("sl_c", [128, 2, CTX], BF16)
    cs64_d = din("cs64", [128, 128], BF16)
    rc_x_d = din("rc_x", [128, 2, SEQ])
    rc_c_d = din("rc_c", [128, 2, CTX])
    cst_d = din("cst", [128, 1920])
    out_d = nc.dram_tensor("out", [NB, 128, 8, SEQ], F32, kind="ExternalOutput").ap()

    xs_d = dscr("xs_s", [NB, 128, 8, T])
    px_d = dscr("px_s", [NB, 17, 128, T])
    yf_d = dscr("yf_s", [NB, 2, 128, T])
    mix_d = dscr("mix_s", [NB, 128, 8, T], BF16)
    winb_d = woutb_d = w1b_d = w2b_d = None

    es_glob = ExitStack()
    uid = [0]

    class Scope:
        def __init__(self):
            self.es = ExitStack()

        def __enter__(self):
            self.es.__enter__()
            return self

        def __exit__(self, *a):
            return self.es.__exit__(*a)

        def sb(self, name, shape, dt=F32):
            uid[0] += 1
            t = self.es.enter_context(nc.sbuf_tensor("%s_%d" % (name, uid[0]), list(shape), dt))
            return TB(t, name)

    def gsb(name, shape, dt=F32):
        t = es_glob.enter_context(nc.sbuf_tensor("g_" + name, list(shape), dt))
        return TB(t, name)

    banks = [TB(es_glob.enter_context(nc.psum_tensor("bank%d" % i, [128, 512], F32)), "bank%d" % i) for i in range(8)]
    bank_i = [0]

    def nb_():
        b = banks[bank_i[0] % 8]
        bank_i[0] += 1
        return b

    rr = [0]

    def ee():
        rr[0] += 1
        return "dve" if rr[0] % 2 else "act"

    F32R = mybir.dt.float32r

    def PE(out, lhsT, rhs, start, stop, reads, writes):
        if USE_F32R and lhsT.dtype == F32 and all(b.r32 for b in reads):
            lhsT = lhsT.bitcast(F32R)
            rhs = rhs.bitcast(F32R)
        tk.op("pe", lambda e: e.matmul(out, lhsT=lhsT, rhs=rhs, start=start, stop=stop), reads, writes)

    def PET(out, in_, reads, writes):
        kp = in_.shape[0]
        if in_.dtype == BF16:
            tk.op("pe", lambda e: e.transpose(out, in_, ident_bf.t[0:kp, 0:kp]), list(reads) + [ident_bf.b], writes)
        else:
            tk.op("pe", lambda e: e.transpose(out, in_, ident.t[0:kp, 0:kp]), list(reads) + [cstb], writes)

    def RO(out, writes):
        if USE_F32R and out.dtype == F32 and any(b.r32 for b in writes):
            return out.bitcast(F32R)
        return out

    def CP(out, in_, reads, writes, eng=None):
        out = RO(out, writes)
        eng = eng or ee()
        if eng == "act":
            tk.op("act", lambda e: e.copy(out=out, in_=in_), reads, writes)
        else:
            tk.op(eng, lambda e: e.tensor_copy(out=out, in_=in_), reads, writes)

    def TT(out, in0, in1, op, reads, writes, eng="dve"):
        out = RO(out, writes)
        tk.op(eng, lambda e: e.tensor_tensor(out=out, in0=in0, in1=in1, op=op), reads, writes)

    def TS(out, in0, s1, s2, op0, op1, reads, writes, eng="dve"):
        out = RO(out, writes)
        if op1 is None:
            tk.op(eng, lambda e: e.tensor_scalar(out=out, in0=in0, scalar1=s1, scalar2=None, op0=op0), reads, writes)
        else:
            tk.op(eng, lambda e: e.tensor_scalar(out=out, in0=in0, scalar1=s1, scalar2=s2, op0=op0, op1=op1), reads, writes)

    def STT(out, in0, scalar, in1, op0, op1, reads, writes):
        out = RO(out, writes)
        tk.op("dve", lambda e: e.scalar_tensor_tensor(out=out, in0=in0, scalar=scalar, in1=in1, op0=op0, op1=op1), reads, writes)

    def ACT(out, in_, func, reads, writes, bias=0.0, scale=1.0):
        out = RO(out, writes)
        tk.op("act", lambda e: e.activation(out=out, in_=in_, func=func, bias=bias, scale=scale), reads, writes)

    def MOD(out, in_, s1, s2, reads, writes, eng=None):
        eng = eng or ee()
        if eng == "act":
            ACT(out, in_, AF.Identity, reads, writes, bias=s2, scale=s1)
        else:
            TS(out, in_, s1, s2, ALU.mult, ALU.add, reads, writes, eng=eng)

    def BIGDMA(dst_tb, src_ap, nrow, step, q="sp"):
        rowB = [None] * nrow
        for r in range(0, nrow, step):
            r2 = min(nrow, r + step)
            pb_ = Buf()
            for rr_ in range(r, r2):
                rowB[rr_] = pb_
            tk.dma(q, dst_tb.t[:, r:r2, :], src_ap[:, r:r2, :], writes=[pb_])
        return rowB

    def BIGDMAC(dst_tb, src_ap, nrow, step):
        rowB = [None] * nrow
        for r in range(0, nrow, step):
            r2 = min(nrow, r + step)
            pb_ = Buf()
            for rr_ in range(r, r2):
                rowB[rr_] = pb_
            tk.dma("pool", dst_tb.t[:, r:r2, :], src_ap[:, r:r2, :], writes=[pb_])
        return rowB

    def MEMSET(ap, val, writes, eng="pool"):
        ap = RO(ap, writes)
        tk.op(eng, lambda e: e.memset(ap, val), (), writes)

    cst = gsb("cst", [128, 1920])
    cstb = cst.b
    tk.dma("sp", cst.t[:], cst_d[:, :], writes=[cstb])
    ident = TB(cst.t[:, 0:128])
    ident.b = cstb
    headones = cst.t[:, 128:256]
    SU4 = cst.t[:, 256:768].rearrange("p (a b) -> p a b", a=4)
    SL4 = cst.t[:, 768:1280].rearrange("p (a b) -> p a b", a=4)
    UI4 = cst.t[:, 1280:1536].rearrange("p (a b) -> p a b", a=4)
    LI4 = cst.t[:, 1536:1792].rearrange("p (a b) -> p a b", a=4)
    ident4 = None
    ident_bf = gsb("ident_bf", [128, 128], BF16)
    CP(ident_bf.t[:], cst.t[:, 0:128], [cstb], [ident_bf.b], eng="dve")
    ones_bf = gsb("ones_bf", [128, 128], BF16)
    MEMSET(ones_bf.t[:], 1.0, [ones_bf.b])
    id4 = gsb("id4", [128, 4, 128])
    for a in range(4):
        CP(id4.t[:, a, :], cst.t[:, 0:128], [cstb], [id4.b], eng="dve")
    scanmask = gsb("scanmask", [128, 256])
    MEMSET(scanmask.t[:], 1.0, [scanmask.b])
    for c in range(4):
        MEMSET(scanmask.t[:, c * 64:c * 64 + 1], 0.0, [scanmask.b])
    eps_kk = gsb("eps_kk", [128, 1])
    MEMSET(eps_kk.t[:], 1e-20, [eps_kk.b])
    eps_gn = gsb("eps_gn", [128, 1])
    MEMSET(eps_gn.t[:], GN_EPS, [eps_gn.b])
    eps_ln = gsb("eps_ln", [128, 1])
    MEMSET(eps_ln.t[:], LN_EPS, [eps_ln.b])
    eps_lna = gsb("eps_lna", [128, 1])
    MEMSET(eps_lna.t[:], LN_EPS / (ALPHA * ALPHA), [eps_lna.b])
    modL = [gsb("mod%d" % i, [128, 48, 3]) for i in range(DEPTH)]
    pvL = [gsb("pv%d" % i, [128, NPV]) for i in range(DEPTH)]
    omkaL = [gsb("omka%d" % i, [128, 2]) for i in range(DEPTH)]
    mod = TB(modL[0].t)
    pv = TB(pvL[0].t)
    omka = TB(omkaL[0].t)

    def set_layer(l):
        mod.t, mod.b = modL[l].t, modL[l].b
        pv.t, pv.b = pvL[l].t, pvL[l].b
        omka.t, omka.b = omkaL[l].t, omkaL[l].b
    set_layer(0)
    sil = gsb("sil", [128, 8, 4])
    cTt = gsb("cTt", [128, 8, 4])
    tk.dma("sp", cTt.t[:], cT_d[:, :, :], writes=[cTt.b])
    ACT(sil.t[:], cTt.t[:], AF.Silu, [cTt.b], [sil.b])

    def pvc(name, j=0):
        c = PV[name] + j
        return pv.t[:, c:c + 1]

    blocks = [(0, CTX, 'c')] + [(CTX + i * 512, 512, 'x') for i in range(SEQ // 512)]

    def run_window(gens, width):
        gens = list(gens)
        active = []
        while gens or active:
            while gens and len(active) < width:
                active.append(gens.pop(0))
            for t in list(active):
                try:
                    next(t)
                except StopIteration:
                    active.remove(t)

    def run_tasks(tasks):
        tasks = [t for t in tasks if t is not None]
        while tasks:
            for t in list(tasks):
                try:
                    next(t)
                except StopIteration:
                    tasks.remove(t)

    def cast_gen(l, s, chunk, nbuf, engs, pref, ql="sp", qs="pool"):
        fb = [s.sb("%sf%d" % (pref, i), [128, chunk]) for i in range(nbuf)]
        bb = [s.sb("%sb%d" % (pref, i), [128, chunk], BF16) for i in range(nbuf)]

        def gen():
            allw = ((win_d, winb_d, 8, INC), (wout_d, woutb_d, 8, D), (w1_d, w1b_d, 8, 2 * DFF), (w2_d, w2b_d, 22, D))
            items = []
            for src, dst, rows, cols in allw:
                sv = src[l].rearrange("p a b -> p (a b)")
                dv = dst[l].rearrange("p a b -> p (a b)")
                tot = rows * cols
                for o in range(0, tot, chunk):
                    items.append((sv, dv, o, min(chunk, tot - o)))

            def load(i):
                if i < len(items):
                    sv, dv, o, n = items[i]
                    tk.dma(ql, fb[i % nbuf].t[:, :n], sv[:, o:o + n], writes=[fb[i % nbuf].b])
            for i in range(nbuf - 1):
                load(i)
            for i, (sv, dv, o, n) in enumerate(items):
                f = fb[i % nbuf]
                b_ = bb[i % nbuf]
                e = engs[i % len(engs)]
                CP(b_.t[:, :n], f.t[:, :n], [f.b], [b_.b], eng=e)
                tk.dma(qs, dv[:, o:o + n], b_.t[:, :n], reads=[b_.b])
                load(i + nbuf - 1)
                yield
        return gen()

    def stage_cast_init():
        with Scope() as s:
            xb = [s.sb("ixb%d" % i, [128, 8, 512]) for i in range(2)]
            pb = [s.sb("ipb%d" % i, [128, 8, 512]) for i in range(2)]

            def init_gen():
                it = 0
                for b in range(NB):
                    tk.dma("pool", xs_d[b][:, :, 0:CTX], ctxT_d[b])
                    for i in range(SEQ // 512):
                        X = xb[it % 2]
                        P = pb[it % 2]
                        it += 1
                        tk.dma("sp", X.t[:], xT_d[b][:, :, i * 512:(i + 1) * 512], writes=[X.b])
                        tk.dma("sp", P.t[:], posT_d[:, :, i * 512:(i + 1) * 512], writes=[P.b])
                        yield
                        TT(X.t[:], X.t[:], P.t[:], ALU.add, [X.b, P.b], [X.b], eng="pool")
                        tk.dma("pool", xs_d[b][:, :, CTX + i * 512:CTX + (i + 1) * 512], X.t[:], reads=[X.b])
                        yield
            shared = ([s.sb("wm%d" % i, [128, 8, 512]) for i in range(2)], s.sb("mrow", [4, 6 * D]))

            def mods():
                for l in range(1):
                    for _ in mod_gen(l, s, shared):
                        yield
            run_tasks([init_gen(), mods()])
        tk.barrier()

    def mod_gen(l, s, shared):
        mod_, pv_l, omka_ = modL[l], pvL[l], omkaL[l]
        wm, mrow = shared
        bm = s.sb("bm%d" % l, [128, 48])

        def gen():
            tk.dma("sp", bm.t[:], bmod_d[l], writes=[bm.b])
            tk.dma("sp", pv_l.t[:], pv_d[l], writes=[pv_l.b])
            for g in range(12):
                W = wm[g % 2]
                tk.dma("sp", W.t[:], wmod_d[l][:, :, g * 512:(g + 1) * 512], writes=[W.b])
                P = nb_()
                for kc in range(8):
                    PE(P.t[0:4, :], sil.t[:, kc, :], W.t[:, kc, :], kc == 0, kc == 7, [W.b, sil.b], [P.b])
                CP(mrow.t[:, g * 512:(g + 1) * 512], P.t[0:4, :], [P.b], [mrow.b], eng="dve")
                yield
            PT = nb_()
            ptv = PT.t[:, 0:192].rearrange("p (a b) -> p a b", b=4)
            for j in range(48):
                PET(ptv[:, j, :], mrow.t[:, j * 128:(j + 1) * 128], [mrow.b], [PT.b])
            TT(mod_.t[:, :, :], ptv[:, :, 0:3], bm.t[:, :].unsqueeze(2).to_broadcast([128, 48, 3]), ALU.add, [PT.b, bm.b], [mod_.b])
            for w in (1, 4):
                TS(mod_.t[:, w * 8:(w + 1) * 8, :], mod_.t[:, w * 8:(w + 1) * 8, :], 1.0, None, ALU.add, None, [mod_.b], [mod_.b])
            TS(omka_.t[:], pv_l.t[:, PV["ka"]:PV["ka"] + 2], -1.0, 1.0, ALU.mult, ALU.add, [pv_l.b], [omka_.b])
            for w in (2, 5):
                TS(mod_.t[:, w * 8:(w + 1) * 8, :], mod_.t[:, w * 8:(w + 1) * 8, :], 1.0 / ALPHA, None, ALU.mult, None, [mod_.b], [mod_.b])
            yield
        return gen()

    def mcol(which, kc, ni):
        return mod.t[:, which * 8 + kc, ni:ni + 1]

    def stage_A(l, last, win, winB):
        with Scope() as s:
            xsb = [s.sb("axs%d" % i, [128, 8, 512]) for i in range(2)]
            hx = [s.sb("ahx%d" % i, [128, 8, 512], BF16) for i in range(2)]
            hxB = [[Buf() for _ in range(8)] for i in range(2)]
            stg = [s.sb("astg%d" % i, [128, 512]) for i in range(4)]
            work = [(b, t0, n, kind) for b in range(NB) for (t0, n, kind) in blocks]
            ie = 0

            def load(k):
                b, t0, n, kind = work[k]
                X = xsb[k % 2]
                tk.dma("sp", X.t[:, :, :n], xs_d[b][:, :, t0:t0 + n], writes=[X.b])

            def domod(k):
                b, t0, n, kind = work[k]
                ni = 2 if kind == 'c' else b
                X, H = xsb[k % 2], hx[k % 2]
                for kc in range(8):
                    MOD(H.t[:, kc, :n], X.t[:, kc, :n], mcol(1, kc, ni), mcol(0, kc, ni), [X.b, mod.b], [hxB[k % 2][kc]])

            load(0)
            domod(0)
            for k in range(len(work)):
                b, t0, n, kind = work[k]
                H = hx[k % 2]
                nt = 9 if (last and kind == 'c') else 17
                for ct in range(nt):
                    if ct == 1 and k + 1 < len(work):
                        load(k + 1)
                    if ct == 7 and k + 1 < len(work):
                        domod(k + 1)
                    P = nb_()
                    for kc in range(8):
                        PE(P.t[:, :n], win.t[:, kc, ct * 128:(ct + 1) * 128], H.t[:, kc, :n], kc == 0, kc == 7,
                           [winB[kc], hxB[k % 2][kc]], [P.b])
                    S = stg[ie % 4]
                    ie += 1
                    CP(S.t[:, :n], P.t[:, :n], [P.b], [S.b])
                    tk.dma("pool", px_d[b, ct][:, t0:t0 + n], S.t[:, :n], reads=[S.b])
        tk.barrier()

    def stage_R(l, last):
        rwkv_pass(l, last)
        tk.barrier()

    def rwkv_pass(l, last):
        BK = 256
        NQ = BK // CH
        xdt = BF16
        base = [(0, CTX, 'c')] + [(CTX + i * BK, BK, 'x') for i in range(SEQ // BK)]
        blks = []
        for d_ in range(2):
            for b_ in range(NB):
                bl = base if d_ == 0 else [base[0]] + base[:0:-1]
                for i_, (t0_, n_, k_) in enumerate(bl):
                    blks.append((t0_, n_, k_, b_, d_, i_ == 0))
        yfd_b = [Buf("yfd%d" % i) for i in range(NB)]
        nblk = len(blks)
        with Scope() as s:
            n = BK
            w2s = s.sb("w2s", [128, 256])
            a2s = s.sb("a2s", [128, 256])
            g2 = s.sb("g2", [128, 256])
            tk.dma("sp", w2s.t[:], w2s_d[l], writes=[w2s.b])
            tk.dma("sp", a2s.t[:], a2s_d[l], writes=[a2s.b])
            tk.dma("sp", g2.t[:], g2_d[l], writes=[g2.b])
            zp = [[s.sb("zp%d_%d" % (p, i), [128, BK + 2]) for i in range(6)] for p in range(2)]
            lo6 = [s.sb("lo6_%d" % p, [128, BK]) for p in range(2)]
            lo7 = [s.sb("lo7_%d" % p, [128, BK]) for p in range(3)]
            lo8 = [s.sb("lo8_%d" % p, [128, BK]) for p in range(3)]
            cv = [[s.sb("cv%d_%d" % (p, i), [128, BK]) for i in range(6)] for p in range(3)]
            aaT = [[s.sb("aa%d_%d" % (p, m), [128, BK]) for m in range(2)] for p in range(3)]
            names = ["tdw", "lw", "Pp", "Qq", "Gx", "Gi", "eG", "eGn", "eGx", "eH", "kraw", "sq",
                     "rs", "kkn", "tmp", "kd", "bq"]
            tbM = [{nm: s.sb("r%d_%s" % (mm, nm), [128, BK]) for nm in names} for mm in range(2)]
            tbM[1]["tdw"] = tbM[0]["tdw"]
            tb = tbM[0]
            RpT = [[s.sb("Rp%d_%d" % (p, m), [128, BK]) for m in range(2)] for p in range(2)]
            RpbT = [[s.sb("Rpb%d_%d" % (p, m), [128, BK], BF16) for m in range(2)] for p in range(2)]
            etotT = [[s.sb("etot%d_%d" % (p, m), [128, NQ]) for m in range(2)] for p in range(2)]
            DgT = [[s.sb("Dg%d_%d" % (p, m), [128, NQ, 128]) for m in range(2)] for p in range(2)]
            exn = ["Ab", "Bb", "Kb", "Bh", "Kh", "Vb"]
            exT = [[{nm: s.sb("x%d%d_%s" % (p, m, nm), [128, NQ, 2, 64], xdt) for nm in exn} for m in range(2)] for p in range(2)]
            for p in range(2):
                for m in range(2):
                    for nm in exn:
                        MEMSET(exT[p][m][nm].t[:], 0.0, [exT[p][m][nm].b], eng="pool" if (m + p) % 2 else "dve")
            qn = ["X0", "X1", "XT0", "XT1", "T0", "T1", "W1", "W2", "BhT", "KhT", "P1T", "P2T", ]
            qT = [{nm: s.sb("q%d_%s" % (m, nm), [128, NQ, 128], xdt) for nm in qn} for m in range(2)]
            MbrT = [s.sb("Mbr%d" % m, [128, NQ, 64], xdt) for m in range(2)]
            MkrT = [s.sb("Mkr%d" % m, [128, NQ, 64], xdt) for m in range(2)]
            QG = [[dict(Q1=s.sb("Q1_%d%d" % (p, m), [128, NQ, 64], xdt), Q2=s.sb("Q2_%d%d" % (p, m), [128, NQ, 64], xdt),
                        G1=s.sb("G1_%d%d" % (p, m), [128, NQ, 128], xdt), G2=s.sb("G2_%d%d" % (p, m), [128, NQ, 128], xdt),
                        Vt=s.sb("Vt_%d%d" % (p, m), [128, NQ, 128], xdt)) for m in range(2)] for p in range(2)]
            St = [[s.sb("St%d%d" % (m, i), [128, 128], xdt) for i in range(2)] for m in range(2)]
            sti = [0, 0]
            for m in range(2):
                MEMSET(St[m][0].t[:], 0.0, [St[m][0].b])
            ybT = [[s.sb("yb%d_%d" % (p, m), [128, BK]) for m in range(2)] for p in range(2)]
            if True:
                yfT = [[s.sb("yf%d_%d" % (p, m), [128, BK]) for m in range(2)] for p in range(3)]
                ro = {nm: s.sb("ro_" + nm, [128, BK]) for nm in ["ys", "sq", "mu", "var", "rstd", "bon", "sg", "af", "t2"]}
                ob = [s.sb("ob%d" % m, [128, BK], BF16) for m in range(2)]
            PYb = [banks[6], banks[7]]

            def nbk():
                b_ = banks[bank_i[0] % 6]
                bank_i[0] += 1
                return b_

            def is_readout(k):
                return (blks[k][4] == 1) and not (last and blks[k][2] == 'c')

            def prep(k):
                t0, _, kind, b, d, _f = blks[k]
                p2, p3 = k % 2, k % 3
                s0, L = (0, CTX) if kind == 'c' else (CTX, SEQ)
                first = (t0 == s0)
                lastb = (t0 + n == s0 + L)
                for ct in range(6):
                    Z = zp[p2][ct]
                    lo_ = t0 - (0 if first else 1)
                    hi_ = t0 + n + (0 if lastb else 1)
                    o0 = 1 - (t0 - lo_)
                    tk.dma("sp", Z.t[:, o0:o0 + (hi_ - lo_)], px_d[b, ct][:, lo_:hi_], writes=[Z.b])
                    if first:
                        MEMSET(Z.t[:, 0:1], 0.0, [Z.b])
                    if lastb:
                        MEMSET(Z.t[:, n + 1:n + 2], 0.0, [Z.b])
                L6, L7, L8 = lo6[p2], lo7[p3], lo8[p3]
                tk.dma("sp", L6.t[:, :], px_d[b, 6][:, t0:t0 + n], writes=[L6.b])
                tk.dma("sp", L7.t[:, :], px_d[b, 7][:, t0:t0 + n], writes=[L7.b])
                if is_readout(k):
                    tk.dma("sp", L8.t[:, :], px_d[b, 8][:, t0:t0 + n], writes=[L8.b])
                    for m in range(2):
                        tk.dma("sp", yfT[p3][m].t[:, :], yf_d[b, m][:, t0:t0 + n], reads=[yfd_b[b]], writes=[yfT[p3][m].b])
                yield
                for ct in range(6):
                    Z = zp[p2][ct]
                    C = cv[p3][ct]
                    ACT(C.t[:, :], Z.t[:, 1:n + 1], AF.Identity, [Z.b, pv.b], [C.b], scale=pvc("conv", 6 + ct))
                    STT(C.t[:, :], Z.t[:, 0:n], pvc("conv", ct), C.t[:, :], ALU.mult, ALU.add, [Z.b, pv.b, C.b], [C.b])
                    STT(C.t[:, :], Z.t[:, 2:n + 2], pvc("conv", 12 + ct), C.t[:, :], ALU.mult, ALU.add, [Z.b, pv.b, C.b], [C.b])
                    yield
                hs = slice(d * 64, (d + 1) * 64)
                ACT(tb["tdw"].t[hs, :], L6.t[hs, :], AF.Tanh, [L6.b], [tb["tdw"].b])

                def pm(m, tb):
                    rr_, kk_, vv_ = cv[p3][m], cv[p3][2 + m], cv[p3][4 + m]
                    ms = slice(m * 128, (m + 1) * 128)
                    aa = aaT[p3][m]
                    Rp, Rpb, etot, ex = RpT[p2][m], RpbT[p2][m], etotT[p2][m], exT[p2][m]
                    P = nbk()
                    PE(P.t[:, :n], w2s.t[hs, ms], tb["tdw"].t[hs, :], True, True, [w2s.b, tb["tdw"].b], [P.b])
                    ACT(tb["lw"].t[:, :], P.t[:, :n], AF.Sigmoid, [P.b, pv.b], [tb["lw"].b], bias=pvc("w0", d * 2 + m))
                    ACT(tb["lw"].t[:, :], tb["lw"].t[:, :], AF.Identity, [tb["lw"].b], [tb["lw"].b], scale=-math.exp(-0.5))
                    P = nbk()
                    PE(P.t[:, :n], a2s.t[hs, ms], L7.t[hs, :], True, True, [a2s.b, L7.b], [P.b])
                    ACT(aa.t[:, :], P.t[:, :n], AF.Sigmoid, [P.b, pv.b], [aa.b], bias=pvc("a0", d * 2 + m))
                    yield
                    lw = tb["lw"]
                    Pp, Qq, Gx, Gi = tb["Pp"], tb["Qq"], tb["Gx"], tb["Gi"]
                    tk.op("dve", lambda e: e.tensor_tensor_scan(out=Pp.t[:, :], data0=scanmask.t[:, :n], data1=lw.t[:, :],
                                                                 initial=0.0, op0=ALU.mult, op1=ALU.add),
                          [scanmask.b, lw.b], [Pp.b])
                    P3 = Pp.t[:, :].rearrange("p (c j) -> p c j", j=CH)
                    tot_b = P3[:, :, CH - 1:CH].to_broadcast([128, NQ, CH])
                    TT(Qq.t[:, :].rearrange("p (c j) -> p c j", j=CH), tot_b, P3, ALU.subtract, [Pp.b], [Qq.b])
                    TT(Gx.t[:, :], Pp.t[:, :], lw.t[:, :], ALU.subtract, [Pp.b, lw.b], [Gx.b], eng="pool")
                    if d == 0:
                        Gin, Gex, Hh = Pp, Gx, Qq
                    else:
                        TT(Gi.t[:, :], Qq.t[:, :], lw.t[:, :], ALU.add, [Qq.b, lw.b], [Gi.b], eng="pool")
                        Gin, Gex, Hh = Gi, Qq, Gx
                    yield
                    ACT(tb["eG"].t[:, :], Gin.t[:, :], AF.Exp, [Gin.b], [tb["eG"].b])
                    ACT(tb["eGn"].t[:, :], Gin.t[:, :], AF.Exp, [Gin.b], [tb["eGn"].b], scale=-1.0)
                    ACT(tb["eGx"].t[:, :], Gex.t[:, :], AF.Exp, [Gex.b], [tb["eGx"].b])
                    ACT(tb["eH"].t[:, :], Hh.t[:, :], AF.Exp, [Hh.b], [tb["eH"].b])
                    ACT(etot.t[:, :], P3[:, :, CH - 1], AF.Exp, [Pp.b], [etot.b])
                    TT(DgT[p2][m].t[:], id4.t[:], etot.t[:, :].unsqueeze(2).to_broadcast([128, NQ, 128]), ALU.mult,
                       [id4.b, etot.b], [DgT[p2][m].b], eng="pool")
                    yield
                    kraw, sq, rs, kkn = tb["kraw"], tb["sq"], tb["rs"], tb["kkn"]
                    ACT(kraw.t[:, :], kk_.t[:, :], AF.Identity, [kk_.b, pv.b], [kraw.b], scale=pvc("kk", m))
                    ACT(sq.t[:, :], kk_.t[:, :], AF.Square, [kk_.b, pv.b], [sq.b], scale=pvc("kk", m))
                    P = nbk()
                    PE(P.t[:, :n], headones, sq.t[:, :], True, True, [cstb, sq.b], [P.b])
                    ACT(rs.t[:, :], P.t[:, :n], AF.Ln, [P.b], [rs.b], bias=eps_kk.t[:, 0:1])
                    ACT(rs.t[:, :], rs.t[:, :], AF.Exp, [rs.b], [rs.b], scale=-0.5)
                    TT(kkn.t[:, :], kraw.t[:, :], rs.t[:, :], ALU.mult, [kraw.b, rs.b], [kkn.b])
                    yield
                    tmp, kd, bq = tb["tmp"], tb["kd"], tb["bq"]
                    TS(tmp.t[:, :], aa.t[:, :], pvc("ka", m), omka.t[:, m:m + 1], ALU.mult, ALU.add, [aa.b, pv.b, omka.b], [tmp.b], eng="pool")
                    TT(kd.t[:, :], kk_.t[:, :], tmp.t[:, :], ALU.mult, [kk_.b, tmp.b], [kd.b], eng="pool")
                    TT(bq.t[:, :], kkn.t[:, :], aa.t[:, :], ALU.mult, [kkn.b, aa.b], [bq.b])
                    TT(Rp.t[:, :], rr_.t[:, :], tb["eG"].t[:, :], ALU.mult, [rr_.b, tb["eG"].b], [Rp.b])
                    CP(Rpb.t[:, :], Rp.t[:, :], [Rp.b], [Rpb.b], eng="act")
                    yield

                    def v3(tbuf, hh):
                        return tbuf.t[hh * 64:(hh + 1) * 64, :].rearrange("p (c j) -> p c j", j=CH)

                    for hh in range(2):
                        def xo(nm):
                            return ex[nm].t[hh * 64:(hh + 1) * 64, :, hh, :]
                        STT(xo("Ab"), v3(kkn, hh), -1.0, v3(tb["eGx"], hh), ALU.mult, ALU.mult, [kkn.b, tb["eGx"].b], [ex["Ab"].b])
                        TT(xo("Bb"), v3(bq, hh), v3(tb["eGn"], hh), ALU.mult, [bq.b, tb["eGn"].b], [ex["Bb"].b])
                        TT(xo("Kb"), v3(kd, hh), v3(tb["eGn"], hh), ALU.mult, [kd.b, tb["eGn"].b], [ex["Kb"].b], eng="pool")
                        TT(xo("Bh"), v3(bq, hh), v3(tb["eH"], hh), ALU.mult, [bq.b, tb["eH"].b], [ex["Bh"].b])
                        TT(xo("Kh"), v3(kd, hh), v3(tb["eH"], hh), ALU.mult, [kd.b, tb["eH"].b], [ex["Kh"].b], eng="pool")
                        CP(xo("Vb"), v3(vv_, hh), [vv_.b], [ex["Vb"].b], eng="act")
                        yield

                alive = [pm(0, tbM[0]), pm(1, tbM[1])]
                while alive:
                    for g in list(alive):
                        try:
                            next(g)
                        except StopIteration:
                            alive.remove(g)
                    yield

            def chain(k, m):
                p2 = k % 2
                d = blks[k][4]
                maskN, maskNT, maskM = (SU4, SL4, UI4) if d == 0 else (SL4, SU4, LI4)
                Rp, Rpb, etot, ex = RpT[p2][m], RpbT[p2][m], etotT[p2][m], exT[p2][m]
                Dg = DgT[p2][m]
                q = qT[m]
                Mbr, Mkr = MbrT[m], MkrT[m]
                o = QG[p2][m]
                Q1, Q2, G1, G2, Vt = o["Q1"], o["Q2"], o["G1"], o["G2"], o["Vt"]

                def xc(nm, c):
                    return ex[nm].t[:, c, :, :].rearrange("p a b -> p (a b)")

                def quad_mm(lhs_fn, rhs_fn, ncol, reads):
                    P = nbk()
                    pv_ = P.t[:, 0:4 * ncol].rearrange("p (a b) -> p a b", a=4)
                    for c in range(NQ):
                        PE(pv_[:, c, :], lhs_fn(c), rhs_fn(c), True, True, reads, [P.b])
                    return P, pv_

                X0, X1, XT0, XT1, T0, T1 = q["X0"], q["X1"], q["XT0"], q["XT1"], q["T0"], q["T1"]
                W1, W2, BhT, KhT, P1T, P2T = q["W1"], q["W2"], q["BhT"], q["KhT"], q["P1T"], q["P2T"]
                P, pv_ = quad_mm(lambda c: xc("Bb", c), lambda c: xc("Ab", c), 128, [ex["Bb"].b, ex["Ab"].b])
                TT(X0.t[:], pv_, maskN, ALU.mult, [P.b, cstb], [X0.b])
                TT(T0.t[:], X0.t[:], id4.t[:], ALU.add, [X0.b, id4.b], [T0.b], eng="pool")
                yield
                P, pv_ = quad_mm(lambda c: xc("Ab", c), lambda c: xc("Bb", c), 128, [ex["Bb"].b, ex["Ab"].b])
                TT(XT0.t[:], pv_, maskNT, ALU.mult, [P.b, cstb], [XT0.b])
                yield
                Xc, XTc, Tc = X0, XT0, T0
                Xn, XTn, Tn = X1, XT1, T1
                side = [
                    ("mm", lambda c: xc("Ab", c), lambda c: xc("Kb", c), 128, [ex["Kb"].b, ex["Ab"].b], W2, maskNT),
                    ("mm", lambda c: xc("Bb", c), lambda c: Rpb.t[:, c * 64:(c + 1) * 64], 64, [ex["Bb"].b, Rpb.b], Mbr, maskM),
                    ("mm", lambda c: xc("Kb", c), lambda c: Rpb.t[:, c * 64:(c + 1) * 64], 64, [ex["Kb"].b, Rpb.b], Mkr, maskM),
                    ("tr", "Ab", W1), ("tr", "Bh", BhT), ("tr", "Kh", KhT), ("tr", "Vb", Vt)]

                def do_side():
                    if not side:
                        return
                    it = side.pop(0)
                    if it[0] == "mm":
                        _, lf, rf, nco, rd, dst, msk = it
                        P, pv_ = quad_mm(lf, rf, nco, rd)
                        TT(dst.t[:], pv_, msk, ALU.mult, [P.b, cstb], [dst.b])
                    else:
                        _, nm, dst = it
                        P = nbk()
                        pv_ = P.t[:, 0:256].bitcast(BF16).rearrange("p (a b) -> p a b", a=4)
                        for c in range(NQ):
                            PET(pv_[:, c, :], xc(nm, c), [ex[nm].b], [P.b])
                        CP(dst.t[:], pv_, [P.b], [dst.b], eng="act")

                for lev in range(1, 6):
                    if lev < 5:
                        P, pv_ = quad_mm(lambda c: XTc.t[:, c, :], lambda c: Xc.t[:, c, :], 128, [XTc.b, Xc.b])
                        CP(Xn.t[:], pv_, [P.b], [Xn.b], eng="act")
                    P, pv_ = quad_mm(lambda c: Xc.t[:, c, :], lambda c: XTc.t[:, c, :], 128, [XTc.b, Xc.b])
                    CP(XTn.t[:], pv_, [P.b], [XTn.b], eng="act")
                    do_side()
                    yield
                    P, pv_ = quad_mm(lambda c: XTn.t[:, c, :], lambda c: Tc.t[:, c, :], 128, [XTn.b, Tc.b])
                    TT(Tn.t[:], pv_, Tc.t[:], ALU.add, [P.b, Tc.b], [Tn.b])
                    do_side()
                    yield
                    Xc, Xn = Xn, Xc
                    XTc, XTn = XTn, XTc
                    Tc, Tn = Tn, Tc
                while side:
                    do_side()
                    yield
                P, pv_ = quad_mm(lambda c: Tc.t[:, c, :], lambda c: W1.t[:, c, :], 128, [Tc.b, W1.b])
                CP(P1T.t[:], pv_, [P.b], [P1T.b], eng="act")
                P, pv_ = quad_mm(lambda c: Tc.t[:, c, :], lambda c: W2.t[:, c, :], 128, [Tc.b, W2.b])
                CP(P2T.t[:], pv_, [P.b], [P2T.b], eng="act")
                yield
                P, pv_ = quad_mm(lambda c: P1T.t[:, c, :], lambda c: Mbr.t[:, c, :], 64, [P1T.b, Mbr.b])
                TT(Q1.t[:], pv_, Rp.t[:, :].rearrange("p (c j) -> p c j", j=CH), ALU.add, [P.b, Rp.b], [Q1.b])
                P, pv_ = quad_mm(lambda c: P2T.t[:, c, :], lambda c: Mbr.t[:, c, :], 64, [P2T.b, Mbr.b])
                TT(Q2.t[:], pv_, Mkr.t[:], ALU.add, [P.b, Mkr.b], [Q2.b])
                yield
                P, pv_ = quad_mm(lambda c: P1T.t[:, c, :], lambda c: BhT.t[:, c, :], 128, [P1T.b, BhT.b])
                TT(G1.t[:], pv_, Dg.t[:], ALU.add, [P.b, Dg.b], [G1.b])
                P, pv_ = quad_mm(lambda c: P2T.t[:, c, :], lambda c: BhT.t[:, c, :], 128, [P2T.b, BhT.b])
                TT(G2.t[:], pv_, KhT.t[:], ALU.add, [P.b, KhT.b], [G2.b])
                yield

            def seq(k):
                t0, _, kind, b, d, fpass = blks[k]
                p2, p3 = k % 2, k % 3
                if fpass:
                    for m in range(2):
                        MEMSET(St[m][sti[m] % 2].t[:], 0.0, [St[m][sti[m] % 2].b], eng="dve")
                corder = list(range(NQ)) if d == 0 else list(range(NQ - 1, -1, -1))
                for ci, c in enumerate(corder):
                    for m in range(2):
                        o = QG[p2][m]
                        Sc = St[m][sti[m] % 2]
                        Sn = St[m][(sti[m] + 1) % 2]
                        sti[m] += 1
                        PY = PYb[m]
                        yv = PY.t[:, 0:256].rearrange("p (a b) -> p a b", a=4)[:, c, :]
                        PE(yv, Sc.t[:], o["Q1"].t[:, c, :], True, False, [Sc.b, o["Q1"].b], [PY.b])
                        PE(yv, o["Vt"].t[:, c, :], o["Q2"].t[:, c, :], False, True, [o["Vt"].b, o["Q2"].b], [PY.b])
                        PS = nbk()
                        PE(PS.t[:, 0:128], o["G1"].t[:, c, :], Sc.t[:], True, False, [o["G1"].b, Sc.b], [PS.b])
                        PE(PS.t[:, 0:128], o["G2"].t[:, c, :], o["Vt"].t[:, c, :], False, True, [o["G2"].b, o["Vt"].b], [PS.b])
                        CP(Sn.t[:], PS.t[:, 0:128], [PS.b], [Sn.b], eng="act")
                    yield
                for m in range(2):
                    CP(ybT[p2][m].t[:, :], PYb[m].t[:, 0:256], [PYb[m].b], [ybT[p2][m].b], eng="act")
                yield
                if d == 0:
                    for m in range(2):
                        tk.dma("pool", yf_d[b, m][:, t0:t0 + n], ybT[p2][m].t[:, :], reads=[ybT[p2][m].b], writes=[yfd_b[b]])
                    return
                if not is_readout(k):
                    return
                L7, L8 = lo7[p3], lo8[p3]
                ACT(ro["sg"].t[:, :], L8.t[:, :], AF.Sigmoid, [L8.b], [ro["sg"].b])
                for m in range(2):
                    rr_, kk_, vv_ = cv[p3][m], cv[p3][2 + m], cv[p3][4 + m]
                    ms = slice(m * 128, (m + 1) * 128)
                    aa = aaT[p3][m]
                    yb, yf = ybT[p2][m], yfT[p3][m]
                    P = nbk()
                    PE(P.t[:, :n], a2s.t[0:64, ms], L7.t[0:64, :], True, True, [a2s.b, L7.b], [P.b])
                    af = ro["af"]
                    ACT(af.t[:, :], P.t[:, :n], AF.Sigmoid, [P.b, pv.b], [af.b], bias=pvc("a0", m))
                    t2 = ro["t2"]
                    TT(t2.t[:, :], af.t[:, :], aa.t[:, :], ALU.add, [af.b, aa.b], [t2.b], eng="pool")
                    TS(t2.t[:, :], t2.t[:, :], pvc("ka", m), None, ALU.mult, None, [t2.b, pv.b], [t2.b])
                    STT(t2.t[:, :], omka.t[:, m:m + 1].to_broadcast([128, n]), 2.0, t2.t[:, :], ALU.mult, ALU.add,
                        [omka.b, t2.b], [t2.b])
                    TT(t2.t[:, :], t2.t[:, :], kk_.t[:, :], ALU.mult, [t2.b, kk_.b], [t2.b])
                    TT(t2.t[:, :], t2.t[:, :], rr_.t[:, :], ALU.mult, [t2.b, rr_.b], [t2.b])
                    TS(t2.t[:, :], t2.t[:, :], pvc("rk", m), None, ALU.mult, None, [t2.b, pv.b], [t2.b])
                    yield
                    PB = nbk()
                    PE(PB.t[:, :n], headones, t2.t[:, :], True, True, [cstb, t2.b], [PB.b])
                    bon = ro["bon"]
                    TT(bon.t[:, :], PB.t[:, :n], vv_.t[:, :], ALU.mult, [PB.b, vv_.b], [bon.b])
                    ys, sq2, mu, var, rstd = ro["ys"], ro["sq"], ro["mu"], ro["var"], ro["rstd"]
                    TT(ys.t[:, :], yf.t[:, :], yb.t[:, :], ALU.add, [yf.b, yb.b], [ys.b], eng="pool")
                    ACT(sq2.t[:, :], ys.t[:, :], AF.Square, [ys.b], [sq2.b])
                    yield
                    P1 = nbk()
                    PE(P1.t[:, :n], headones, ys.t[:, :], True, True, [cstb, ys.b], [P1.b])
                    P2 = nbk()
                    PE(P2.t[:, :n], headones, sq2.t[:, :], True, True, [cstb, sq2.b], [P2.b])
                    TS(mu.t[:, :], P1.t[:, :n], 1.0 / 64, None, ALU.mult, None, [P1.b], [mu.b])
                    TT(var.t[:, :], mu.t[:, :], mu.t[:, :], ALU.mult, [mu.b], [var.b])
                    STT(var.t[:, :], P2.t[:, :n], 1.0 / 64, var.t[:, :], ALU.mult, ALU.subtract, [P2.b, var.b], [var.b])
                    ACT(rstd.t[:, :], var.t[:, :], AF.Ln, [var.b], [rstd.b], bias=eps_gn.t[:, 0:1])
                    ACT(rstd.t[:, :], rstd.t[:, :], AF.Exp, [rstd.b], [rstd.b], scale=-0.5)
                    yield
                    TT(ys.t[:, :], ys.t[:, :], mu.t[:, :], ALU.subtract, [ys.b, mu.b], [ys.b], eng="pool")
                    TT(ys.t[:, :], ys.t[:, :], rstd.t[:, :], ALU.mult, [ys.b, rstd.b], [ys.b])
                    TS(ys.t[:, :], ys.t[:, :], pvc("gng", m), pvc("gnb", m), ALU.mult, ALU.add, [ys.b, pv.b], [ys.b])
                    TT(ys.t[:, :], ys.t[:, :], bon.t[:, :], ALU.add, [ys.b, bon.b], [ys.b])
                    PG = nbk()
                    PE(PG.t[:, :n], g2.t[:, ms], ro["sg"].t[:, :], True, True, [g2.b, ro["sg"].b], [PG.b])
                    TT(ob[m].t[:, :], ys.t[:, :], PG.t[:, :n], ALU.mult, [ys.b, PG.b], [ob[m].b])
                    tk.dma("pool", mix_d[b][:, m, t0:t0 + n], ob[m].t[:, :], reads=[ob[m].b])
                    yield

            run_tasks([prep(0)])
            for r in range(nblk + 1):
                tasks = []
                if r < nblk:
                    tasks += [chain(r, 0), chain(r, 1)]
                if r + 1 < nblk:
                    tasks.append(prep(r + 1))
                if r >= 1:
                    tasks.append(seq(r - 1))
                run_tasks(tasks)

    def seqs(last):
        return [(CTX, SEQ, 'x')] if last else [(0, CTX, 'c'), (CTX, SEQ, 'x')]

    def stage_P(l, last):
        with Scope() as s:
            pw = s.sb("pw", [128, 2, 128])
            pwb = s.sb("pwb", [128, 2, 128], BF16)
            tk.dma("sp", pw.t[:], poolbd_d[l], writes=[pw.b])
            CP(pwb.t[:], pw.t[:], [pw.b], [pwb.b])
            rcx = s.sb("rcx", [128, 2, SEQ])
            rcc = s.sb("rcc", [128, 2, CTX])
            tk.dma("sp", rcx.t[:], rc_x_d[:, :, :], writes=[rcx.b])
            tk.dma("sp", rcc.t[:], rc_c_d[:, :, :], writes=[rcc.b])
            PADW = SEQ + 16
            zz = [s.sb("pz%d" % i, [128, PADW]) for i in range(2)]
            sA = [s.sb("ps%d" % i, [128, PADW]) for i in range(4)]
            acc = s.sb("pacc", [128, SEQ])
            dd = [s.sb("pdd%d" % i, [128, SEQ], BF16) for i in range(2)]
            ob = [s.sb("pob%d" % i, [128, 512], BF16) for i in range(2)]
            io = 0
            units = [(b, s0, L, kind, m) for b in range(NB) for (s0, L, kind) in seqs(last) for m in range(2)]

            def load(u):
                b, s0, L, kind, m = units[u]
                z = zz[u % 2]
                MEMSET(z.t[:, 0:8], 0.0, [z.b])
                MEMSET(z.t[:, 8 + L:16 + L], 0.0, [z.b])
                tk.dma("sp", z.t[:, 8:8 + L], px_d[b, 9 + m][:, s0:s0 + L], writes=[z.b])

            load(0)

            def pgen():
              io = 0
              for u, (b, s0, L, kind, m) in enumerate(units):
                  rc = rcc if kind == 'c' else rcx
                  z = zz[u % 2]
                  D_ = dd[u % 2]
                  if u + 1 < len(units):
                      load(u + 1)
                  prev = z
                  step = 1
                  wl = L + 16
                  nlev = 2 if m == 0 else 4
                  for lev in range(nlev):
                      cur = sA[lev]
                      wl2 = wl - step
                      TT(cur.t[:, 0:wl2], prev.t[:, 0:wl2], prev.t[:, step:step + wl2], ALU.add, [prev.b], [cur.b])
                      prev = cur
                      step *= 2
                      wl = wl2
                  for hh in range(2):
                      w = POOLW[2 * m + hh]
                      lev = int(math.log2(w)) - 1
                      hs = slice(hh * 64, (hh + 1) * 64)
                      o = 8 - w // 2
                      TT(acc.t[hs, :L], sA[lev].t[hs, o:o + L], rc.t[hs, m, :L], ALU.mult, [sA[lev].b, rc.b], [acc.b])
                  TT(D_.t[:, :L], acc.t[:, :L], z.t[:, 8:8 + L], ALU.subtract, [acc.b, z.b], [D_.b])
                  yield
                  for o in range(0, L, 512):
                      n = min(512, L - o)
                      P = nb_()
                      PE(P.t[:, :n], pwb.t[:, m, :], D_.t[:, o:o + n], True, True, [pwb.b, D_.b], [P.b])
                      O = ob[io % 2]
                      io += 1
                      ACT(O.t[:, :n], P.t[:, :n], AF.Identity, [P.b, pv.b], [O.b], scale=pvc("psc", m))
                      tk.dma("pool", mix_d[b][:, 2 + m, s0 + o:s0 + o + n], O.t[:, :n], reads=[O.b])
                      yield

            tasks = [pgen()]
            if l + 1 < nlayers:
                shared = ([s.sb("pwm%d" % i, [128, 8, 512]) for i in range(2)], s.sb("pmrow", [4, 6 * D]))
                tasks.append(mod_gen(l + 1, s, shared))
            run_tasks(tasks)
        tk.barrier()

    def stage_G(l, last):
        with Scope() as s:
            wsf = s.sb("wsf", [128, 4, 128])
            wsb = s.sb("wsb", [128, 4, 128], BF16)
            tk.dma("sp", wsf.t[:], wsT_d[l], writes=[wsf.b])
            CP(wsb.t[:], wsf.t[:], [wsf.b], [wsb.b])
            bsb = s.sb("bsb", [128, 2, 128])
            tk.dma("sp", bsb.t[:], bsb_d[l], writes=[bsb.b])
            glg = s.sb("glg", [128, 256])
            glb = s.sb("glb", [128, 256])
            tk.dma("sp", glg.t[:], glg_d[l], writes=[glg.b])
            tk.dma("sp", glb.t[:], glb_d[l], writes=[glb.b])
            zin = [s.sb("gz%d" % i, [128, 512]) for i in range(4)]
            NS = 2
            geS = [[s.sb("gg%d_%d" % (u, i), [128, 512]) for i in range(4)] for u in range(NS)]
            vtS = [s.sb("gvt%d" % u, [128, 4, 256]) for u in range(NS)]
            vsqS = [s.sb("gvsq%d" % u, [128, 4, 256]) for u in range(NS)]
            vnS = [s.sb("gvn%d" % u, [128, 4, 256], BF16) for u in range(NS)]
            stS = [[s.sb("gs%d_%d" % (u, i), [128, 16]) for i in range(4)] for u in range(NS)]
            svS = [s.sb("gsv%d" % u, [128, 4, 128]) for u in range(NS)]
            ob = [s.sb("gob%d" % i, [128, 512], BF16) for i in range(2)]
            cnt = [0]
            work = [(b, t0, n) for b in range(NB) for (t0, n, kind) in blocks if not (last and kind == 'c')]

            def blk(k):
                b, t0, n = work[k]
                u = k % NS
                ge, vt, vsq, vn, sv = geS[u], vtS[u], vsqS[u], vnS[u], svS[u]
                st1, st2, mu, rstd = stS[u]
                nck = n // 128
                for i in range(4):
                    tk.dma("sp", zin[i].t[:, :n], px_d[b, 11 + i][:, t0:t0 + n], writes=[zin[i].b])
                    ACT(ge[i].t[:, :n], zin[i].t[:, :n], AF.Gelu_apprx_tanh, [zin[i].b], [ge[i].b])
                yield
                for mm in range(2):
                    P = nb_()
                    pv_ = P.t[:, :].rearrange("p (a b) -> p a b", a=4)
                    for ck in range(nck):
                        PET(pv_[:, ck, :], ge[2 + mm].t[:, ck * 128:(ck + 1) * 128], [ge[2 + mm].b], [P.b])
                    CP(vt.t[:, :nck, mm * 128:(mm + 1) * 128], pv_[:, :nck, :], [P.b], [vt.b], eng="act")
                yield
                v4 = vt.t[:, :nck, :].rearrange("p a (g c) -> p (a g) c", c=64)
                ng = nck * 4
                tk.op("dve", lambda e: e.tensor_reduce(out=st1.t[:, :ng], in_=v4, axis=AX.X, op=ALU.add), [vt.b], [st1.b])
                ACT(vsq.t[:, :nck, :], vt.t[:, :nck, :], AF.Square, [vt.b], [vsq.b])
                q4 = vsq.t[:, :nck, :].rearrange("p a (g c) -> p (a g) c", c=64)
                tk.op("dve", lambda e: e.tensor_reduce(out=st2.t[:, :ng], in_=q4, axis=AX.X, op=ALU.add), [vsq.b], [st2.b])
                yield
                TS(mu.t[:, :ng], st1.t[:, :ng], 1.0 / 64, None, ALU.mult, None, [st1.b], [mu.b])
                TT(st1.t[:, :ng], mu.t[:, :ng], mu.t[:, :ng], ALU.mult, [mu.b], [st1.b])
                STT(st2.t[:, :ng], st2.t[:, :ng], 1.0 / 64, st1.t[:, :ng], ALU.mult, ALU.subtract, [st2.b, st1.b], [st2.b])
                ACT(rstd.t[:, :ng], st2.t[:, :ng], AF.Ln, [st2.b], [rstd.b], bias=eps_ln.t[:, 0:1])
                ACT(rstd.t[:, :ng], rstd.t[:, :ng], AF.Exp, [rstd.b], [rstd.b], scale=-0.5)
                yield
                TT(v4, v4, mu.t[:, :ng].unsqueeze(2).to_broadcast([128, ng, 64]), ALU.subtract, [vt.b, mu.b], [vt.b], eng="pool")
                yield
                TT(v4, v4, rstd.t[:, :ng].unsqueeze(2).to_broadcast([128, ng, 64]), ALU.mult, [vt.b, rstd.b], [vt.b])
                yield
                gb = glg.t[:, :].unsqueeze(1).to_broadcast([128, nck, 256])
                bb_ = glb.t[:, :].unsqueeze(1).to_broadcast([128, nck, 256])
                TT(vt.t[:, :nck, :], vt.t[:, :nck, :], gb, ALU.mult, [vt.b, glg.b], [vt.b])
                yield
                TT(vn.t[:, :nck, :], vt.t[:, :nck, :], bb_, ALU.add, [vt.b, glb.b], [vn.b], eng="pool")
                yield
                for mm in range(2):
                    PA = nb_()
                    PBk = nb_()
                    pa = PA.t[:, :].rearrange("p (a b) -> p a b", a=4)
                    pb_ = PBk.t[:, :].rearrange("p (a b) -> p a b", a=4)
                    for ck in range(nck):
                        PE(pa[:, ck, :], vn.t[:, ck, mm * 128:(mm + 1) * 128], wsb.t[:, 2 * mm, :], True, True, [vn.b, wsb.b], [PA.b])
                        PE(pb_[:, ck, :], vn.t[:, ck, mm * 128:(mm + 1) * 128], wsb.t[:, 2 * mm + 1, :], True, True, [vn.b, wsb.b], [PBk.b])
                    bs0 = bsb.t[0:64, mm, :].unsqueeze(1).to_broadcast([64, nck, 128])
                    bs1 = bsb.t[64:128, mm, :].unsqueeze(1).to_broadcast([64, nck, 128])
                    TT(sv.t[0:64, :nck, :], pa[0:64, :nck, :], bs0, ALU.add, [PA.b, bsb.b], [sv.b])
                    TT(sv.t[64:128, :nck, :], pb_[64:128, :nck, :], bs1, ALU.add, [PBk.b, bsb.b], [sv.b])
                    O = ob[cnt[0] % 2]
                    cnt[0] += 1
                    TT(O.t[:, :n], ge[mm].t[:, :n], sv.t[:, :nck, :].rearrange("p a b -> p (a b)"), ALU.mult,
                       [ge[mm].b, sv.b], [O.b])
                    tk.dma("pool", mix_d[b][:, 4 + mm, t0:t0 + n], O.t[:, :n], reads=[O.b])
                    yield

            run_window([blk(k) for k in range(len(work))], NS)
        tk.barrier()

    def stage_F(l, last, clx, slx, clxB, slxB):
        with Scope() as s:
            if not last:
                clc = s.sb("clc", [128, 2, CTX], BF16)
                slc = s.sb("slc", [128, 2, CTX], BF16)
                tk.dma("sp", clc.t[:], cl_c_d[:, :, :], writes=[clc.b])
                tk.dma("sp", slc.t[:], sl_c_d[:, :, :], writes=[slc.b])
            cs64 = s.sb("cs64", [128, 128], BF16)
            tk.dma("sp", cs64.t[:], cs64_d[:, :], writes=[cs64.b])
            fw = s.sb("fw", [128, 2, 128])
            fwb = s.sb("fwb", [128, 2, 128], BF16)
            tk.dma("sp", fw.t[:], fnetbd_d[l], writes=[fw.b])
            CP(fwb.t[:], fw.t[:], [fw.b], [fwb.b])
            zf = [s.sb("fz%d" % i, [128, 512]) for i in range(2)]
            units = [(b, s0, L, kind) for b in range(NB) for (s0, L, kind) in seqs(last)]
            zbU = [[s.sb("fzb%d_%d" % (u, i), [128, SEQ], BF16) for i in range(2)] for u in range(2)]
            zcsU = [s.sb("zcs%d" % u, [128, 16, 2, 256], BF16) for u in range(2)]
            fb = [s.sb("ffb%d" % i, [128, 512], BF16) for i in range(2)]
            ob = [s.sb("fob%d" % i, [128, 512], BF16) for i in range(2)]
            cnt = [0, 0, 0]

            def nbf():
                b_ = banks[bank_i[0] % 6]
                bank_i[0] += 1
                return b_

            def front(ui):
                b, s0, L, kind = units[ui]
                zb, zcs = zbU[ui % 2], zcsU[ui % 2]
                ntc = L // 128
                for m in range(2):
                    for o in range(0, L, 512):
                        n = min(512, L - o)
                        Z = zf[cnt[1] % 2]
                        cnt[1] += 1
                        tk.dma("sp", Z.t[:, :n], px_d[b, 15 + m][:, s0 + o:s0 + o + n], writes=[Z.b])
                        CP(zb[m].t[:, o:o + n], Z.t[:, :n], [Z.b], [zb[m].b])
                        yield
                for tc in range(ntc):
                    zv = zcs.t[:, tc, :, :].rearrange("p x (m h c) -> p h m x c", m=2, h=2)
                    for hh in range(2):
                        P = nbf()
                        hs = slice(hh * 64, (hh + 1) * 64)
                        for m in range(2):
                            PE(P.t[:, m * 128:(m + 1) * 128], zb[m].t[hs, tc * 128:(tc + 1) * 128], cs64.t[hs, :], True, True,
                               [zb[m].b, cs64.b], [P.b])
                        CP(zv[:, hh], P.t[:, 0:256].rearrange("p (m x c) -> p m x c", m=2, x=2), [P.b], [zcs.b])
                    yield

            def back(ui):
                b, s0, L, kind = units[ui]
                zcs = zcsU[ui % 2]
                cl, sl = (clc, slc) if kind == 'c' else (clx, slx)
                clB = [clc.b] * 2 if kind == 'c' else clxB
                slB = [slc.b] * 2 if kind == 'c' else slxB
                ntc = L // 128
                for m in range(2):
                    for o in range(0, L, 512):
                        n = min(512, L - o)
                        P = banks[6 + cnt[2] % 2]
                        cnt[2] += 1
                        for tc in range(ntc):
                            PE(P.t[:, :n], zcs.t[:, tc, 0, m * 128:(m + 1) * 128], cl.t[:, tc, o:o + n], tc == 0, False,
                               [zcs.b, clB[tc]], [P.b])
                            PE(P.t[:, :n], zcs.t[:, tc, 1, m * 128:(m + 1) * 128], sl.t[:, tc, o:o + n], False, tc == ntc - 1,
                               [zcs.b, slB[tc]], [P.b])
                            if tc % 4 == 3:
                                yield
                        Fb = fb[cnt[0] % 2]
                        O = ob[cnt[0] % 2]
                        cnt[0] += 1
                        CP(Fb.t[:, :n], P.t[:, :n], [P.b], [Fb.b])
                        P2 = nbf()
                        PE(P2.t[:, :n], fwb.t[:, m, :], Fb.t[:, :n], True, True, [fwb.b, Fb.b], [P2.b])
                        TS(O.t[:, :n], P2.t[:, :n], pvc("fnb", m), None, ALU.add, None, [P2.b, pv.b], [O.b])
                        tk.dma("pool", mix_d[b][:, 6 + m, s0 + o:s0 + o + n], O.t[:, :n], reads=[O.b])
                        yield

            run_tasks([front(0)])
            for ui in range(len(units)):
                run_tasks([back(ui), front(ui + 1) if ui + 1 < len(units) else None])
        tk.barrier()

    def ln_gen(y, n, gname, bname, sc, eps_t=None):
        eps_t = eps_t or eps_ln
        ybf, ysq, mu, rstd, var = sc
        CP(ybf.t[:, :, :n], y.t[:, :, :n], [y.b], [ybf.b], eng="dve")
        ACT(ysq.t[:, :, :n], y.t[:, :, :n], AF.Square, [y.b], [ysq.b])
        yield
        P1 = nb_()
        P2 = nb_()
        for ot in range(8):
            PE(P1.t[:, :n], ones_bf.t[:], ybf.t[:, ot, :n], ot == 0, ot == 7, [ones_bf.b, ybf.b], [P1.b])
        for ot in range(8):
            PE(P2.t[:, :n], ones_bf.t[:], ysq.t[:, ot, :n], ot == 0, ot == 7, [ones_bf.b, ysq.b], [P2.b])
        TS(mu.t[:, :n], P1.t[:, :n], 1.0 / D, None, ALU.mult, None, [P1.b], [mu.b])
        TT(var.t[:, :n], mu.t[:, :n], mu.t[:, :n], ALU.mult, [mu.b], [var.b])
        STT(var.t[:, :n], P2.t[:, :n], 1.0 / D, var.t[:, :n], ALU.mult, ALU.subtract, [P2.b, var.b], [var.b])
        yield
        ACT(rstd.t[:, :n], var.t[:, :n], AF.Ln, [var.b], [rstd.b], bias=eps_t.t[:, 0:1])
        ACT(rstd.t[:, :n], rstd.t[:, :n], AF.Exp, [rstd.b], [rstd.b], scale=-0.5)
        yield
        mub = mu.t[:, :n].unsqueeze(1).to_broadcast([128, 8, n])
        rsb = rstd.t[:, :n].unsqueeze(1).to_broadcast([128, 8, n])
        TT(y.t[:, :, :n], y.t[:, :, :n], mub, ALU.subtract, [y.b, mu.b], [y.b])
        yield
        TT(y.t[:, :, :n], y.t[:, :, :n], rsb, ALU.mult, [y.b, rstd.b], [y.b])
        yield
        for ot in range(8):
            MOD(y.t[:, ot, :n], y.t[:, ot, :n], pvc(gname, ot), pvc(bname, ot), [y.b, pv.b], [y.b], eng="act")
        yield

    def ln_gen2(y, yb, n, gname, bname, sc, eps_t, sub_eng="pool"):
        ybf, ysq, mu, rstd, var = sc
        for ot in range(8):
            CP(ybf.t[:, ot, :n], y.t[:, ot, :n], [yb[ot]], [ybf.b], eng="act")
            ACT(ysq.t[:, ot, :n], y.t[:, ot, :n], AF.Square, [yb[ot]], [ysq.b])
            if ot % 2:
                yield
        P1 = nb_()
        P2 = nb_()
        for ot in range(8):
            PE(P1.t[:, :n], ones_bf.t[:], ybf.t[:, ot, :n], ot == 0, ot == 7, [ones_bf.b, ybf.b], [P1.b])
        for ot in range(8):
            PE(P2.t[:, :n], ones_bf.t[:], ysq.t[:, ot, :n], ot == 0, ot == 7, [ones_bf.b, ysq.b], [P2.b])
        TS(mu.t[:, :n], P1.t[:, :n], 1.0 / D, None, ALU.mult, None, [P1.b], [mu.b])
        TT(var.t[:, :n], mu.t[:, :n], mu.t[:, :n], ALU.mult, [mu.b], [var.b])
        STT(var.t[:, :n], P2.t[:, :n], 1.0 / D, var.t[:, :n], ALU.mult, ALU.subtract, [P2.b, var.b], [var.b])
        yield
        ACT(rstd.t[:, :n], var.t[:, :n], AF.Ln, [var.b], [rstd.b], bias=eps_t.t[:, 0:1])
        ACT(rstd.t[:, :n], rstd.t[:, :n], AF.Exp, [rstd.b], [rstd.b], scale=-0.5)
        yield
        for ot in range(8):
            TT(y.t[:, ot, :n], y.t[:, ot, :n], mu.t[:, :n], ALU.subtract, [yb[ot], mu.b], [yb[ot]], eng=sub_eng)
            TT(y.t[:, ot, :n], y.t[:, ot, :n], rstd.t[:, :n], ALU.mult, [yb[ot], rstd.b], [yb[ot]])
            MOD(y.t[:, ot, :n], y.t[:, ot, :n], pvc(gname, ot), pvc(bname, ot), [yb[ot], pv.b], [yb[ot]], eng="act")
            if ot % 2:
                yield

    def ln_feat(y, n, gname, bname, sc, eps_t=None):
        for _ in ln_gen(y, n, gname, bname, sc, eps_t):
            pass

    def ln_scratch(s, nmax, pref):
        return (s.sb(pref + "ybf", [128, 8, nmax], BF16), s.sb(pref + "ysq", [128, 8, nmax], BF16),
                s.sb(pref + "mu", [128, nmax]), s.sb(pref + "rstd", [128, nmax]), s.sb(pref + "var", [128, nmax]))

    def stage_O(l, last, pre=None):
        OB = 256
        NW = 2
        with Scope() as s:
            wo = s.sb("wo", [128, 8, D], BF16)
            woB = BIGDMAC(wo, wout_d[l], 8, 2)
            NBUF = 2 * NW
            mx = [s.sb("omx%d" % i, [128, 8, OB], BF16) for i in range(NBUF)]
            xs = [s.sb("oxs%d" % i, [128, 8, OB]) for i in range(NBUF)]
            xsb = [[Buf() for _ in range(8)] for i in range(NBUF)]
            scs = [ln_scratch(s, OB, "o%d" % i) for i in range(NW)]
            work = []
            for b in range(NB):
                for (t0b, nb, kind) in blocks:
                    if last and kind == 'c':
                        continue
                    for t0 in range(t0b, t0b + nb, OB):
                        work.append((b, t0, OB, 2 if kind == 'c' else b))

            def load(k):
                if k >= len(work):
                    return
                b, t0, n, ni = work[k]
                M, X, XB = mx[k % NBUF], xs[k % NBUF], xsb[k % NBUF]
                tk.dma("sp", M.t[:, :, :n], mix_d[b][:, :, t0:t0 + n], writes=[M.b])
                tk.dma("sp", X.t[:, :, :n], xs_d[b][:, :, t0:t0 + n], writes=XB)

            def blk(k):
                b, t0, n, ni = work[k]
                M, X, XB, sc = mx[k % NBUF], xs[k % NBUF], xsb[k % NBUF], scs[k % NW]
                load(k + NW)
                if pre is not None:
                    for _ in range(2):
                        try:
                            next(pre)
                        except StopIteration:
                            break
                yield
                for ot in range(8):
                    P = nb_()
                    for kc in range(8):
                        PE(P.t[:, :n], wo.t[:, kc, ot * 128:(ot + 1) * 128], M.t[:, kc, :n], kc == 0, kc == 7, [woB[kc], M.b], [P.b])
                    STT(X.t[:, ot, :n], P.t[:, :n], mcol(2, ot, ni), X.t[:, ot, :n], ALU.mult, ALU.add, [P.b, mod.b, XB[ot]], [XB[ot]])
                    if ot % 2:
                        yield
                for _ in ln_gen2(X, XB, n, "l1g", "l1b", sc, eps_lna):
                    yield
                tk.dma("pool", xs_d[b][:, :, t0:t0 + n], X.t[:, :, :n], reads=XB)
                yield

            for k in range(NW):
                load(k)
            run_window([blk(k) for k in range(len(work))], NW)
            if pre is not None:
                for _ in pre:
                    pass
        tk.barrier()

    def stage_FF(l, last, w1a, w1b, w1B):
        NBK = 256
        with Scope() as s:
            w2 = s.sb("w2", [128, 22, D], BF16)
            w2B = BIGDMAC(w2, w2_d[l], 22, 2)

            def w1s(kc, c0, c1):
                t_ = w1a if kc < 4 else w1b
                return t_.t[:, kc % 4, c0:c1], w1B[kc][1 if c0 >= DFF else 0]
            side = None
            if l + 1 < nlayers:
                side = None
            xs = [s.sb("fxs%d" % i, [128, 8, NBK]) for i in range(2)]
            xsb = [[Buf() for _ in range(8)] for i in range(2)]
            h2 = [s.sb("fh2%d" % i, [128, 8, NBK], BF16) for i in range(2)]
            act = s.sb("fact", [128, 22, NBK], BF16)
            sg = [s.sb("fsg%d" % i, [128, NBK]) for i in range(2)]
            sc = ln_scratch(s, NBK, "f")
            work = []
            for b in range(NB):
                for (t0b, nb, kind) in blocks:
                    if last and kind == 'c':
                        continue
                    for t0 in range(t0b, t0b + nb, NBK):
                        work.append((b, t0, 2 if kind == 'c' else b))
            n = NBK
            nw = len(work)

            def load_mod(k):
                b, t0, ni = work[k]
                X = xs[k % 2]
                H = h2[k % 2]
                XB = xsb[k % 2]
                tk.dma("sp", X.t[:, :, :], xs_d[b][:, :, t0:t0 + n], writes=XB)
                for kc in range(8):
                    MOD(H.t[:, kc, :], X.t[:, kc, :], mcol(4, kc, ni), mcol(3, kc, ni), [XB[kc], mod.b], [H.b], eng="pool")

            def fin(k):
                b, t0, ni = work[k]
                X = xs[k % 2]
                XB = xsb[k % 2]
                for _ in ln_gen2(X, XB, n, "l2g", "l2b", sc, eps_lna, sub_eng="pool"):
                    yield
                if last:
                    tk.dma("act", out_d[b][:, :, t0 - CTX:t0 - CTX + n], X.t[:, :, :], reads=XB)
                else:
                    tk.dma("act", xs_d[b][:, :, t0:t0 + n], X.t[:, :, :], reads=XB)
                yield

            load_mod(0)
            for k in range(nw):
                b, t0, ni = work[k]
                X = xs[k % 2]
                H = h2[k % 2]
                fq = fin(k - 1) if k >= 1 else None
                loaded = False
                for ft in range(22):
                    if fq is not None:
                        try:
                            next(fq)
                        except StopIteration:
                            fq = None
                    elif not loaded and k + 1 < nw:
                        load_mod(k + 1)
                        loaded = True
                    Pg = nb_()
                    Pu = nb_()
                    for kc in range(8):
                        wa, wb_ = w1s(kc, ft * 128, (ft + 1) * 128)
                        PE(Pg.t[:, :n], wa, H.t[:, kc, :], kc == 0, kc == 7, [wb_, H.b], [Pg.b])
                    for kc in range(8):
                        wa, wb_ = w1s(kc, DFF + ft * 128, DFF + (ft + 1) * 128)
                        PE(Pu.t[:, :n], wa, H.t[:, kc, :], kc == 0, kc == 7, [wb_, H.b], [Pu.b])
                    S = sg[ft % 2]
                    ACT(S.t[:, :], Pg.t[:, :n], AF.Silu, [Pg.b], [S.b])
                    TT(act.t[:, ft, :], S.t[:, :], Pu.t[:, :n], ALU.mult, [S.b, Pu.b], [act.b])
                if fq is not None:
                    for _ in fq:
                        pass
                if not loaded and k + 1 < nw:
                    load_mod(k + 1)
                if side is not None:
                    nside = 12 if k < nw - 1 else 100000
                    for _ in range(nside):
                        try:
                            next(side)
                        except StopIteration:
                            side = None
                            break
                for ot in range(8):
                    P = nb_()
                    for ft in range(22):
                        PE(P.t[:, :n], w2.t[:, ft, ot * 128:(ot + 1) * 128], act.t[:, ft, :], ft == 0, ft == 21, [w2B[ft], act.b], [P.b])
                    STT(X.t[:, ot, :], P.t[:, :n], mcol(5, ot, ni), X.t[:, ot, :], ALU.mult, ALU.add, [P.b, mod.b, xsb[k % 2][ot]], [xsb[k % 2][ot]])
            for _ in fin(nw - 1):
                pass
        tk.barrier()

    def run(name, fn, *a):
        with nc.named_scope(name):
            fn(*a)

    tk.barrier()
    for l in range(nlayers):
        last = (l == DEPTH - 1)
        with Scope() as so:
            set_layer(l)
            win = so.sb("win", [128, 8, INC], BF16)
            winB = BIGDMAC(win, win_d[l], 8, 1)
            if l == 0:
                run("cast", stage_cast_init)
            run("A%d" % l, stage_A, l, last, win, winB)
        if "stopA" in dbg:
            break
        if "skipR" not in dbg:
            run("R%d" % l, stage_R, l, last)
        if "stopR" in dbg:
            break
        if "skipP" not in dbg:
            run("P%d" % l, stage_P, l, last)
        with Scope() as so:
            clx = so.sb("clx", [128, 16, SEQ], BF16)
            slx = so.sb("slx", [128, 16, SEQ], BF16)
            clxB = BIGDMA(clx, cl_x_d, 16, 2, q="pool")
            slxB = BIGDMA(slx, sl_x_d, 16, 2, q="pool")
            if "skipG" not in dbg:
                run("G%d" % l, stage_G, l, last)
            if "skipF" not in dbg:
                run("F%d" % l, stage_F, l, last, clx, slx, clxB, slxB)
        if "stopM" in dbg:
            break
        with Scope() as so:
            w1a = so.sb("w1a", [128, 4, 2 * DFF], BF16)
            w1b = so.sb("w1b", [128, 4, 2 * DFF], BF16)
            w1B = [[Buf() for _ in range(2)] for _ in range(8)]

            def w1pre(l=l, w1a=w1a, w1b=w1b, w1B=w1B):
                for r in range(8):
                    dst = w1a if r < 4 else w1b
                    for h in range(2):
                        tk.dma("pool", dst.t[:, r % 4, h * DFF:(h + 1) * DFF], w1_d[l][:, r, h * DFF:(h + 1) * DFF], writes=[w1B[r][h]])
                        yield
            run("O%d" % l, stage_O, l, last, w1pre())
            run("FF%d" % l, stage_FF, l, last, w1a, w1b, w1B)
    tk.barrier()
    es_glob.close()
    return nc


def _pos_embed():
    rows = SEQ // GRID_W
    row, col = np.meshgrid(np.arange(rows, dtype=np.float32), np.arange(GRID_W, dtype=np.float32), indexing='ij')
    quarter = D // 4
    freqs = np.exp(np.float32(-math.log(10000.0)) * np.arange(quarter, dtype=np.float32) / np.float32(quarter)).astype(np.float32)

    def enc(p):
        ang = p.reshape(-1, 1).astype(np.float32) * freqs[None, :]
        return np.concatenate([np.sin(ang), np.cos(ang)], -1)
    return np.concatenate([enc(row), enc(col)], -1).astype(np.float32)


def _fm(a):
    sh = a.shape
    kc = sh[-2] // 128
    a = a.reshape(sh[:-2] + (kc, 128, sh[-1]))
    return np.ascontiguousarray(np.swapaxes(a, -3, -2))


def _cols(v):
    return np.ascontiguousarray(v.reshape(-1, 128).T)


def _consts():
    bf = ml_dtypes.bfloat16
    c = {}
    c["posT"] = _fm(np.ascontiguousarray(_pos_embed().T))
    for L, nm in ((SEQ, "x"), (CTX, "c")):
        idx = np.arange(L, dtype=np.int64)
        ang = 2.0 * np.pi * ((idx[:, None] * idx[None, :]) % L).astype(np.float64) / L
        cl = np.cos(ang) / math.sqrt(L)
        sl = -np.sin(ang) / math.sqrt(L)
        c["cl_" + nm] = _fm(cl.astype(np.float32)).astype(bf)
        c["sl_" + nm] = _fm(sl.astype(np.float32)).astype(bf)
        rc = np.zeros((2, 128, L), np.float32)
        for m in range(2):
            for hh in range(2):
                w = POOLW[2 * m + hh]
                lo = np.clip(idx - w // 2, 0, L)
                hi = np.clip(idx + w - w // 2, 0, L)
                rc[m, hh * 64:(hh + 1) * 64, :] = (1.0 / (hi - lo).astype(np.float32))[None, :]
        c["rc_" + nm] = np.ascontiguousarray(rc.transpose(1, 0, 2))
    i64 = np.arange(64)
    a64 = 2.0 * np.pi * ((i64[:, None] * i64[None, :]) % 64) / 64.0
    cs = np.concatenate([np.cos(a64), np.sin(a64)], 1) / 8.0
    c["cs64"] = np.concatenate([cs, cs], 0).astype(np.float32).astype(bf)
    ident = np.eye(128, dtype=np.float32)
    ho = np.zeros((128, 128), np.float32)
    ho[:64, :64] = 1
    ho[64:, 64:] = 1
    su = np.triu(np.ones((128, 128), np.float32), 1)
    sl_ = np.tril(np.ones((128, 128), np.float32), -1)
    ui = np.triu(np.ones((64, 64), np.float32), 0)
    li = np.tril(np.ones((64, 64), np.float32), 0)
    ui_st = np.concatenate([ui, ui], 0)
    li_st = np.concatenate([li, li], 0)
    cst = np.concatenate([ident, ho, np.tile(su, (1, 4)), np.tile(sl_, (1, 4)), np.tile(ui_st, (1, 4)), np.tile(li_st, (1, 4)),
                          np.zeros((128, 128), np.float32)], 1)
    assert cst.shape == (128, 1920)
    c["cst"] = cst
    return c


_CONSTS = None


def _prep_shared(inp):
    global _CONSTS
    if _CONSTS is None:
        _CONSTS = _consts()
    f = np.float32
    sh = dict(_CONSTS)
    sh["w_mod"] = _fm(np.asarray(inp["w_mod"], f))
    sh["b_mod"] = np.stack([_cols(np.asarray(inp["b_mod"], f)[l]) for l in range(DEPTH)])
    sh["w_in"] = _fm(np.asarray(inp["w_in"], f))
    sh["w_out"] = _fm(np.asarray(inp["w_out"], f))
    sh["ffn_w1"] = _fm(np.asarray(inp["ffn_w1"], f))
    sh["ffn_w2"] = _fm(np.asarray(inp["ffn_w2"], f))
    pv = np.zeros((DEPTH, 128, NPV), f)
    for l in range(DEPTH):
        def put(name, arr):
            arr = np.asarray(arr, f)
            pv[l, :, PV[name]:PV[name] + arr.shape[1]] = arr
        conv = np.asarray(inp["rkv_conv"], f)[l]
        put("conv", np.concatenate([_cols(conv[j]) for j in range(3)], 1))
        put("w0", np.concatenate([_cols(np.asarray(inp["decay_w0"], f)[l, d]) for d in range(2)], 1))
        put("a0", np.concatenate([_cols(np.asarray(inp["iclr_a0"], f)[l, d]) for d in range(2)], 1))
        put("kk", _cols(np.asarray(inp["k_k"], f)[l]))
        put("ka", _cols(np.asarray(inp["k_a"], f)[l]))
        put("rk", _cols(np.asarray(inp["r_k"], f)[l].reshape(-1)))
        put("gng", _cols(np.asarray(inp["gn_g"], f)[l]))
        put("gnb", _cols(np.asarray(inp["gn_b"], f)[l]))
        put("psc", _cols(np.asarray(inp["pool_scale"], f)[l]))
        put("fnb", _cols(np.asarray(inp["fnet_b"], f)[l]))
        put("l1g", _cols(np.asarray(inp["ln1_g"], f)[l]))
        put("l1b", _cols(np.asarray(inp["ln1_b"], f)[l]))
        put("l2g", _cols(np.asarray(inp["ln2_g"], f)[l]))
        put("l2b", _cols(np.asarray(inp["ln2_b"], f)[l]))
    sh["pv"] = pv
    sh["w2s"] = np.ascontiguousarray(np.asarray(inp["decay_w2"], f).reshape(DEPTH, 128, 256))
    sh["a2s"] = np.ascontiguousarray(np.asarray(inp["iclr_a2"], f).reshape(DEPTH, 128, 256))
    sh["g2"] = np.ascontiguousarray(np.asarray(inp["gate_g2"], f))

    def bd(w):
        o = np.zeros((DEPTH, 128, 2, 128), f)
        for m in range(2):
            o[:, 0:64, m, 0:64] = w[:, 2 * m]
            o[:, 64:128, m, 64:128] = w[:, 2 * m + 1]
        return o
    sh["poolbd"] = bd(np.asarray(inp["pool_w"], f))
    sh["fnetbd"] = bd(np.asarray(inp["fnet_w"], f))
    ws = np.asarray(inp["gmlp_ws"], f)
    sh["wsT"] = np.ascontiguousarray(ws.transpose(0, 3, 1, 2))
    bs = np.asarray(inp["gmlp_bs"], f)
    bsb = np.zeros((DEPTH, 128, 2, 128), f)
    for m in range(2):
        bsb[:, 0:64, m, :] = bs[:, 2 * m][:, None, :]
        bsb[:, 64:128, m, :] = bs[:, 2 * m + 1][:, None, :]
    sh["bsb"] = bsb
    sh["glg"] = np.ascontiguousarray(np.broadcast_to(np.asarray(inp["gmlp_ln_g"], f)[:, None, :], (DEPTH, 128, 256)))
    sh["glb"] = np.ascontiguousarray(np.broadcast_to(np.asarray(inp["gmlp_ln_b"], f)[:, None, :], (DEPTH, 128, 256)))
    return sh


def _prep_core(inp, sh, i):
    f = np.float32
    m = dict(sh)
    xb = np.asarray(inp["x"], f)[i * NB:(i + 1) * NB]
    m["xT"] = _fm(np.ascontiguousarray(np.swapaxes(xb, 1, 2)))
    cb = np.asarray(inp["ctx"], f)[i * NB:(i + 1) * NB]
    m["ctxT"] = _fm(np.ascontiguousarray(np.swapaxes(cb, 1, 2)))
    cc = np.zeros((4, D), f)
    cc[0:NB] = np.asarray(inp["c"], f)[i * NB:(i + 1) * NB]
    cc[2] = np.asarray(inp["c_ctx"], f)
    m["cT"] = _fm(np.ascontiguousarray(cc.T))
    return m


_NC = None


def kernel(**inputs):
    global _NC
    if _NC is None:
        _NC = build()
    sh = _prep_shared(inputs)
    in_maps = [_prep_core(inputs, sh, i) for i in range(NCORES)]
    res = run_bass_kernel_spmd(_NC, in_maps, core_ids=list(range(NCORES)))
    outs = []
    for i in range(NCORES):
        o = np.asarray(res.results[i]["out"])
        o = o.transpose(0, 2, 1, 3).reshape(NB, D, SEQ)
        outs.append(np.swapaxes(o, 1, 2))
    return np.ascontiguousarray(np.concatenate(outs, 0)).astype(np.float32)
```

```python
import math
from contextlib import ExitStack
import numpy as np
import ml_dtypes
import concourse.bass as bass
import concourse.mybir as mybir
from concourse.bass_utils import run_bass_kernel_spmd

F32 = mybir.dt.float32
BF16 = mybir.dt.bfloat16
AF = mybir.ActivationFunctionType
ALU = mybir.AluOpType
AX = mybir.AxisListType

D = 1024
NBATCH = 16
SEQ = 2048
CTX = 256
DEPTH = 2
NCORES = 8
NB = NBATCH // NCORES
T = CTX + SEQ
GRID_W = 64
GW = 256
HEAD = 64
INC = 2176
RWC = 1152
DFF = 2816
ALPHA = (2 * DEPTH) ** 0.25
LN_EPS = 1e-5
GN_EPS = 64e-5
POOLW = (2, 4, 8, 16)
CH = 64
SCAN_DT = F32
USE_F32R = True
SES = True

PV = {}
_c = 0
for _n, _k in (("conv", 18), ("w0", 4), ("a0", 4), ("kk", 2), ("ka", 2), ("rk", 2), ("gng", 2), ("gnb", 2),
               ("psc", 2), ("fnb", 2), ("l1g", 8), ("l1b", 8), ("l2g", 8), ("l2b", 8)):
    PV[_n] = _c
    _c += _k
NPV = _c


class Buf:
    __slots__ = ("name", "lw", "rd", "r32")

    def __init__(self, name=""):
        self.name = name
        self.lw = None
        self.rd = {}
        self.r32 = False


class TB:
    __slots__ = ("t", "b")

    def __init__(self, t, name=""):
        self.t = t
        self.b = Buf(name)


class Trk:
    NDMA = 8

    def __init__(self, nc, ses=True):
        self.nc = nc
        self.ses = ses
        self.eng = {"pe": nc.tensor, "act": nc.scalar, "dve": nc.vector, "pool": nc.gpsimd, "sp": nc.sync}
        self.sem = {k: nc.alloc_semaphore("s_" + k) for k in self.eng}
        self.cnt = {k: 0 for k in self.eng}
        self.dq = {}
        for q in ("sp", "pool", "act"):
            self.dq[q] = dict(sems=[nc.alloc_semaphore("d_%s%d" % (q, i)) for i in range(self.NDMA)], k=0)
        self.waited = {k: {} for k in self.eng}

    def _wait(self, e, key, semh, val):
        if self.waited[e].get(key, 0) >= val:
            return
        self.eng[e].wait_ge(semh, val)
        self.waited[e][key] = val

    def _need(self, e, ev):
        key, val = ev
        if key[0] == "e":
            if key[1] == e and (e == "pe" or e == "sp" or not self.ses):
                return
            self._wait(e, key, self.sem[key[1]], val)
        else:
            self._wait(e, key, self.dq[key[1]]["sems"][key[2]], val)

    def deps(self, e, reads, writes):
        for b in reads:
            if b.lw is not None:
                self._need(e, b.lw)
        for b in writes:
            if b.lw is not None:
                self._need(e, b.lw)
            for k, v in b.rd.items():
                self._need(e, (k, v))

    def _mark(self, ev, reads, writes):
        for b in reads:
            b.rd[ev[0]] = ev[1]
        for b in writes:
            b.lw = ev
            b.rd = {}

    def op(self, e, fn, reads=(), writes=()):
        self.deps(e, reads, writes)
        inst = fn(self.eng[e])
        self.cnt[e] += 1
        inst.then_inc(self.sem[e], 1)
        self._mark((("e", e), self.cnt[e]), reads, writes)
        return inst

    def dma(self, q, out, in_, reads=(), writes=()):
        d = self.dq[q]
        k = d["k"]
        slot = k % self.NDMA
        rnd = k // self.NDMA
        key = ("d", q, slot)
        if rnd > 0:
            self._wait(q, key, d["sems"][slot], 16 * rnd)
        self.deps(q, reads, writes)
        inst = self.eng[q].dma_start(out=out, in_=in_)
        inst.then_inc(d["sems"][slot], 16)
        d["k"] = k + 1
        self._mark((key, 16 * (rnd + 1)), reads, writes)
        return inst

    def barrier(self):
        for e in self.eng:
            for o in self.eng:
                if o != e and self.cnt[o] > 0:
                    self._wait(e, ("e", o), self.sem[o], self.cnt[o])
            for q, d in self.dq.items():
                k = d["k"]
                for slot in range(self.NDMA):
                    n = (k - slot + self.NDMA - 1) // self.NDMA
                    if n > 0:
                        self._wait(e, ("d", q, slot), d["sems"][slot], 16 * n)


def build(nlayers=DEPTH, dbg=()):
    nc = bass.Bass("TRN2", target_bir_lowering=False)
    tk = Trk(nc, ses=SES)

    def din(name, shape, dt=F32):
        return nc.dram_tensor(name, list(shape), dt, kind="ExternalInput").ap()

    def dscr(name, shape, dt=F32):
        kind = "ExternalOutput" if name in dbg else "Internal"
        return nc.dram_tensor(name, list(shape), dt, kind=kind).ap()

    xT_d = din("xT", [NB, 128, 8, SEQ])
    ctxT_d = din("ctxT", [NB, 128, 8, CTX])
    posT_d = din("posT", [128, 8, SEQ])
    cT_d = din("cT", [128, 8, 4])
    wmod_d = din("w_mod", [DEPTH, 128, 8, 6 * D])
    bmod_d = din("b_mod", [DEPTH, 128, 48])
    win_d = din("w_in", [DEPTH, 128, 8, INC])
    wout_d = din("w_out", [DEPTH, 128, 8, D])
    w1_d = din("ffn_w1", [DEPTH, 128, 8, 2 * DFF])
    w2_d = din("ffn_w2", [DEPTH, 128, 22, D])
    pv_d = din("pv", [DEPTH, 128, NPV])
    w2s_d = din("w2s", [DEPTH, 128, 256])
    a2s_d = din("a2s", [DEPTH, 128, 256])
    g2_d = din("g2", [DEPTH, 128, 256])
    poolbd_d = din("poolbd", [DEPTH, 128, 2, 128])
    fnetbd_d = din("fnetbd", [DEPTH, 128, 2, 128])
    wsT_d = din("wsT", [DEPTH, 128, 4, 128])
    bsb_d = din("bsb", [DEPTH, 128, 2, 128])
    glg_d = din("glg", [DEPTH, 128, 256])
    glb_d = din("glb", [DEPTH, 128, 256])
    cl_x_d = din("cl_x", [128, 16, SEQ], BF16)
    sl_x_d = din("sl_x", [128, 16, SEQ], BF16)
    cl_c_d = din("cl_c", [128, 2, CTX], BF16)
    sl_c_d = din("sl_c", [128, 2, CTX], BF16)
    cs64_d = din("cs64", [128, 128], BF16)
    rc_x_d = din("rc_x", [128, 2, SEQ])
    rc_c_d = din("rc_c", [128, 2, CTX])
    cst_d = din("cst", [128, 1920])
    out_d = nc.dram_tensor("out", [NB, 128, 8, SEQ], F32, kind="ExternalOutput").ap()

    xs_d = dscr("xs_s", [NB, 128, 8, T])
    px_d = dscr("px_s", [NB, 17, 128, T])
    yf_d = dscr("yf_s", [NB, 2, 128, T])
    mix_d = dscr("mix_s", [NB, 128, 8, T], BF16)
    winb_d = woutb_d = w1b_d = w2b_d = None

    es_glob = ExitStack()
    uid = [0]

    class Scope:
        def __init__(self):
            self.es = ExitStack()

        def __enter__(self):
            self.es.__enter__()
            return self

        def __exit__(self, *a):
            return self.es.__exit__(*a)

        def sb(self, name, shape, dt=F32):
            uid[0] += 1
            t = self.es.enter_context(nc.sbuf_tensor("%s_%d" % (name, uid[0]), list(shape), dt))
            return TB(t, name)

    def gsb(name, shape, dt=F32):
        t = es_glob.enter_context(nc.sbuf_tensor("g_" + name, list(shape), dt))
        return TB(t, name)

    banks = [TB(es_glob.enter_context(nc.psum_tensor("bank%d" % i, [128, 512], F32)), "bank%d" % i) for i in range(8)]
    bank_i = [0]

    def nb_():
        b = banks[bank_i[0] % 8]
        bank_i[0] += 1
        return b

    rr = [0]

    def ee():
        rr[0] += 1
        return "dve" if rr[0] % 2 else "act"

    F32R = mybir.dt.float32r

    def PE(out, lhsT, rhs, start, stop, reads, writes):
        if USE_F32R and lhsT.dtype == F32 and all(b.r32 for b in reads):
            lhsT = lhsT.bitcast(F32R)
            rhs = rhs.bitcast(F32R)
        tk.op("pe", lambda e: e.matmul(out, lhsT=lhsT, rhs=rhs, start=start, stop=stop), reads, writes)

    def PET(out, in_, reads, writes):
        kp = in_.shape[0]
        if in_.dtype == BF16:
            tk.op("pe", lambda e: e.transpose(out, in_, ident_bf.t[0:kp, 0:kp]), list(reads) + [ident_bf.b], writes)
        else:
            tk.op("pe", lambda e: e.transpose(out, in_, ident.t[0:kp, 0:kp]), list(reads) + [cstb], writes)

    def RO(out, writes):
        if USE_F32R and out.dtype == F32 and any(b.r32 for b in writes):
            return out.bitcast(F32R)
        return out

    def CP(out, in_, reads, writes, eng=None):
        out = RO(out, writes)
        eng = eng or ee()
        if eng == "act":
            tk.op("act", lambda e: e.copy(out=out, in_=in_), reads, writes)
        else:
            tk.op(eng, lambda e: e.tensor_copy(out=out, in_=in_), reads, writes)

    def TT(out, in0, in1, op, reads, writes, eng="dve"):
        out = RO(out, writes)
        tk.op(eng, lambda e: e.tensor_tensor(out=out, in0=in0, in1=in1, op=op), reads, writes)

    def TS(out, in0, s1, s2, op0, op1, reads, writes, eng="dve"):
        out = RO(out, writes)
        if op1 is None:
            tk.op(eng, lambda e: e.tensor_scalar(out=out, in0=in0, scalar1=s1, scalar2=None, op0=op0), reads, writes)
        else:
            tk.op(eng, lambda e: e.tensor_scalar(out=out, in0=in0, scalar1=s1, scalar2=s2, op0=op0, op1=op1), reads, writes)

    def STT(out, in0, scalar, in1, op0, op1, reads, writes):
        out = RO(out, writes)
        tk.op("dve", lambda e: e.scalar_tensor_tensor(out=out, in0=in0, scalar=scalar, in1=in1, op0=op0, op1=op1), reads, writes)

    def ACT(out, in_, func, reads, writes, bias=0.0, scale=1.0):
        out = RO(out, writes)
        tk.op("act", lambda e: e.activation(out=out, in_=in_, func=func, bias=bias, scale=scale), reads, writes)

    def MOD(out, in_, s1, s2, reads, writes, eng=None):
        eng = eng or ee()
        if eng == "act":
            ACT(out, in_, AF.Identity, reads, writes, bias=s2, scale=s1)
        else:
            TS(out, in_, s1, s2, ALU.mult, ALU.add, reads, writes, eng=eng)

    def BIGDMA(dst_tb, src_ap, nrow, step, q="sp"):
        rowB = [None] * nrow
        for r in range(0, nrow, step):
            r2 = min(nrow, r + step)
            pb_ = Buf()
            for rr_ in range(r, r2):
                rowB[rr_] = pb_
            tk.dma(q, dst_tb.t[:, r:r2, :], src_ap[:, r:r2, :], writes=[pb_])
        return rowB

    def BIGDMAC(dst_tb, src_ap, nrow, step):
        rowB = [None] * nrow
        for r in range(0, nrow, step):
            r2 = min(nrow, r + step)
            pb_ = Buf()
            for rr_ in range(r, r2):
                rowB[rr_] = pb_
            tk.dma("pool", dst_tb.t[:, r:r2, :], src_ap[:, r:r2, :], writes=[pb_])
        return rowB

    def MEMSET(ap, val, writes, eng="pool"):
        ap = RO(ap, writes)
        tk.op(eng, lambda e: e.memset(ap, val), (), writes)

    cst = gsb("cst", [128, 1920])
    cstb = cst.b
    tk.dma("sp", cst.t[:], cst_d[:, :], writes=[cstb])
    ident = TB(cst.t[:, 0:128])
    ident.b = cstb
    headones = cst.t[:, 128:256]
    SU4 = cst.t[:, 256:768].rearrange("p (a b) -> p a b", a=4)
    SL4 = cst.t[:, 768:1280].rearrange("p (a b) -> p a b", a=4)
    UI4 = cst.t[:, 1280:1536].rearrange("p (a b) -> p a b", a=4)
    LI4 = cst.t[:, 1536:1792].rearrange("p (a b) -> p a b", a=4)
    ident4 = None
    ident_bf = gsb("ident_bf", [128, 128], BF16)
    CP(ident_bf.t[:], cst.t[:, 0:128], [cstb], [ident_bf.b], eng="dve")
    ones_bf = gsb("ones_bf", [128, 128], BF16)
    MEMSET(ones_bf.t[:], 1.0, [ones_bf.b])
    id4 = gsb("id4", [128, 4, 128])
    for a in range(4):
        CP(id4.t[:, a, :], cst.t[:, 0:128], [cstb], [id4.b], eng="dve")
    scanmask = gsb("scanmask", [128, 256])
    MEMSET(scanmask.t[:], 1.0, [scanmask.b])
    for c in range(4):
        MEMSET(scanmask.t[:, c * 64:c * 64 + 1], 0.0, [scanmask.b])
    eps_kk = gsb("eps_kk", [128, 1])
    MEMSET(eps_kk.t[:], 1e-20, [eps_kk.b])
    eps_gn = gsb("eps_gn", [128, 1])
    MEMSET(eps_gn.t[:], GN_EPS, [eps_gn.b])
    eps_ln = gsb("eps_ln", [128, 1])
    MEMSET(eps_ln.t[:], LN_EPS, [eps_ln.b])
    eps_lna = gsb("eps_lna", [128, 1])
    MEMSET(eps_lna.t[:], LN_EPS / (ALPHA * ALPHA), [eps_lna.b])
    modL = [gsb("mod%d" % i, [128, 48, 3]) for i in range(DEPTH)]
    pvL = [gsb("pv%d" % i, [128, NPV]) for i in range(DEPTH)]
    omkaL = [gsb("omka%d" % i, [128, 2]) for i in range(DEPTH)]
    mod = TB(modL[0].t)
    pv = TB(pvL[0].t)
    omka = TB(omkaL[0].t)

    def set_layer(l):
        mod.t, mod.b = modL[l].t, modL[l].b
        pv.t, pv.b = pvL[l].t, pvL[l].b
        omka.t, omka.b = omkaL[l].t, omkaL[l].b
    set_layer(0)
    sil = gsb("sil", [128, 8, 4])
    cTt = gsb("cTt", [128, 8, 4])
    tk.dma("sp", cTt.t[:], cT_d[:, :, :], writes=[cTt.b])
    ACT(sil.t[:], cTt.t[:], AF.Silu, [cTt.b], [sil.b])

    def pvc(name, j=0):
        c = PV[name] + j
        return pv.t[:, c:c + 1]

    blocks = [(0, CTX, 'c')] + [(CTX + i * 512, 512, 'x') for i in range(SEQ // 512)]

    def run_window(gens, width):
        gens = list(gens)
        active = []
        while gens or active:
            while gens and len(active) < width:
                active.append(gens.pop(0))
            for t in list(active):
                try:
                    next(t)
                except StopIteration:
                    active.remove(t)

    def run_tasks(tasks):
        tasks = [t for t in tasks if t is not None]
        while tasks:
            for t in list(tasks):
                try:
                    next(t)
                except StopIteration:
                    tasks.remove(t)

    def cast_gen(l, s, chunk, nbuf, engs, pref, ql="sp", qs="pool"):
        fb = [s.sb("%sf%d" % (pref, i), [128, chunk]) for i in range(nbuf)]
        bb = [s.sb("%sb%d" % (pref, i), [128, chunk], BF16) for i in range(nbuf)]

        def gen():
            allw = ((win_d, winb_d, 8, INC), (wout_d, woutb_d, 8, D), (w1_d, w1b_d, 8, 2 * DFF), (w2_d, w2b_d, 22, D))
            items = []
            for src, dst, rows, cols in allw:
                sv = src[l].rearrange("p a b -> p (a b)")
                dv = dst[l].rearrange("p a b -> p (a b)")
                tot = rows * cols
                for o in range(0, tot, chunk):
                    items.append((sv, dv, o, min(chunk, tot - o)))

            def load(i):
                if i < len(items):
                    sv, dv, o, n = items[i]
                    tk.dma(ql, fb[i % nbuf].t[:, :n], sv[:, o:o + n], writes=[fb[i % nbuf].b])
            for i in range(nbuf - 1):
                load(i)
            for i, (sv, dv, o, n) in enumerate(items):
                f = fb[i % nbuf]
                b_ = bb[i % nbuf]
                e = engs[i % len(engs)]
                CP(b_.t[:, :n], f.t[:, :n], [f.b], [b_.b], eng=e)
                tk.dma(qs, dv[:, o:o + n], b_.t[:, :n], reads=[b_.b])
                load(i + nbuf - 1)
                yield
        return gen()

    def stage_cast_init():
        with Scope() as s:
            xb = [s.sb("ixb%d" % i, [128, 8, 512]) for i in range(2)]
            pb = [s.sb("ipb%d" % i, [128, 8, 512]) for i in range(2)]

            def init_gen():
                it = 0
                for b in range(NB):
                    tk.dma("pool", xs_d[b][:, :, 0:CTX], ctxT_d[b])
                    for i in range(SEQ // 512):
                        X = xb[it % 2]
                        P = pb[it % 2]
                        it += 1
                        tk.dma("sp", X.t[:], xT_d[b][:, :, i * 512:(i + 1) * 512], writes=[X.b])
                        tk.dma("sp", P.t[:], posT_d[:, :, i * 512:(i + 1) * 512], writes=[P.b])
                        yield
                        TT(X.t[:], X.t[:], P.t[:], ALU.add, [X.b, P.b], [X.b], eng="pool")
                        tk.dma("pool", xs_d[b][:, :, CTX + i * 512:CTX + (i + 1) * 512], X.t[:], reads=[X.b])
                        yield
            shared = ([s.sb("wm%d" % i, [128, 8, 512]) for i in range(2)], s.sb("mrow", [4, 6 * D]))

            def mods():
                for l in range(1):
                    for _ in mod_gen(l, s, shared):
                        yield
            run_tasks([init_gen(), mods()])
        tk.barrier()

    def mod_gen(l, s, shared):
        mod_, pv_l, omka_ = modL[l], pvL[l], omkaL[l]
        wm, mrow = shared
        bm = s.sb("bm%d" % l, [128, 48])

        def gen():
            tk.dma("sp", bm.t[:], bmod_d[l], writes=[bm.b])
            tk.dma("sp", pv_l.t[:], pv_d[l], writes=[pv_l.b])
            for g in range(12):
                W = wm[g % 2]
                tk.dma("sp", W.t[:], wmod_d[l][:, :, g * 512:(g + 1) * 512], writes=[W.b])
                P = nb_()
                for kc in range(8):
                    PE(P.t[0:4, :], sil.t[:, kc, :], W.t[:, kc, :], kc == 0, kc == 7, [W.b, sil.b], [P.b])
                CP(mrow.t[:, g * 512:(g + 1) * 512], P.t[0:4, :], [P.b], [mrow.b], eng="dve")
                yield
            PT = nb_()
            ptv = PT.t[:, 0:192].rearrange("p (a b) -> p a b", b=4)
            for j in range(48):
                PET(ptv[:, j, :], mrow.t[:, j * 128:(j + 1) * 128], [mrow.b], [PT.b])
            TT(mod_.t[:, :, :], ptv[:, :, 0:3], bm.t[:, :].unsqueeze(2).to_broadcast([128, 48, 3]), ALU.add, [PT.b, bm.b], [mod_.b])
            for w in (1, 4):
                TS(mod_.t[:, w * 8:(w + 1) * 8, :], mod_.t[:, w * 8:(w + 1) * 8, :], 1.0, None, ALU.add, None, [mod_.b], [mod_.b])
            TS(omka_.t[:], pv_l.t[:, PV["ka"]:PV["ka"] + 2], -1.0, 1.0, ALU.mult, ALU.add, [pv_l.b], [omka_.b])
            for w in (2, 5):
                TS(mod_.t[:, w * 8:(w + 1) * 8, :], mod_.t[:, w * 8:(w + 1) * 8, :], 1.0 / ALPHA, None, ALU.mult, None, [mod_.b], [mod_.b])
            yield
        return gen()

    def mcol(which, kc, ni):
        return mod.t[:, which * 8 + kc, ni:ni + 1]

    def stage_A(l, last, win, winB):
        with Scope() as s:
            xsb = [s.sb("axs%d" % i, [128, 8, 512]) for i in range(2)]
            hx = [s.sb("ahx%d" % i, [128, 8, 512], BF16) for i in range(2)]
            hxB = [[Buf() for _ in range(8)] for i in range(2)]
            stg = [s.sb("astg%d" % i, [128, 512]) for i in range(4)]
            work = [(b, t0, n, kind) for b in range(NB) for (t0, n, kind) in blocks]
            ie = 0

            def load(k):
                b, t0, n, kind = work[k]
                X = xsb[k % 2]
                tk.dma("sp", X.t[:, :, :n], xs_d[b][:, :, t0:t0 + n], writes=[X.b])

            def domod(k):
                b, t0, n, kind = work[k]
                ni = 2 if kind == 'c' else b
                X, H = xsb[k % 2], hx[k % 2]
                for kc in range(8):
                    MOD(H.t[:, kc, :n], X.t[:, kc, :n], mcol(1, kc, ni), mcol(0, kc, ni), [X.b, mod.b], [hxB[k % 2][kc]])

            load(0)
            domod(0)
            for k in range(len(work)):
                b, t0, n, kind = work[k]
                H = hx[k % 2]
                nt = 9 if (last and kind == 'c') else 17
                for ct in range(nt):
                    if ct == 1 and k + 1 < len(work):
                        load(k + 1)
                    if ct == 7 and k + 1 < len(work):
                        domod(k + 1)
                    P = nb_()
                    for kc in range(8):
                        PE(P.t[:, :n], win.t[:, kc, ct * 128:(ct + 1) * 128], H.t[:, kc, :n], kc == 0, kc == 7,
                           [winB[kc], hxB[k % 2][kc]], [P.b])
                    S = stg[ie % 4]
                    ie += 1
                    CP(S.t[:, :n], P.t[:, :n], [P.b], [S.b])
                    tk.dma("pool", px_d[b, ct][:, t0:t0 + n], S.t[:, :n], reads=[S.b])
        tk.barrier()

    def stage_R(l, last):
        rwkv_pass(l, last)
        tk.barrier()

    def rwkv_pass(l, last):
        BK = 256
        NQ = BK // CH
        xdt = BF16
        base = [(0, CTX, 'c')] + [(CTX + i * BK, BK, 'x') for i in range(SEQ // BK)]
        blks = []
        for d_ in range(2):
            for b_ in range(NB):
                bl = base if d_ == 0 else [base[0]] + base[:0:-1]
                for i_, (t0_, n_, k_) in enumerate(bl):
                    blks.append((t0_, n_, k_, b_, d_, i_ == 0))
        yfd_b = [Buf("yfd%d" % i) for i in range(NB)]
        nblk = len(blks)
        with Scope() as s:
            n = BK
            w2s = s.sb("w2s", [128, 256])
            a2s = s.sb("a2s", [128, 256])
            g2 = s.sb("g2", [128, 256])
            tk.dma("sp", w2s.t[:], w2s_d[l], writes=[w2s.b])
            tk.dma("sp", a2s.t[:], a2s_d[l], writes=[a2s.b])
            tk.dma("sp", g2.t[:], g2_d[l], writes=[g2.b])
            zp = [[s.sb("zp%d_%d" % (p, i), [128, BK + 2]) for i in range(6)] for p in range(2)]
            lo6 = [s.sb("lo6_%d" % p, [128, BK]) for p in range(2)]
            lo7 = [s.sb("lo7_%d" % p, [128, BK]) for p in range(3)]
            lo8 = [s.sb("lo8_%d" % p, [128, BK]) for p in range(3)]
            cv = [[s.sb("cv%d_%d" % (p, i), [128, BK]) for i in range(6)] for p in range(3)]
            aaT = [[s.sb("aa%d_%d" % (p, m), [128, BK]) for m in range(2)] for p in range(3)]
            names = ["tdw", "lw", "Pp", "Qq", "Gx", "Gi", "eG", "eGn", "eGx", "eH", "kraw", "sq",
                     "rs", "kkn", "tmp", "kd", "bq"]
            tbM = [{nm: s.sb("r%d_%s" % (mm, nm), [128, BK]) for nm in names} for mm in range(2)]
            tbM[1]["tdw"] = tbM[0]["tdw"]
            tb = tbM[0]
            RpT = [[s.sb("Rp%d_%d" % (p, m), [128, BK]) for m in range(2)] for p in range(2)]
            RpbT = [[s.sb("Rpb%d_%d" % (p, m), [128, BK], BF16) for m in range(2)] for p in range(2)]
            etotT = [[s.sb("etot%d_%d" % (p, m), [128, NQ]) for m in range(2)] for p in range(2)]
            DgT = [[s.sb("Dg%d_%d" % (p, m), [128, NQ, 128]) for m in range(2)] for p in range(2)]
            exn = ["Ab", "Bb", "Kb", "Bh", "Kh", "Vb"]
            exT = [[{nm: s.sb("x%d%d_%s" % (p, m, nm), [128, NQ, 2, 64], xdt) for nm in exn} for m in range(2)] for p in range(2)]
            for p in range(2):
                for m in range(2):
                    for nm in exn:
                        MEMSET(exT[p][m][nm].t[:], 0.0, [exT[p][m][nm].b], eng="pool" if (m + p) % 2 else "dve")
            qn = ["X0", "X1", "XT0", "XT1", "T0", "T1", "W1", "W2", "BhT", "KhT", "P1T", "P2T", ]
            qT = [{nm: s.sb("q%d_%s" % (m, nm), [128, NQ, 128], xdt) for nm in qn} for m in range(2)]
            MbrT = [s.sb("Mbr%d" % m, [128, NQ, 64], xdt) for m in range(2)]
            MkrT = [s.sb("Mkr%d" % m, [128, NQ, 64], xdt) for m in range(2)]
            QG = [[dict(Q1=s.sb("Q1_%d%d" % (p, m), [128, NQ, 64], xdt), Q2=s.sb("Q2_%d%d" % (p, m), [128, NQ, 64], xdt),
                        G1=s.sb("G1_%d%d" % (p, m), [128, NQ, 128], xdt), G2=s.sb("G2_%d%d" % (p, m), [128, NQ, 128], xdt),
                        Vt=s.sb("Vt_%d%d" % (p, m), [128, NQ, 128], xdt)) for m in range(2)] for p in range(2)]
            St = [[s.sb("St%d%d" % (m, i), [128, 128], xdt) for i in range(2)] for m in range(2)]
            sti = [0, 0]
            for m in range(2):
                MEMSET(St[m][0].t[:], 0.0, [St[m][0].b])
            ybT = [[s.sb("yb%d_%d" % (p, m), [128, BK]) for m in range(2)] for p in range(2)]
            if True:
                yfT = [[s.sb("yf%d_%d" % (p, m), [128, BK]) for m in range(2)] for p in range(3)]
                ro = {nm: s.sb("ro_" + nm, [128, BK]) for nm in ["ys", "sq", "mu", "var", "rstd", "bon", "sg", "af", "t2"]}
                ob = [s.sb("ob%d" % m, [128, BK], BF16) for m in range(2)]
            PYb = [banks[6], banks[7]]

            def nbk():
                b_ = banks[bank_i[0] % 6]
                bank_i[0] += 1
                return b_

            def is_readout(k):
                return (blks[k][4] == 1) and not (last and blks[k][2] == 'c')

            def prep(k):
                t0, _, kind, b, d, _f = blks[k]
                p2, p3 = k % 2, k % 3
                s0, L = (0, CTX) if kind == 'c' else (CTX, SEQ)
                first = (t0 == s0)
                lastb = (t0 + n == s0 + L)
                for ct in range(6):
                    Z = zp[p2][ct]
                    lo_ = t0 - (0 if first else 1)
                    hi_ = t0 + n + (0 if lastb else 1)
                    o0 = 1 - (t0 - lo_)
                    tk.dma("sp", Z.t[:, o0:o0 + (hi_ - lo_)], px_d[b, ct][:, lo_:hi_], writes=[Z.b])
                    if first:
                        MEMSET(Z.t[:, 0:1], 0.0, [Z.b])
                    if lastb:
                        MEMSET(Z.t[:, n + 1:n + 2], 0.0, [Z.b])
                L6, L7, L8 = lo6[p2], lo7[p3], lo8[p3]
                tk.dma("sp", L6.t[:, :], px_d[b, 6][:, t0:t0 + n], writes=[L6.b])
                tk.dma("sp", L7.t[:, :], px_d[b, 7][:, t0:t0 + n], writes=[L7.b])
                if is_readout(k):
                    tk.dma("sp", L8.t[:, :], px_d[b, 8][:, t0:t0 + n], writes=[L8.b])
                    for m in range(2):
                        tk.dma("sp", yfT[p3][m].t[:, :], yf_d[b, m][:, t0:t0 + n], reads=[yfd_b[b]], writes=[yfT[p3][m].b])
                yield
                for ct in range(6):
                    Z = zp[p2][ct]
                    C = cv[p3][ct]
                    ACT(C.t[:, :], Z.t[:, 1:n + 1], AF.Identity, [Z.b, pv.b], [C.b], scale=pvc("conv", 6 + ct))
                    STT(C.t[:, :], Z.t[:, 0:n], pvc("conv", ct), C.t[:, :], ALU.mult, ALU.add, [Z.b, pv.b, C.b], [C.b])
                    STT(C.t[:, :], Z.t[:, 2:n + 2], pvc("conv", 12 + ct), C.t[:, :], ALU.mult, ALU.add, [Z.b, pv.b, C.b], [C.b])
                    yield
                hs = slice(d * 64, (d + 1) * 64)
                ACT(tb["tdw"].t[hs, :], L6.t[hs, :], AF.Tanh, [L6.b], [tb["tdw"].b])

                def pm(m, tb):
                    rr_, kk_, vv_ = cv[p3][m], cv[p3][2 + m], cv[p3][4 + m]
                    ms = slice(m * 128, (m + 1) * 128)
                    aa = aaT[p3][m]
                    Rp, Rpb, etot, ex = RpT[p2][m], RpbT[p2][m], etotT[p2][m], exT[p2][m]
                    P = nbk()
                    PE(P.t[:, :n], w2s.t[hs, ms], tb["tdw"].t[hs, :], True, True, [w2s.b, tb["tdw"].b], [P.b])
                    ACT(tb["lw"].t[:, :], P.t[:, :n], AF.Sigmoid, [P.b, pv.b], [tb["lw"].b], bias=pvc("w0", d * 2 + m))
                    ACT(tb["lw"].t[:, :], tb["lw"].t[:, :], AF.Identity, [tb["lw"].b], [tb["lw"].b], scale=-math.exp(-0.5))
                    P = nbk()
                    PE(P.t[:, :n], a2s.t[hs, ms], L7.t[hs, :], True, True, [a2s.b, L7.b], [P.b])
                    ACT(aa.t[:, :], P.t[:, :n], AF.Sigmoid, [P.b, pv.b], [aa.b], bias=pvc("a0", d * 2 + m))
                    yield
                    lw = tb["lw"]
                    Pp, Qq, Gx, Gi = tb["Pp"], tb["Qq"], tb["Gx"], tb["Gi"]
                    tk.op("dve", lambda e: e.tensor_tensor_scan(out=Pp.t[:, :], data0=scanmask.t[:, :n], data1=lw.t[:, :],
                                                                 initial=0.0, op0=ALU.mult, op1=ALU.add),
                          [scanmask.b, lw.b], [Pp.b])
                    P3 = Pp.t[:, :].rearrange("p (c j) -> p c j", j=CH)
                    tot_b = P3[:, :, CH - 1:CH].to_broadcast([128, NQ, CH])
                    TT(Qq.t[:, :].rearrange("p (c j) -> p c j", j=CH), tot_b, P3, ALU.subtract, [Pp.b], [Qq.b])
                    TT(Gx.t[:, :], Pp.t[:, :], lw.t[:, :], ALU.subtract, [Pp.b, lw.b], [Gx.b], eng="pool")
                    if d == 0:
                        Gin, Gex, Hh = Pp, Gx, Qq
                    else:
                        TT(Gi.t[:, :], Qq.t[:, :], lw.t[:, :], ALU.add, [Qq.b, lw.b], [Gi.b], eng="pool")
                        Gin, Gex, Hh = Gi, Qq, Gx
                    yield
                    ACT(tb["eG"].t[:, :], Gin.t[:, :], AF.Exp, [Gin.b], [tb["eG"].b])
                    ACT(tb["eGn"].t[:, :], Gin.t[:, :], AF.Exp, [Gin.b], [tb["eGn"].b], scale=-1.0)
                    ACT(tb["eGx"].t[:, :], Gex.t[:, :], AF.Exp, [Gex.b], [tb["eGx"].b])
                    ACT(tb["eH"].t[:, :], Hh.t[:, :], AF.Exp, [Hh.b], [tb["eH"].b])
                    ACT(etot.t[:, :], P3[:, :, CH - 1], AF.Exp, [Pp.b], [etot.b])
                    TT(DgT[p2][m].t[:], id4.t[:], etot.t[:, :].unsqueeze(2).to_broadcast([128, NQ, 128]), ALU.mult,
                       [id4.b, etot.b], [DgT[p2][m].b], eng="pool")
                    yield
                    kraw, sq, rs, kkn = tb["kraw"], tb["sq"], tb["rs"], tb["kkn"]
                    ACT(kraw.t[:, :], kk_.t[:, :], AF.Identity, [kk_.b, pv.b], [kraw.b], scale=pvc("kk", m))
                    ACT(sq.t[:, :], kk_.t[:, :], AF.Square, [kk_.b, pv.b], [sq.b], scale=pvc("kk", m))
                    P = nbk()
                    PE(P.t[:, :n], headones, sq.t[:, :], True, True, [cstb, sq.b], [P.b])
                    ACT(rs.t[:, :], P.t[:, :n], AF.Ln, [P.b], [rs.b], bias=eps_kk.t[:, 0:1])
                    ACT(rs.t[:, :], rs.t[:, :], AF.Exp, [rs.b], [rs.b], scale=-0.5)
                    TT(kkn.t[:, :], kraw.t[:, :], rs.t[:, :], ALU.mult, [kraw.b, rs.b], [kkn.b])
                    yield
                    tmp, kd, bq = tb["tmp"], tb["kd"], tb["bq"]
                    TS(tmp.t[:, :], aa.t[:, :], pvc("ka", m), omka.t[:, m:m + 1], ALU.mult, ALU.add, [aa.b, pv.b, omka.b], [tmp.b], eng="pool")
                    TT(kd.t[:, :], kk_.t[:, :], tmp.t[:, :], ALU.mult, [kk_.b, tmp.b], [kd.b], eng="pool")
                    TT(bq.t[:, :], kkn.t[:, :], aa.t[:, :], ALU.mult, [kkn.b, aa.b], [bq.b])
                    TT(Rp.t[:, :], rr_.t[:, :], tb["eG"].t[:, :], ALU.mult, [rr_.b, tb["eG"].b], [Rp.b])
                    CP(Rpb.t[:, :], Rp.t[:, :], [Rp.b], [Rpb.b], eng="act")
                    yield

                    def v3(tbuf, hh):
                        return tbuf.t[hh * 64:(hh + 1) * 64, :].rearrange("p (c j) -> p c j", j=CH)

                    for hh in range(2):
                        def xo(nm):
                            return ex[nm].t[hh * 64:(hh + 1) * 64, :, hh, :]
                        STT(xo("Ab"), v3(kkn, hh), -1.0, v3(tb["eGx"], hh), ALU.mult, ALU.mult, [kkn.b, tb["eGx"].b], [ex["Ab"].b])
                        TT(xo("Bb"), v3(bq, hh), v3(tb["eGn"], hh), ALU.mult, [bq.b, tb["eGn"].b], [ex["Bb"].b])
                        TT(xo("Kb"), v3(kd, hh), v3(tb["eGn"], hh), ALU.mult, [kd.b, tb["eGn"].b], [ex["Kb"].b], eng="pool")
                        TT(xo("Bh"), v3(bq, hh), v3(tb["eH"], hh), ALU.mult, [bq.b, tb["eH"].b], [ex["Bh"].b])
                        TT(xo("Kh"), v3(kd, hh), v3(tb["eH"], hh), ALU.mult, [kd.b, tb["eH"].b], [ex["Kh"].b], eng="pool")
                        CP(xo("Vb"), v3(vv_, hh), [vv_.b], [ex["Vb"].b], eng="act")
                        yield

                alive = [pm(0, tbM[0]), pm(1, tbM[1])]
                while alive:
                    for g in list(alive):
                        try:
                            next(g)
                        except StopIteration:
                            alive.remove(g)
                    yield

            def chain(k, m):
                p2 = k % 2
                d = blks[k][4]
                maskN, maskNT, maskM = (SU4, SL4, UI4) if d == 0 else (SL4, SU4, LI4)
                Rp, Rpb, etot, ex = RpT[p2][m], RpbT[p2][m], etotT[p2][m], exT[p2][m]
                Dg = DgT[p2][m]
                q = qT[m]
                Mbr, Mkr = MbrT[m], MkrT[m]
                o = QG[p2][m]
                Q1, Q2, G1, G2, Vt = o["Q1"], o["Q2"], o["G1"], o["G2"], o["Vt"]

                def xc(nm, c):
                    return ex[nm].t[:, c, :, :].rearrange("p a b -> p (a b)")

                def quad_mm(lhs_fn, rhs_fn, ncol, reads):
                    P = nbk()
                    pv_ = P.t[:, 0:4 * ncol].rearrange("p (a b) -> p a b", a=4)
                    for c in range(NQ):
                        PE(pv_[:, c, :], lhs_fn(c), rhs_fn(c), True, True, reads, [P.b])
                    return P, pv_

                X0, X1, XT0, XT1, T0, T1 = q["X0"], q["X1"], q["XT0"], q["XT1"], q["T0"], q["T1"]
                W1, W2, BhT, KhT, P1T, P2T = q["W1"], q["W2"], q["BhT"], q["KhT"], q["P1T"], q["P2T"]
                P, pv_ = quad_mm(lambda c: xc("Bb", c), lambda c: xc("Ab", c), 128, [ex["Bb"].b, ex["Ab"].b])
                TT(X0.t[:], pv_, maskN, ALU.mult, [P.b, cstb], [X0.b])
                TT(T0.t[:], X0.t[:], id4.t[:], ALU.add, [X0.b, id4.b], [T0.b], eng="pool")
                yield
                P, pv_ = quad_mm(lambda c: xc("Ab", c), lambda c: xc("Bb", c), 128, [ex["Bb"].b, ex["Ab"].b])
                TT(XT0.t[:], pv_, maskNT, ALU.mult, [P.b, cstb], [XT0.b])
                yield
                Xc, XTc, Tc = X0, XT0, T0
                Xn, XTn, Tn = X1, XT1, T1
                side = [
                    ("mm", lambda c: xc("Bb", c), lambda c: Rpb.t[:, c * 64:(c + 1) * 64], 64, [ex["Bb"].b, Rpb.b], Mbr, maskM),
                    ("tr", "Ab", W1), ("tr", "Bh", BhT), ("tr", "Vb", Vt),
                    ("mm", lambda c: xc("Ab", c), lambda c: xc("Kb", c), 128, [ex["Kb"].b, ex["Ab"].b], W2, maskNT),
                    ("mm", lambda c: xc("Kb", c), lambda c: Rpb.t[:, c * 64:(c + 1) * 64], 64, [ex["Kb"].b, Rpb.b], Mkr, maskM),
                    ("tr", "Kh", KhT)]

                def do_side():
                    if not side:
                        return
                    it = side.pop(0)
                    if it[0] == "mm":
                        _, lf, rf, nco, rd, dst, msk = it
                        P, pv_ = quad_mm(lf, rf, nco, rd)
                        TT(dst.t[:], pv_, msk, ALU.mult, [P.b, cstb], [dst.b])
                    else:
                        _, nm, dst = it
                        P = nbk()
                        pv_ = P.t[:, 0:256].bitcast(BF16).rearrange("p (a b) -> p a b", a=4)
                        for c in range(NQ):
                            PET(pv_[:, c, :], xc(nm, c), [ex[nm].b], [P.b])
                        CP(dst.t[:], pv_, [P.b], [dst.b], eng="act")

                for lev in range(1, 6):
                    if lev < 5:
                        P, pv_ = quad_mm(lambda c: XTc.t[:, c, :], lambda c: Xc.t[:, c, :], 128, [XTc.b, Xc.b])
                        CP(Xn.t[:], pv_, [P.b], [Xn.b], eng="act")
                    P, pv_ = quad_mm(lambda c: Xc.t[:, c, :], lambda c: XTc.t[:, c, :], 128, [XTc.b, Xc.b])
                    CP(XTn.t[:], pv_, [P.b], [XTn.b], eng="act")
                    do_side()
                    yield
                    P, pv_ = quad_mm(lambda c: XTn.t[:, c, :], lambda c: Tc.t[:, c, :], 128, [XTn.b, Tc.b])
                    TT(Tn.t[:], pv_, Tc.t[:], ALU.add, [P.b, Tc.b], [Tn.b])
                    do_side()
                    yield
                    Xc, Xn = Xn, Xc
                    XTc, XTn = XTn, XTc
                    Tc, Tn = Tn, Tc
                while side:
                    do_side()
                    yield
                P, pv_ = quad_mm(lambda c: Tc.t[:, c, :], lambda c: W1.t[:, c, :], 128, [Tc.b, W1.b])
                CP(P1T.t[:], pv_, [P.b], [P1T.b], eng="act")
                P, pv_ = quad_mm(lambda c: Tc.t[:, c, :], lambda c: W2.t[:, c, :], 128, [Tc.b, W2.b])
                CP(P2T.t[:], pv_, [P.b], [P2T.b], eng="act")
                yield
                P, pv_ = quad_mm(lambda c: P1T.t[:, c, :], lambda c: Mbr.t[:, c, :], 64, [P1T.b, Mbr.b])
                TT(Q1.t[:], pv_, Rp.t[:, :].rearrange("p (c j) -> p c j", j=CH), ALU.add, [P.b, Rp.b], [Q1.b])
                P, pv_ = quad_mm(lambda c: P2T.t[:, c, :], lambda c: Mbr.t[:, c, :], 64, [P2T.b, Mbr.b])
                TT(Q2.t[:], pv_, Mkr.t[:], ALU.add, [P.b, Mkr.b], [Q2.b])
                yield
                P, pv_ = quad_mm(lambda c: P1T.t[:, c, :], lambda c: BhT.t[:, c, :], 128, [P1T.b, BhT.b])
                TT(G1.t[:], pv_, Dg.t[:], ALU.add, [P.b, Dg.b], [G1.b])
                P, pv_ = quad_mm(lambda c: P2T.t[:, c, :], lambda c: BhT.t[:, c, :], 128, [P2T.b, BhT.b])
                TT(G2.t[:], pv_, KhT.t[:], ALU.add, [P.b, KhT.b], [G2.b])
                yield

            def seq(k):
                t0, _, kind, b, d, fpass = blks[k]
                p2, p3 = k % 2, k % 3
                if fpass:
                    for m in range(2):
                        MEMSET(St[m][sti[m] % 2].t[:], 0.0, [St[m][sti[m] % 2].b], eng="dve")
                corder = list(range(NQ)) if d == 0 else list(range(NQ - 1, -1, -1))
                for ci, c in enumerate(corder):
                    for m in range(2):
                        o = QG[p2][m]
                        Sc = St[m][sti[m] % 2]
                        Sn = St[m][(sti[m] + 1) % 2]
                        sti[m] += 1
                        PY = PYb[m]
                        yv = PY.t[:, 0:256].rearrange("p (a b) -> p a b", a=4)[:, c, :]
                        PE(yv, Sc.t[:], o["Q1"].t[:, c, :], True, False, [Sc.b, o["Q1"].b], [PY.b])
                        PE(yv, o["Vt"].t[:, c, :], o["Q2"].t[:, c, :], False, True, [o["Vt"].b, o["Q2"].b], [PY.b])
                        PS = nbk()
                        PE(PS.t[:, 0:128], o["G1"].t[:, c, :], Sc.t[:], True, False, [o["G1"].b, Sc.b], [PS.b])
                        PE(PS.t[:, 0:128], o["G2"].t[:, c, :], o["Vt"].t[:, c, :], False, True, [o["G2"].b, o["Vt"].b], [PS.b])
                        CP(Sn.t[:], PS.t[:, 0:128], [PS.b], [Sn.b], eng="act")
                    yield
                for m in range(2):
                    CP(ybT[p2][m].t[:, :], PYb[m].t[:, 0:256], [PYb[m].b], [ybT[p2][m].b], eng="act")
                yield
                if d == 0:
                    for m in range(2):
                        tk.dma("pool", yf_d[b, m][:, t0:t0 + n], ybT[p2][m].t[:, :], reads=[ybT[p2][m].b], writes=[yfd_b[b]])
                    return
                if not is_readout(k):
                    return
                L7, L8 = lo7[p3], lo8[p3]
                ACT(ro["sg"].t[:, :], L8.t[:, :], AF.Sigmoid, [L8.b], [ro["sg"].b])
                for m in range(2):
                    rr_, kk_, vv_ = cv[p3][m], cv[p3][2 + m], cv[p3][4 + m]
                    ms = slice(m * 128, (m + 1) * 128)
                    aa = aaT[p3][m]
                    yb, yf = ybT[p2][m], yfT[p3][m]
                    P = nbk()
                    PE(P.t[:, :n], a2s.t[0:64, ms], L7.t[0:64, :], True, True, [a2s.b, L7.b], [P.b])
                    af = ro["af"]
                    ACT(af.t[:, :], P.t[:, :n], AF.Sigmoid, [P.b, pv.b], [af.b], bias=pvc("a0", m))
                    t2 = ro["t2"]
                    TT(t2.t[:, :], af.t[:, :], aa.t[:, :], ALU.add, [af.b, aa.b], [t2.b], eng="pool")
                    TS(t2.t[:, :], t2.t[:, :], pvc("ka", m), None, ALU.mult, None, [t2.b, pv.b], [t2.b])
                    STT(t2.t[:, :], omka.t[:, m:m + 1].to_broadcast([128, n]), 2.0, t2.t[:, :], ALU.mult, ALU.add,
                        [omka.b, t2.b], [t2.b])
                    TT(t2.t[:, :], t2.t[:, :], kk_.t[:, :], ALU.mult, [t2.b, kk_.b], [t2.b])
                    TT(t2.t[:, :], t2.t[:, :], rr_.t[:, :], ALU.mult, [t2.b, rr_.b], [t2.b])
                    TS(t2.t[:, :], t2.t[:, :], pvc("rk", m), None, ALU.mult, None, [t2.b, pv.b], [t2.b])
                    yield
                    PB = nbk()
                    PE(PB.t[:, :n], headones, t2.t[:, :], True, True, [cstb, t2.b], [PB.b])
                    bon = ro["bon"]
                    TT(bon.t[:, :], PB.t[:, :n], vv_.t[:, :], ALU.mult, [PB.b, vv_.b], [bon.b])
                    ys, sq2, mu, var, rstd = ro["ys"], ro["sq"], ro["mu"], ro["var"], ro["rstd"]
                    TT(ys.t[:, :], yf.t[:, :], yb.t[:, :], ALU.add, [yf.b, yb.b], [ys.b], eng="pool")
                    ACT(sq2.t[:, :], ys.t[:, :], AF.Square, [ys.b], [sq2.b])
                    yield
                    P1 = nbk()
                    PE(P1.t[:, :n], headones, ys.t[:, :], True, True, [cstb, ys.b], [P1.b])
                    P2 = nbk()
                    PE(P2.t[:, :n], headones, sq2.t[:, :], True, True, [cstb, sq2.b], [P2.b])
                    TS(mu.t[:, :], P1.t[:, :n], 1.0 / 64, None, ALU.mult, None, [P1.b], [mu.b])
                    TT(var.t[:, :], mu.t[:, :], mu.t[:, :], ALU.mult, [mu.b], [var.b])
                    STT(var.t[:, :], P2.t[:, :n], 1.0 / 64, var.t[:, :], ALU.mult, ALU.subtract, [P2.b, var.b], [var.b])
                    ACT(rstd.t[:, :], var.t[:, :], AF.Ln, [var.b], [rstd.b], bias=eps_gn.t[:, 0:1])
                    ACT(rstd.t[:, :], rstd.t[:, :], AF.Exp, [rstd.b], [rstd.b], scale=-0.5)
                    yield
                    TT(ys.t[:, :], ys.t[:, :], mu.t[:, :], ALU.subtract, [ys.b, mu.b], [ys.b], eng="pool")
                    TT(ys.t[:, :], ys.t[:, :], rstd.t[:, :], ALU.mult, [ys.b, rstd.b], [ys.b])
                    TS(ys.t[:, :], ys.t[:, :], pvc("gng", m), pvc("gnb", m), ALU.mult, ALU.add, [ys.b, pv.b], [ys.b])
                    TT(ys.t[:, :], ys.t[:, :], bon.t[:, :], ALU.add, [ys.b, bon.b], [ys.b])
                    PG = nbk()
                    PE(PG.t[:, :n], g2.t[:, ms], ro["sg"].t[:, :], True, True, [g2.b, ro["sg"].b], [PG.b])
                    TT(ob[m].t[:, :], ys.t[:, :], PG.t[:, :n], ALU.mult, [ys.b, PG.b], [ob[m].b])
                    tk.dma("pool", mix_d[b][:, m, t0:t0 + n], ob[m].t[:, :], reads=[ob[m].b])
                    yield

            run_tasks([prep(0)])
            for r in range(nblk + 1):
                tasks = []
                if r < nblk:
                    tasks += [chain(r, 0), chain(r, 1)]
                if r + 1 < nblk:
                    tasks.append(prep(r + 1))
                if r >= 1:
                    tasks.append(seq(r - 1))
                run_tasks(tasks)

    def seqs(last):
        return [(CTX, SEQ, 'x')] if last else [(0, CTX, 'c'), (CTX, SEQ, 'x')]

    def stage_P(l, last):
        with Scope() as s:
            pw = s.sb("pw", [128, 2, 128])
            pwb = s.sb("pwb", [128, 2, 128], BF16)
            tk.dma("sp", pw.t[:], poolbd_d[l], writes=[pw.b])
            CP(pwb.t[:], pw.t[:], [pw.b], [pwb.b])
            rcx = s.sb("rcx", [128, 2, SEQ])
            rcc = s.sb("rcc", [128, 2, CTX])
            tk.dma("sp", rcx.t[:], rc_x_d[:, :, :], writes=[rcx.b])
            tk.dma("sp", rcc.t[:], rc_c_d[:, :, :], writes=[rcc.b])
            PADW = SEQ + 16
            zz = [s.sb("pz%d" % i, [128, PADW]) for i in range(2)]
            sA = [s.sb("ps%d" % i, [128, PADW]) for i in range(4)]
            acc = s.sb("pacc", [128, SEQ])
            dd = [s.sb("pdd%d" % i, [128, SEQ], BF16) for i in range(2)]
            ob = [s.sb("pob%d" % i, [128, 512], BF16) for i in range(2)]
            io = 0
            units = [(b, s0, L, kind, m) for b in range(NB) for (s0, L, kind) in seqs(last) for m in range(2)]

            def load(u):
                b, s0, L, kind, m = units[u]
                z = zz[u % 2]
                MEMSET(z.t[:, 0:8], 0.0, [z.b])
                MEMSET(z.t[:, 8 + L:16 + L], 0.0, [z.b])
                tk.dma("sp", z.t[:, 8:8 + L], px_d[b, 9 + m][:, s0:s0 + L], writes=[z.b])

            load(0)

            def pgen():
              io = 0
              for u, (b, s0, L, kind, m) in enumerate(units):
                  rc = rcc if kind == 'c' else rcx
                  z = zz[u % 2]
                  D_ = dd[u % 2]
                  if u + 1 < len(units):
                      load(u + 1)
                  prev = z
                  step = 1
                  wl = L + 16
                  nlev = 2 if m == 0 else 4
                  for lev in range(nlev):
                      cur = sA[lev]
                      wl2 = wl - step
                      TT(cur.t[:, 0:wl2], prev.t[:, 0:wl2], prev.t[:, step:step + wl2], ALU.add, [prev.b], [cur.b])
                      prev = cur
                      step *= 2
                      wl = wl2
                  for hh in range(2):
                      w = POOLW[2 * m + hh]
                      lev = int(math.log2(w)) - 1
                      hs = slice(hh * 64, (hh + 1) * 64)
                      o = 8 - w // 2
                      TT(acc.t[hs, :L], sA[lev].t[hs, o:o + L], rc.t[hs, m, :L], ALU.mult, [sA[lev].b, rc.b], [acc.b])
                  TT(D_.t[:, :L], acc.t[:, :L], z.t[:, 8:8 + L], ALU.subtract, [acc.b, z.b], [D_.b])
                  yield
                  for o in range(0, L, 512):
                      n = min(512, L - o)
                      P = nb_()
                      PE(P.t[:, :n], pwb.t[:, m, :], D_.t[:, o:o + n], True, True, [pwb.b, D_.b], [P.b])
                      O = ob[io % 2]
                      io += 1
                      ACT(O.t[:, :n], P.t[:, :n], AF.Identity, [P.b, pv.b], [O.b], scale=pvc("psc", m))
                      tk.dma("pool", mix_d[b][:, 2 + m, s0 + o:s0 + o + n], O.t[:, :n], reads=[O.b])
                      yield

            tasks = [pgen()]
            if l + 1 < nlayers:
                shared = ([s.sb("pwm%d" % i, [128, 8, 512]) for i in range(2)], s.sb("pmrow", [4, 6 * D]))
                tasks.append(mod_gen(l + 1, s, shared))
            run_tasks(tasks)
        tk.barrier()

    def stage_G(l, last):
        with Scope() as s:
            wsf = s.sb("wsf", [128, 4, 128])
            wsb = s.sb("wsb", [128, 4, 128], BF16)
            tk.dma("sp", wsf.t[:], wsT_d[l], writes=[wsf.b])
            CP(wsb.t[:], wsf.t[:], [wsf.b], [wsb.b])
            bsb = s.sb("bsb", [128, 2, 128])
            tk.dma("sp", bsb.t[:], bsb_d[l], writes=[bsb.b])
            glg = s.sb("glg", [128, 256])
            glb = s.sb("glb", [128, 256])
            tk.dma("sp", glg.t[:], glg_d[l], writes=[glg.b])
            tk.dma("sp", glb.t[:], glb_d[l], writes=[glb.b])
            zin = [s.sb("gz%d" % i, [128, 512]) for i in range(4)]
            NS = 2
            geS = [[s.sb("gg%d_%d" % (u, i), [128, 512]) for i in range(4)] for u in range(NS)]
            vtS = [s.sb("gvt%d" % u, [128, 4, 256]) for u in range(NS)]
            vsqS = [s.sb("gvsq%d" % u, [128, 4, 256]) for u in range(NS)]
            vnS = [s.sb("gvn%d" % u, [128, 4, 256], BF16) for u in range(NS)]
            stS = [[s.sb("gs%d_%d" % (u, i), [128, 16]) for i in range(4)] for u in range(NS)]
            svS = [s.sb("gsv%d" % u, [128, 4, 128]) for u in range(NS)]
            ob = [s.sb("gob%d" % i, [128, 512], BF16) for i in range(2)]
            cnt = [0]
            work = [(b, t0, n) for b in range(NB) for (t0, n, kind) in blocks if not (last and kind == 'c')]

            def blk(k):
                b, t0, n = work[k]
                u = k % NS
                ge, vt, vsq, vn, sv = geS[u], vtS[u], vsqS[u], vnS[u], svS[u]
                st1, st2, mu, rstd = stS[u]
                nck = n // 128
                for i in range(4):
                    tk.dma("sp", zin[i].t[:, :n], px_d[b, 11 + i][:, t0:t0 + n], writes=[zin[i].b])
                    ACT(ge[i].t[:, :n], zin[i].t[:, :n], AF.Gelu_apprx_tanh, [zin[i].b], [ge[i].b])
                yield
                for mm in range(2):
                    P = nb_()
                    pv_ = P.t[:, :].rearrange("p (a b) -> p a b", a=4)
                    for ck in range(nck):
                        PET(pv_[:, ck, :], ge[2 + mm].t[:, ck * 128:(ck + 1) * 128], [ge[2 + mm].b], [P.b])
                    CP(vt.t[:, :nck, mm * 128:(mm + 1) * 128], pv_[:, :nck, :], [P.b], [vt.b], eng="act")
                yield
                v4 = vt.t[:, :nck, :].rearrange("p a (g c) -> p (a g) c", c=64)
                ng = nck * 4
                tk.op("dve", lambda e: e.tensor_reduce(out=st1.t[:, :ng], in_=v4, axis=AX.X, op=ALU.add), [vt.b], [st1.b])
                ACT(vsq.t[:, :nck, :], vt.t[:, :nck, :], AF.Square, [vt.b], [vsq.b])
                q4 = vsq.t[:, :nck, :].rearrange("p a (g c) -> p (a g) c", c=64)
                tk.op("dve", lambda e: e.tensor_reduce(out=st2.t[:, :ng], in_=q4, axis=AX.X, op=ALU.add), [vsq.b], [st2.b])
                yield
                TS(mu.t[:, :ng], st1.t[:, :ng], 1.0 / 64, None, ALU.mult, None, [st1.b], [mu.b])
                TT(st1.t[:, :ng], mu.t[:, :ng], mu.t[:, :ng], ALU.mult, [mu.b], [st1.b])
                STT(st2.t[:, :ng], st2.t[:, :ng], 1.0 / 64, st1.t[:, :ng], ALU.mult, ALU.subtract, [st2.b, st1.b], [st2.b])
                ACT(rstd.t[:, :ng], st2.t[:, :ng], AF.Ln, [st2.b], [rstd.b], bias=eps_ln.t[:, 0:1])
                ACT(rstd.t[:, :ng], rstd.t[:, :ng], AF.Exp, [rstd.b], [rstd.b], scale=-0.5)
                yield
                TT(v4, v4, mu.t[:, :ng].unsqueeze(2).to_broadcast([128, ng, 64]), ALU.subtract, [vt.b, mu.b], [vt.b], eng="pool")
                yield
                TT(v4, v4, rstd.t[:, :ng].unsqueeze(2).to_broadcast([128, ng, 64]), ALU.mult, [vt.b, rstd.b], [vt.b])
                yield
                gb = glg.t[:, :].unsqueeze(1).to_broadcast([128, nck, 256])
                bb_ = glb.t[:, :].unsqueeze(1).to_broadcast([128, nck, 256])
                TT(vt.t[:, :nck, :], vt.t[:, :nck, :], gb, ALU.mult, [vt.b, glg.b], [vt.b])
                yield
                TT(vn.t[:, :nck, :], vt.t[:, :nck, :], bb_, ALU.add, [vt.b, glb.b], [vn.b], eng="pool")
                yield
                for mm in range(2):
                    PA = nb_()
                    PBk = nb_()
                    pa = PA.t[:, :].rearrange("p (a b) -> p a b", a=4)
                    pb_ = PBk.t[:, :].rearrange("p (a b) -> p a b", a=4)
                    for ck in range(nck):
                        PE(pa[:, ck, :], vn.t[:, ck, mm * 128:(mm + 1) * 128], wsb.t[:, 2 * mm, :], True, True, [vn.b, wsb.b], [PA.b])
                        PE(pb_[:, ck, :], vn.t[:, ck, mm * 128:(mm + 1) * 128], wsb.t[:, 2 * mm + 1, :], True, True, [vn.b, wsb.b], [PBk.b])
                    bs0 = bsb.t[0:64, mm, :].unsqueeze(1).to_broadcast([64, nck, 128])
                    bs1 = bsb.t[64:128, mm, :].unsqueeze(1).to_broadcast([64, nck, 128])
                    TT(sv.t[0:64, :nck, :], pa[0:64, :nck, :], bs0, ALU.add, [PA.b, bsb.b], [sv.b])
                    TT(sv.t[64:128, :nck, :], pb_[64:128, :nck, :], bs1, ALU.add, [PBk.b, bsb.b], [sv.b])
                    O = ob[cnt[0] % 2]
                    cnt[0] += 1
                    TT(O.t[:, :n], ge[mm].t[:, :n], sv.t[:, :nck, :].rearrange("p a b -> p (a b)"), ALU.mult,
                       [ge[mm].b, sv.b], [O.b])
                    tk.dma("pool", mix_d[b][:, 4 + mm, t0:t0 + n], O.t[:, :n], reads=[O.b])
                    yield

            run_window([blk(k) for k in range(len(work))], NS)
        tk.barrier()

    def stage_F(l, last, clx, slx, clxB, slxB):
        with Scope() as s:
            if not last:
                clc = s.sb("clc", [128, 2, CTX], BF16)
                slc = s.sb("slc", [128, 2, CTX], BF16)
                tk.dma("sp", clc.t[:], cl_c_d[:, :, :], writes=[clc.b])
                tk.dma("sp", slc.t[:], sl_c_d[:, :, :], writes=[slc.b])
            cs64 = s.sb("cs64", [128, 128], BF16)
            tk.dma("sp", cs64.t[:], cs64_d[:, :], writes=[cs64.b])
            fw = s.sb("fw", [128, 2, 128])
            fwb = s.sb("fwb", [128, 2, 128], BF16)
            tk.dma("sp", fw.t[:], fnetbd_d[l], writes=[fw.b])
            CP(fwb.t[:], fw.t[:], [fw.b], [fwb.b])
            zf = [s.sb("fz%d" % i, [128, 512]) for i in range(2)]
            units = [(b, s0, L, kind) for b in range(NB) for (s0, L, kind) in seqs(last)]
            zbU = [[s.sb("fzb%d_%d" % (u, i), [128, SEQ], BF16) for i in range(2)] for u in range(2)]
            zcsU = [s.sb("zcs%d" % u, [128, 16, 2, 256], BF16) for u in range(2)]
            fb = [s.sb("ffb%d" % i, [128, 512], BF16) for i in range(2)]
            ob = [s.sb("fob%d" % i, [128, 512], BF16) for i in range(2)]
            cnt = [0, 0, 0]

            def nbf():
                b_ = banks[bank_i[0] % 6]
                bank_i[0] += 1
                return b_

            def front(ui):
                b, s0, L, kind = units[ui]
                zb, zcs = zbU[ui % 2], zcsU[ui % 2]
                ntc = L // 128
                for m in range(2):
                    for o in range(0, L, 512):
                        n = min(512, L - o)
                        Z = zf[cnt[1] % 2]
                        cnt[1] += 1
                        tk.dma("sp", Z.t[:, :n], px_d[b, 15 + m][:, s0 + o:s0 + o + n], writes=[Z.b])
                        CP(zb[m].t[:, o:o + n], Z.t[:, :n], [Z.b], [zb[m].b])
                        yield
                for tc in range(ntc):
                    zv = zcs.t[:, tc, :, :].rearrange("p x (m h c) -> p h m x c", m=2, h=2)
                    for hh in range(2):
                        P = nbf()
                        hs = slice(hh * 64, (hh + 1) * 64)
                        for m in range(2):
                            PE(P.t[:, m * 128:(m + 1) * 128], zb[m].t[hs, tc * 128:(tc + 1) * 128], cs64.t[hs, :], True, True,
                               [zb[m].b, cs64.b], [P.b])
                        CP(zv[:, hh], P.t[:, 0:256].rearrange("p (m x c) -> p m x c", m=2, x=2), [P.b], [zcs.b])
                    yield

            def back(ui):
                b, s0, L, kind = units[ui]
                zcs = zcsU[ui % 2]
                cl, sl = (clc, slc) if kind == 'c' else (clx, slx)
                clB = [clc.b] * 2 if kind == 'c' else clxB
                slB = [slc.b] * 2 if kind == 'c' else slxB
                ntc = L // 128
                for m in range(2):
                    for o in range(0, L, 512):
                        n = min(512, L - o)
                        P = banks[6 + cnt[2] % 2]
                        cnt[2] += 1
                        for tc in range(ntc):
                            PE(P.t[:, :n], zcs.t[:, tc, 0, m * 128:(m + 1) * 128], cl.t[:, tc, o:o + n], tc == 0, False,
                               [zcs.b, clB[tc]], [P.b])
                            PE(P.t[:, :n], zcs.t[:, tc, 1, m * 128:(m + 1) * 128], sl.t[:, tc, o:o + n], False, tc == ntc - 1,
                               [zcs.b, slB[tc]], [P.b])
                            if tc % 4 == 3:
                                yield
                        Fb = fb[cnt[0] % 2]
                        O = ob[cnt[0] % 2]
                        cnt[0] += 1
                        CP(Fb.t[:, :n], P.t[:, :n], [P.b], [Fb.b])
                        P2 = nbf()
                        PE(P2.t[:, :n], fwb.t[:, m, :], Fb.t[:, :n], True, True, [fwb.b, Fb.b], [P2.b])
                        TS(O.t[:, :n], P2.t[:, :n], pvc("fnb", m), None, ALU.add, None, [P2.b, pv.b], [O.b])
                        tk.dma("pool", mix_d[b][:, 6 + m, s0 + o:s0 + o + n], O.t[:, :n], reads=[O.b])
                        yield

            run_tasks([front(0)])
            for ui in range(len(units)):
                run_tasks([back(ui), front(ui + 1) if ui + 1 < len(units) else None])
        tk.barrier()

    def ln_gen(y, n, gname, bname, sc, eps_t=None):
        eps_t = eps_t or eps_ln
        ybf, ysq, mu, rstd, var = sc
        CP(ybf.t[:, :, :n], y.t[:, :, :n], [y.b], [ybf.b], eng="dve")
        ACT(ysq.t[:, :, :n], y.t[:, :, :n], AF.Square, [y.b], [ysq.b])
        yield
        P1 = nb_()
        P2 = nb_()
        for ot in range(8):
            PE(P1.t[:, :n], ones_bf.t[:], ybf.t[:, ot, :n], ot == 0, ot == 7, [ones_bf.b, ybf.b], [P1.b])
        for ot in range(8):
            PE(P2.t[:, :n], ones_bf.t[:], ysq.t[:, ot, :n], ot == 0, ot == 7, [ones_bf.b, ysq.b], [P2.b])
        TS(mu.t[:, :n], P1.t[:, :n], 1.0 / D, None, ALU.mult, None, [P1.b], [mu.b])
        TT(var.t[:, :n], mu.t[:, :n], mu.t[:, :n], ALU.mult, [mu.b], [var.b])
        STT(var.t[:, :n], P2.t[:, :n], 1.0 / D, var.t[:, :n], ALU.mult, ALU.subtract, [P2.b, var.b], [var.b])
        yield
        ACT(rstd.t[:, :n], var.t[:, :n], AF.Ln, [var.b], [rstd.b], bias=eps_t.t[:, 0:1])
        ACT(rstd.t[:, :n], rstd.t[:, :n], AF.Exp, [rstd.b], [rstd.b], scale=-0.5)
        yield
        mub = mu.t[:, :n].unsqueeze(1).to_broadcast([128, 8, n])
        rsb = rstd.t[:, :n].unsqueeze(1).to_broadcast([128, 8, n])
        TT(y.t[:, :, :n], y.t[:, :, :n], mub, ALU.subtract, [y.b, mu.b], [y.b])
        yield
        TT(y.t[:, :, :n], y.t[:, :, :n], rsb, ALU.mult, [y.b, rstd.b], [y.b])
        yield
        for ot in range(8):
            MOD(y.t[:, ot, :n], y.t[:, ot, :n], pvc(gname, ot), pvc(bname, ot), [y.b, pv.b], [y.b], eng="act")
        yield

    def ln_gen2(y, yb, n, gname, bname, sc, eps_t, sub_eng="pool"):
        ybf, ysq, mu, rstd, var = sc
        for ot in range(8):
            CP(ybf.t[:, ot, :n], y.t[:, ot, :n], [yb[ot]], [ybf.b], eng="act")
            ACT(ysq.t[:, ot, :n], y.t[:, ot, :n], AF.Square, [yb[ot]], [ysq.b])
            if ot % 2:
                yield
        P1 = nb_()
        P2 = nb_()
        for ot in range(8):
            PE(P1.t[:, :n], ones_bf.t[:], ybf.t[:, ot, :n], ot == 0, ot == 7, [ones_bf.b, ybf.b], [P1.b])
        for ot in range(8):
            PE(P2.t[:, :n], ones_bf.t[:], ysq.t[:, ot, :n], ot == 0, ot == 7, [ones_bf.b, ysq.b], [P2.b])
        TS(mu.t[:, :n], P1.t[:, :n], 1.0 / D, None, ALU.mult, None, [P1.b], [mu.b])
        TT(var.t[:, :n], mu.t[:, :n], mu.t[:, :n], ALU.mult, [mu.b], [var.b])
        STT(var.t[:, :n], P2.t[:, :n], 1.0 / D, var.t[:, :n], ALU.mult, ALU.subtract, [P2.b, var.b], [var.b])
        yield
        ACT(rstd.t[:, :n], var.t[:, :n], AF.Ln, [var.b], [rstd.b], bias=eps_t.t[:, 0:1])
        ACT(rstd.t[:, :n], rstd.t[:, :n], AF.Exp, [rstd.b], [rstd.b], scale=-0.5)
        yield
        for ot in range(8):
            TT(y.t[:, ot, :n], y.t[:, ot, :n], mu.t[:, :n], ALU.subtract, [yb[ot], mu.b], [yb[ot]], eng=sub_eng)
            TT(y.t[:, ot, :n], y.t[:, ot, :n], rstd.t[:, :n], ALU.mult, [yb[ot], rstd.b], [yb[ot]])
            MOD(y.t[:, ot, :n], y.t[:, ot, :n], pvc(gname, ot), pvc(bname, ot), [yb[ot], pv.b], [yb[ot]], eng="act")
            if ot % 2:
                yield

    def ln_feat(y, n, gname, bname, sc, eps_t=None):
        for _ in ln_gen(y, n, gname, bname, sc, eps_t):
            pass

    def ln_scratch(s, nmax, pref):
        return (s.sb(pref + "ybf", [128, 8, nmax], BF16), s.sb(pref + "ysq", [128, 8, nmax], BF16),
                s.sb(pref + "mu", [128, nmax]), s.sb(pref + "rstd", [128, nmax]), s.sb(pref + "var", [128, nmax]))

    def stage_O(l, last, pre=None):
        OB = 256
        NW = 2
        with Scope() as s:
            wo = s.sb("wo", [128, 8, D], BF16)
            woB = BIGDMAC(wo, wout_d[l], 8, 2)
            NBUF = 2 * NW
            mx = [s.sb("omx%d" % i, [128, 8, OB], BF16) for i in range(NBUF)]
            xs = [s.sb("oxs%d" % i, [128, 8, OB]) for i in range(NBUF)]
            xsb = [[Buf() for _ in range(8)] for i in range(NBUF)]
            scs = [ln_scratch(s, OB, "o%d" % i) for i in range(NW)]
            work = []
            for b in range(NB):
                for (t0b, nb, kind) in blocks:
                    if last and kind == 'c':
                        continue
                    for t0 in range(t0b, t0b + nb, OB):
                        work.append((b, t0, OB, 2 if kind == 'c' else b))

            def load(k):
                if k >= len(work):
                    return
                b, t0, n, ni = work[k]
                M, X, XB = mx[k % NBUF], xs[k % NBUF], xsb[k % NBUF]
                tk.dma("sp", M.t[:, :, :n], mix_d[b][:, :, t0:t0 + n], writes=[M.b])
                tk.dma("sp", X.t[:, :, :n], xs_d[b][:, :, t0:t0 + n], writes=XB)

            def blk(k):
                b, t0, n, ni = work[k]
                M, X, XB, sc = mx[k % NBUF], xs[k % NBUF], xsb[k % NBUF], scs[k % NW]
                load(k + NW)
                if pre is not None:
                    for _ in range(2):
                        try:
                            next(pre)
                        except StopIteration:
                            break
                yield
                for ot in range(8):
                    P = nb_()
                    for kc in range(8):
                        PE(P.t[:, :n], wo.t[:, kc, ot * 128:(ot + 1) * 128], M.t[:, kc, :n], kc == 0, kc == 7, [woB[kc], M.b], [P.b])
                    STT(X.t[:, ot, :n], P.t[:, :n], mcol(2, ot, ni), X.t[:, ot, :n], ALU.mult, ALU.add, [P.b, mod.b, XB[ot]], [XB[ot]])
                    if ot % 2:
                        yield
                for _ in ln_gen2(X, XB, n, "l1g", "l1b", sc, eps_lna):
                    yield
                tk.dma("pool", xs_d[b][:, :, t0:t0 + n], X.t[:, :, :n], reads=XB)
                yield

            for k in range(NW):
                load(k)
            run_window([blk(k) for k in range(len(work))], NW)
            if pre is not None:
                for _ in pre:
                    pass
        tk.barrier()

    def stage_FF(l, last, w1a, w1b, w1B):
        NBK = 256
        with Scope() as s:
            w2 = s.sb("w2", [128, 22, D], BF16)
            w2B = BIGDMAC(w2, w2_d[l], 22, 2)

            def w1s(kc, c0, c1):
                t_ = w1a if kc < 4 else w1b
                return t_.t[:, kc % 4, c0:c1], w1B[kc][1 if c0 >= DFF else 0]
            side = None
            if l + 1 < nlayers:
                side = None
            xs = [s.sb("fxs%d" % i, [128, 8, NBK]) for i in range(2)]
            xsb = [[Buf() for _ in range(8)] for i in range(2)]
            h2 = [s.sb("fh2%d" % i, [128, 8, NBK], BF16) for i in range(2)]
            act = s.sb("fact", [128, 22, NBK], BF16)
            sg = [s.sb("fsg%d" % i, [128, NBK]) for i in range(2)]
            sc = ln_scratch(s, NBK, "f")
            work = []
            for b in range(NB):
                for (t0b, nb, kind) in blocks:
                    if last and kind == 'c':
                        continue
                    for t0 in range(t0b, t0b + nb, NBK):
                        work.append((b, t0, 2 if kind == 'c' else b))
            n = NBK
            nw = len(work)

            def load_mod(k):
                b, t0, ni = work[k]
                X = xs[k % 2]
                H = h2[k % 2]
                XB = xsb[k % 2]
                tk.dma("sp", X.t[:, :, :], xs_d[b][:, :, t0:t0 + n], writes=XB)
                for kc in range(8):
                    MOD(H.t[:, kc, :], X.t[:, kc, :], mcol(4, kc, ni), mcol(3, kc, ni), [XB[kc], mod.b], [H.b], eng="pool")

            def fin(k):
                b, t0, ni = work[k]
                X = xs[k % 2]
                XB = xsb[k % 2]
                for _ in ln_gen2(X, XB, n, "l2g", "l2b", sc, eps_lna, sub_eng="pool"):
                    yield
                if last:
                    tk.dma("act", out_d[b][:, :, t0 - CTX:t0 - CTX + n], X.t[:, :, :], reads=XB)
                else:
                    tk.dma("act", xs_d[b][:, :, t0:t0 + n], X.t[:, :, :], reads=XB)
                yield

            load_mod(0)
            for k in range(nw):
                b, t0, ni = work[k]
                X = xs[k % 2]
                H = h2[k % 2]
                fq = fin(k - 1) if k >= 1 else None
                loaded = False
                for ft in range(22):
                    if fq is not None:
                        try:
                            next(fq)
                        except StopIteration:
                            fq = None
                    elif not loaded and k + 1 < nw:
                        load_mod(k + 1)
                        loaded = True
                    Pg = nb_()
                    Pu = nb_()
                    for kc in range(8):
                        wa, wb_ = w1s(kc, ft * 128, (ft + 1) * 128)
                        PE(Pg.t[:, :n], wa, H.t[:, kc, :], kc == 0, kc == 7, [wb_, H.b], [Pg.b])
                    for kc in range(8):
                        wa, wb_ = w1s(kc, DFF + ft * 128, DFF + (ft + 1) * 128)
                        PE(Pu.t[:, :n], wa, H.t[:, kc, :], kc == 0, kc == 7, [wb_, H.b], [Pu.b])
                    S = sg[ft % 2]
                    ACT(S.t[:, :], Pg.t[:, :n], AF.Silu, [Pg.b], [S.b])
                    TT(act.t[:, ft, :], S.t[:, :], Pu.t[:, :n], ALU.mult, [S.b, Pu.b], [act.b])
                if fq is not None:
                    for _ in fq:
                        pass
                if not loaded and k + 1 < nw:
                    load_mod(k + 1)
                if side is not None:
                    nside = 12 if k < nw - 1 else 100000
                    for _ in range(nside):
                        try:
                            next(side)
                        except StopIteration:
                            side = None
                            break
                for ot in range(8):
                    P = nb_()
                    for ft in range(22):
                        PE(P.t[:, :n], w2.t[:, ft, ot * 128:(ot + 1) * 128], act.t[:, ft, :], ft == 0, ft == 21, [w2B[ft], act.b], [P.b])
                    STT(X.t[:, ot, :], P.t[:, :n], mcol(5, ot, ni), X.t[:, ot, :], ALU.mult, ALU.add, [P.b, mod.b, xsb[k % 2][ot]], [xsb[k % 2][ot]])
            for _ in fin(nw - 1):
                pass
        tk.barrier()

    def run(name, fn, *a):
        with nc.named_scope(name):
            fn(*a)

    tk.barrier()
    for l in range(nlayers):
        last = (l == DEPTH - 1)
        with Scope() as so:
            set_layer(l)
            win = so.sb("win", [128, 8, INC], BF16)
            winB = BIGDMAC(win, win_d[l], 8, 1)
            if l == 0:
                run("cast", stage_cast_init)
            run("A%d" % l, stage_A, l, last, win, winB)
        if "stopA" in dbg:
            break
        if "skipR" not in dbg:
            run("R%d" % l, stage_R, l, last)
        if "stopR" in dbg:
            break
        if "skipP" not in dbg:
            run("P%d" % l, stage_P, l, last)
        with Scope() as so:
            clx = so.sb("clx", [128, 16, SEQ], BF16)
            slx = so.sb("slx", [128, 16, SEQ], BF16)
            clxB = BIGDMA(clx, cl_x_d, 16, 2, q="pool")
            slxB = BIGDMA(slx, sl_x_d, 16, 2, q="pool")
            if "skipG" not in dbg:
                run("G%d" % l, stage_G, l, last)
            if "skipF" not in dbg:
                run("F%d" % l, stage_F, l, last, clx, slx, clxB, slxB)
        if "stopM" in dbg:
            break
        with Scope() as so:
            w1a = so.sb("w1a", [128, 4, 2 * DFF], BF16)
            w1b = so.sb("w1b", [128, 4, 2 * DFF], BF16)
            w1B = [[Buf() for _ in range(2)] for _ in range(8)]

            def w1pre(l=l, w1a=w1a, w1b=w1b, w1B=w1B):
                for r in range(8):
                    dst = w1a if r < 4 else w1b
                    for h in range(2):
                        tk.dma("pool", dst.t[:, r % 4, h * DFF:(h + 1) * DFF], w1_d[l][:, r, h * DFF:(h + 1) * DFF], writes=[w1B[r][h]])
                        yield
            run("O%d" % l, stage_O, l, last, w1pre())
            run("FF%d" % l, stage_FF, l, last, w1a, w1b, w1B)
    tk.barrier()
    es_glob.close()
    return nc


def _pos_embed():
    rows = SEQ // GRID_W
    row, col = np.meshgrid(np.arange(rows, dtype=np.float32), np.arange(GRID_W, dtype=np.float32), indexing='ij')
    quarter = D // 4
    freqs = np.exp(np.float32(-math.log(10000.0)) * np.arange(quarter, dtype=np.float32) / np.float32(quarter)).astype(np.float32)

    def enc(p):
        ang = p.reshape(-1, 1).astype(np.float32) * freqs[None, :]
        return np.concatenate([np.sin(ang), np.cos(ang)], -1)
    return np.concatenate([enc(row), enc(col)], -1).astype(np.float32)


def _fm(a):
    sh = a.shape
    kc = sh[-2] // 128
    a = a.reshape(sh[:-2] + (kc, 128, sh[-1]))
    return np.ascontiguousarray(np.swapaxes(a, -3, -2))


def _cols(v):
    return np.ascontiguousarray(v.reshape(-1, 128).T)


def _consts():
    bf = ml_dtypes.bfloat16
    c = {}
    c["posT"] = _fm(np.ascontiguousarray(_pos_embed().T))
    for L, nm in ((SEQ, "x"), (CTX, "c")):
        idx = np.arange(L, dtype=np.int64)
        ang = 2.0 * np.pi * ((idx[:, None] * idx[None, :]) % L).astype(np.float64) / L
        cl = np.cos(ang) / math.sqrt(L)
        sl = -np.sin(ang) / math.sqrt(L)
        c["cl_" + nm] = _fm(cl.astype(np.float32)).astype(bf)
        c["sl_" + nm] = _fm(sl.astype(np.float32)).astype(bf)
        rc = np.zeros((2, 128, L), np.float32)
        for m in range(2):
            for hh in range(2):
                w = POOLW[2 * m + hh]
                lo = np.clip(idx - w // 2, 0, L)
                hi = np.clip(idx + w - w // 2, 0, L)
                rc[m, hh * 64:(hh + 1) * 64, :] = (1.0 / (hi - lo).astype(np.float32))[None, :]
        c["rc_" + nm] = np.ascontiguousarray(rc.transpose(1, 0, 2))
    i64 = np.arange(64)
    a64 = 2.0 * np.pi * ((i64[:, None] * i64[None, :]) % 64) / 64.0
    cs = np.concatenate([np.cos(a64), np.sin(a64)], 1) / 8.0
    c["cs64"] = np.concatenate([cs, cs], 0).astype(np.float32).astype(bf)
    ident = np.eye(128, dtype=np.float32)
    ho = np.zeros((128, 128), np.float32)
    ho[:64, :64] = 1
    ho[64:, 64:] = 1
    su = np.triu(np.ones((128, 128), np.float32), 1)
    sl_ = np.tril(np.ones((128, 128), np.float32), -1)
    ui = np.triu(np.ones((64, 64), np.float32), 0)
    li = np.tril(np.ones((64, 64), np.float32), 0)
    ui_st = np.concatenate([ui, ui], 0)
    li_st = np.concatenate([li, li], 0)
    cst = np.concatenate([ident, ho, np.tile(su, (1, 4)), np.tile(sl_, (1, 4)), np.tile(ui_st, (1, 4)), np.tile(li_st, (1, 4)),
                          np.zeros((128, 128), np.float32)], 1)
    assert cst.shape == (128, 1920)
    c["cst"] = cst
    return c


_CONSTS = None


def _prep_shared(inp):
    global _CONSTS
    if _CONSTS is None:
        _CONSTS = _consts()
    f = np.float32
    sh = dict(_CONSTS)
    sh["w_mod"] = _fm(np.asarray(inp["w_mod"], f))
    sh["b_mod"] = np.stack([_cols(np.asarray(inp["b_mod"], f)[l]) for l in range(DEPTH)])
    sh["w_in"] = _fm(np.asarray(inp["w_in"], f))
    sh["w_out"] = _fm(np.asarray(inp["w_out"], f))
    sh["ffn_w1"] = _fm(np.asarray(inp["ffn_w1"], f))
    sh["ffn_w2"] = _fm(np.asarray(inp["ffn_w2"], f))
    pv = np.zeros((DEPTH, 128, NPV), f)
    for l in range(DEPTH):
        def put(name, arr):
            arr = np.asarray(arr, f)
            pv[l, :, PV[name]:PV[name] + arr.shape[1]] = arr
        conv = np.asarray(inp["rkv_conv"], f)[l]
        put("conv", np.concatenate([_cols(conv[j]) for j in range(3)], 1))
        put("w0", np.concatenate([_cols(np.asarray(inp["decay_w0"], f)[l, d]) for d in range(2)], 1))
        put("a0", np.concatenate([_cols(np.asarray(inp["iclr_a0"], f)[l, d]) for d in range(2)], 1))
        put("kk", _cols(np.asarray(inp["k_k"], f)[l]))
        put("ka", _cols(np.asarray(inp["k_a"], f)[l]))
        put("rk", _cols(np.asarray(inp["r_k"], f)[l].reshape(-1)))
        put("gng", _cols(np.asarray(inp["gn_g"], f)[l]))
        put("gnb", _cols(np.asarray(inp["gn_b"], f)[l]))
        put("psc", _cols(np.asarray(inp["pool_scale"], f)[l]))
        put("fnb", _cols(np.asarray(inp["fnet_b"], f)[l]))
        put("l1g", _cols(np.asarray(inp["ln1_g"], f)[l]))
        put("l1b", _cols(np.asarray(inp["ln1_b"], f)[l]))
        put("l2g", _cols(np.asarray(inp["ln2_g"], f)[l]))
        put("l2b", _cols(np.asarray(inp["ln2_b"], f)[l]))
    sh["pv"] = pv
    sh["w2s"] = np.ascontiguousarray(np.asarray(inp["decay_w2"], f).reshape(DEPTH, 128, 256))
    sh["a2s"] = np.ascontiguousarray(np.asarray(inp["iclr_a2"], f).reshape(DEPTH, 128, 256))
    sh["g2"] = np.ascontiguousarray(np.asarray(inp["gate_g2"], f))

    def bd(w):
        o = np.zeros((DEPTH, 128, 2, 128), f)
        for m in range(2):
            o[:, 0:64, m, 0:64] = w[:, 2 * m]
            o[:, 64:128, m, 64:128] = w[:, 2 * m + 1]
        return o
    sh["poolbd"] = bd(np.asarray(inp["pool_w"], f))
    sh["fnetbd"] = bd(np.asarray(inp["fnet_w"], f))
    ws = np.asarray(inp["gmlp_ws"], f)
    sh["wsT"] = np.ascontiguousarray(ws.transpose(0, 3, 1, 2))
    bs = np.asarray(inp["gmlp_bs"], f)
    bsb = np.zeros((DEPTH, 128, 2, 128), f)
    for m in range(2):
        bsb[:, 0:64, m, :] = bs[:, 2 * m][:, None, :]
        bsb[:, 64:128, m, :] = bs[:, 2 * m + 1][:, None, :]
    sh["bsb"] = bsb
    sh["glg"] = np.ascontiguousarray(np.broadcast_to(np.asarray(inp["gmlp_ln_g"], f)[:, None, :], (DEPTH, 128, 256)))
    sh["glb"] = np.ascontiguousarray(np.broadcast_to(np.asarray(inp["gmlp_ln_b"], f)[:, None, :], (DEPTH, 128, 256)))
    return sh


def _prep_core(inp, sh, i):
    f = np.float32
    m = dict(sh)
    xb = np.asarray(inp["x"], f)[i * NB:(i + 1) * NB]
    m["xT"] = _fm(np.ascontiguousarray(np.swapaxes(xb, 1, 2)))
    cb = np.asarray(inp["ctx"], f)[i * NB:(i + 1) * NB]
    m["ctxT"] = _fm(np.ascontiguousarray(np.swapaxes(cb, 1, 2)))
    cc = np.zeros((4, D), f)
    cc[0:NB] = np.asarray(inp["c"], f)[i * NB:(i + 1) * NB]
    cc[2] = np.asarray(inp["c_ctx"], f)
    m["cT"] = _fm(np.ascontiguousarray(cc.T))
    return m


_NC = None


def kernel(**inputs):
    global _NC
    if _NC is None:
        _NC = build()
    sh = _prep_shared(inputs)
    in_maps = [_prep_core(inputs, sh, i) for i in range(NCORES)]
    res = run_bass_kernel_spmd(_NC, in_maps, core_ids=list(range(NCORES)))
    outs = []
    for i in range(NCORES):
        o = np.asarray(res.results[i]["out"])
        o = o.transpose(0, 2, 1, 3).reshape(NB, D, SEQ)
        outs.append(np.swapaxes(o, 1, 2))
    return np.ascontiguousarray(np.concatenate(outs, 0)).astype(np.float32)
```
